# Optimizing a Trainium2 kernel written in Bass

```python
import math
import jax, jax.numpy as jnp
from jax import lax
import numpy as np

D_MODEL = 4096
BATCH = 2
SEQ = 8192
DEPTH = 1

N_META = 16
CHUNK = 128
Q_BLOCK = 128
H_RET = 16
RET_DK = D_MODEL // H_RET
RET_DV = D_MODEL // H_RET
RET_W = H_RET * RET_DV
DIFF_HD = 128
H_DIFF = D_MODEL // (2 * DIFF_HD)
DIFF_W = H_DIFF * 2 * DIFF_HD
D_FF = ((8 * D_MODEL + 3 * 256 - 1) // (3 * 256)) * 256
EPS = 1e-6
SPLIT_SIZES = (H_RET * RET_DK, H_RET * RET_DK, RET_W, RET_W,
               DIFF_W, DIFF_W, DIFF_W,
               D_MODEL, D_MODEL)
D_IN = sum(SPLIT_SIZES)
SPLIT_POINTS = tuple(int(v) for v in np.cumsum(SPLIT_SIZES)[:-1])

kernel_name = "hybrid_retention_diffattn_gated_encoder"


def _rmsnorm(x, g):
    x32 = x.astype(jnp.float32)
    y = x32 * lax.rsqrt(jnp.mean(x32 * x32, axis=-1, keepdims=True) + EPS)
    return (y * g.astype(jnp.float32)).astype(x.dtype)


def _unit_rms(x32):
    return x32 * lax.rsqrt(jnp.mean(x32 * x32, axis=-1, keepdims=True) + EPS)


def _lambda_init(layer):
    return 0.8 - 0.6 * math.exp(-0.3 * layer)


def _retention_scan(q, k, v, log_g, include_diag):
    b, h, t, dk = q.shape
    dv = v.shape[-1]
    n = t // CHUNK
    idx = jnp.arange(CHUNK, dtype=jnp.float32)
    dist = idx[:, None] - idx[None, :]
    keep = (dist >= 0) if include_diag else (dist > 0)
    intra = jnp.where(keep[None], jnp.exp(log_g[:, None, None] * jnp.maximum(dist, 0.0)[None]), 0.0)
    q_dec = jnp.exp(log_g[:, None] * (idx + 1.0)[None])[..., None]
    k_dec = jnp.exp(log_g[:, None] * (CHUNK - 1.0 - idx)[None])[..., None]
    c_dec = jnp.exp(log_g * CHUNK)[:, None, None]

    def chunks(a):
        return a.reshape(b, h, n, CHUNK, a.shape[-1]).transpose(2, 0, 1, 3, 4)

    def step(state, qkv):
        qc, kc, vc = qkv
        scores = jnp.einsum('bhid,bhjd->bhij', qc, kc) * intra
        out = (jnp.einsum('bhij,bhjv->bhiv', scores, vc)
               + jnp.einsum('bhid,bhdv->bhiv', qc * q_dec, state))
        state = state * c_dec + jnp.einsum('bhjd,bhjv->bhdv', kc * k_dec, vc)
        return state, out

    s0 = jnp.zeros((b, h, dk, dv), jnp.float32)
    _, out = lax.scan(step, s0, (chunks(q), chunks(k), chunks(v)))
    return out.transpose(1, 2, 0, 3, 4).reshape(b, h, t, dv)


def _retention_branch(q, k, v, g, log_decay_param, w_branch):
    b, l, _ = q.shape
    pad = CHUNK - N_META

    def heads(a, d):
        a = a.astype(jnp.float32).reshape(b, l, H_RET, d).transpose(0, 2, 1, 3)
        return jnp.pad(a, ((0, 0), (0, 0), (pad, 0), (0, 0)))

    qh = heads(q, RET_DK)
    kh = heads(k, RET_DK) * (RET_DK ** -0.5)
    vh = heads(v, RET_DV)
    log_g = -jnp.exp(log_decay_param.astype(jnp.float32))
    fwd = _retention_scan(qh, kh, vh, log_g[0], True)
    rev = lambda a: a[:, :, ::-1]
    bwd = rev(_retention_scan(rev(qh), rev(kh), rev(vh), log_g[1], False))
    y = _unit_rms((fwd + bwd)[:, :, pad:])
    y = y.transpose(0, 2, 1, 3).reshape(b, l, RET_W).astype(q.dtype)
    return (jax.nn.silu(g) * y) @ w_branch


def _diff_attn_branch(q, k, v, lam_params, subln_g, lam_init, w_branch):
    b, l, _ = q.shape
    qh = q.reshape(b, l, H_DIFF, 2, DIFF_HD).transpose(0, 2, 3, 1, 4) * (DIFF_HD ** -0.5)
    kh = k.reshape(b, l, H_DIFF, 2, DIFF_HD).transpose(0, 2, 3, 1, 4)
    vh = v.reshape(b, l, H_DIFF, 2 * DIFF_HD).transpose(0, 2, 1, 3)
    n_blk = -(-l // Q_BLOCK)
    qh = jnp.pad(qh, ((0, 0), (0, 0), (0, 0), (0, n_blk * Q_BLOCK - l), (0, 0)))
    lp = lam_params.astype(jnp.float32)
    lam = jnp.exp(jnp.sum(lp[0] * lp[1])) - jnp.exp(jnp.sum(lp[2] * lp[3])) + lam_init
    slopes = 2.0 ** (-8.0 * jnp.arange(1, H_DIFF + 1, dtype=jnp.float32) / H_DIFF)
    kpos = jnp.arange(l, dtype=jnp.float32)

    def block(i):
        qb = lax.dynamic_slice_in_dim(qh, i * Q_BLOCK, Q_BLOCK, axis=3)
        s = jnp.einsum('bhmqd,bhmkd->bhmqk', qb, kh).astype(jnp.float32)
        qpos = (i * Q_BLOCK + jnp.arange(Q_BLOCK)).astype(jnp.float32)
        alibi = -slopes[:, None, None] * jnp.abs(qpos[:, None] - kpos[None, :])[None]
        p = jax.nn.softmax(s + alibi[None, :, None], axis=-1)
        a = p[:, :, 0] - lam * p[:, :, 1]
        return jnp.einsum('bhqk,bhkv->bhqv', a.astype(vh.dtype), vh)

    o = lax.map(block, jnp.arange(n_blk))
    o = o.transpose(1, 2, 0, 3, 4).reshape(b, H_DIFF, n_blk * Q_BLOCK, 2 * DIFF_HD)[:, :, :l]
    o = _unit_rms(o.astype(jnp.float32)) * subln_g.astype(jnp.float32) * (1.0 - lam_init)
    o = o.transpose(0, 2, 1, 3).reshape(b, l, DIFF_W).astype(q.dtype)
    return o @ w_branch


def setup_inputs(seed: int = 0) -> dict:
    key = jax.random.key(seed)
    ks = jax.random.split(key, 16)
    f32 = jnp.float32
    nrm = lambda k, shape, s: jax.random.normal(k, shape, f32) * s
    heads = jnp.arange(H_RET, dtype=f32)
    base = jnp.log(-jnp.log1p(-(2.0 ** (-5.0 - heads))))
    return {
        "x": nrm(ks[0], (BATCH, SEQ, D_MODEL), 1.0),
        "meta_tokens": nrm(ks[1], (N_META, D_MODEL), 1.0),
        "norm_mix_g": 1.0 + nrm(ks[2], (DEPTH, D_MODEL), 0.02),
        "w_in": nrm(ks[3], (DEPTH, D_MODEL, D_IN), D_MODEL ** -0.5),
        "ret_log_decay": base[None, None, :] + nrm(ks[4], (DEPTH, 2, H_RET), 0.01),
        "diff_lambda": nrm(ks[5], (DEPTH, 4, DIFF_HD), 0.1),
        "diff_subln_g": 1.0 + nrm(ks[6], (DEPTH, 2 * DIFF_HD), 0.02),
        "w_branch_ret": nrm(ks[7], (DEPTH, RET_W, D_MODEL), RET_W ** -0.5),
        "w_branch_diff": nrm(ks[8], (DEPTH, DIFF_W, D_MODEL), DIFF_W ** -0.5),
        "w_out": nrm(ks[9], (DEPTH, D_MODEL, D_MODEL), D_MODEL ** -0.5),
        "norm_ffn_g": 1.0 + nrm(ks[10], (DEPTH, D_MODEL), 0.02),
        "w_ffn_gate": nrm(ks[11], (DEPTH, D_MODEL, D_FF), D_MODEL ** -0.5),
        "w_ffn_up": nrm(ks[12], (DEPTH, D_MODEL, D_FF), D_MODEL ** -0.5),
        "w_ffn_down": nrm(ks[13], (DEPTH, D_FF, D_MODEL), D_FF ** -0.5),
        "norm_final_g": 1.0 + nrm(ks[14], (D_MODEL,), 0.02),
    }


def reference(x, meta_tokens, norm_mix_g, w_in, ret_log_decay, diff_lambda, diff_subln_g,
              w_branch_ret, w_branch_diff, w_out, norm_ffn_g, w_ffn_gate, w_ffn_up, w_ffn_down,
              norm_final_g):
    b = x.shape[0]
    meta = jnp.broadcast_to(meta_tokens[None].astype(x.dtype), (b, N_META, D_MODEL))
    h = jnp.concatenate([meta, x], axis=1)
    for layer in range(DEPTH):
        u = _rmsnorm(h, norm_mix_g[layer])
        proj = u @ w_in[layer]
        rq, rk, rv, rg, dq, dk, dv, ga, gb = jnp.split(proj, SPLIT_POINTS, axis=-1)
        y_ret = _retention_branch(rq, rk, rv, rg, ret_log_decay[layer], w_branch_ret[layer])
        y_diff = _diff_attn_branch(dq, dk, dv, diff_lambda[layer], diff_subln_g[layer],
                                   _lambda_init(layer), w_branch_diff[layer])
        mixed = jax.nn.sigmoid(ga) * y_ret + jax.nn.sigmoid(gb) * y_diff
        h = h + mixed @ w_out[layer]
        u = _rmsnorm(h, norm_ffn_g[layer])
        h = h + (jax.nn.silu(u @ w_ffn_gate[layer]) * (u @ w_ffn_up[layer])) @ w_ffn_down[layer]
    return _rmsnorm(h, norm_final_g)[:, N_META:]
```

```python
import math, contextlib
import numpy as np
import ml_dtypes
import concourse.bass as bass
import concourse.mybir as mybir
from concourse.bass_utils import run_bass_kernel_spmd

F32 = mybir.dt.float32
BF16 = mybir.dt.bfloat16
AF = mybir.ActivationFunctionType
ALU = mybir.AluOpType
BIG = 1.0e30
EPS = 1e-6
N_META = 16


class Cfg:
    def __init__(s, D=4096, S=8192):
        s.D, s.S = D, S
        s.T = S // 4
        s.H = D // 256
        s.DIN = 9 * D
        s.DFF = ((8 * D + 3 * 256 - 1) // (3 * 256)) * 256
        s.KC = D // 128
        s.FC = s.DFF // 128
        s.NT = s.T // 128
        s.NTALL = S // 128
        s.NKT = s.NTALL + 1
        s.NQB = s.T // 512
        s.TB = min(1024, s.T)
        s.LAM_INIT = 0.8 - 0.6 * math.exp(-0.3 * 0)


class Buf:
    __slots__ = ("name",)

    def __init__(s, name):
        s.name = name


class Prog:
    COMPUTE = ("pe", "act", "dve", "pool")

    def __init__(s):
        s.ops = []
        s.lastw = {}
        s.rd_eng = {}
        s.rd_dma = {}
        s.dmacnt = {}
        s.bar = None
        s.bar_done = set()

    def op(s, eng, fn, r=(), w=(), dma=None):
        i = len(s.ops)
        deps = set()
        for b in tuple(r) + tuple(w):
            lw = s.lastw.get(b)
            if lw is not None:
                deps.add(lw)
        for b in w:
            for x in s.rd_eng.get(b, {}).values():
                deps.add(x)
            for x in s.rd_dma.get(b, ()):
                deps.add(x)
        o = dict(eng=eng, fn=fn, deps=deps, dma=dma, mark=False, bar=None)
        if s.bar is not None and eng not in s.bar_done:
            o["bar"] = s.bar
            s.bar_done.add(eng)
        if dma is not None:
            s.dmacnt[dma] = s.dmacnt.get(dma, 0) + 1
            o["dval"] = 16 * s.dmacnt[dma]
        s.ops.append(o)
        for b in w:
            s.lastw[b] = i
            s.rd_eng[b] = {}
            s.rd_dma[b] = []
        for b in r:
            if dma is not None:
                s.rd_dma.setdefault(b, []).append(i)
            else:
                s.rd_eng.setdefault(b, {})[eng] = i
        return i

    def barrier(s):
        last = {}
        for i, o in enumerate(s.ops):
            if o["dma"] is None:
                last[o["eng"]] = i
        s.bar = (dict(last), dict(s.dmacnt))
        s.bar_done = set()

    def emit(s, nc, block):
        ops = s.ops
        for o in ops:
            for d in o["deps"]:
                if ops[d]["dma"] is None:
                    ops[d]["mark"] = True
            if o["bar"] is not None:
                for e, d in o["bar"][0].items():
                    ops[d]["mark"] = True
        cnt = {e: 0 for e in ("pe", "act", "dve", "pool", "sp")}
        for o in ops:
            if o["dma"] is None:
                if o["mark"]:
                    cnt[o["eng"]] += 1
                o["sval"] = cnt[o["eng"]]
        stack = contextlib.ExitStack()
        esem = {e: stack.enter_context(nc.semaphore("s_" + e)) for e in cnt}
        dsem = {k: stack.enter_context(nc.semaphore("d_" + str(k))) for k in s.dmacnt}
        byeng = {e: [] for e in cnt}
        for o in ops:
            byeng[o["eng"]].append(o)

        def run(eng_name, E):
            known = {}

            def need(sem, val):
                if known.get(sem.name if hasattr(sem, "name") else id(sem), 0) < val:
                    E.wait_ge(sem, val)
                    known[sem.name if hasattr(sem, "name") else id(sem)] = val

            for o in byeng[eng_name]:
                if o["bar"] is not None:
                    lastc, dcnt = o["bar"]
                    for e, d in lastc.items():
                        if e != eng_name or e != "pe":
                            need(esem[e], ops[d]["sval"])
                    for k, c in dcnt.items():
                        need(dsem[k], 16 * c)
                for d in sorted(o["deps"]):
                    do = ops[d]
                    if do["dma"] is not None:
                        need(dsem[do["dma"]], do["dval"])
                    else:
                        if do["eng"] == "pe" and eng_name == "pe":
                            continue
                        need(esem[do["eng"]], do["sval"])
                ins = o["fn"](E)
                if o["dma"] is not None:
                    ins.then_inc(dsem[o["dma"]], 16)
                elif o["mark"]:
                    ins.then_inc(esem[eng_name], 1)
            if eng_name == "sp":
                for k, c in s.dmacnt.items():
                    need(dsem[k], 16 * c)

        @block.tensor
        def _(E):
            run("pe", E)

        @block.scalar
        def _(E):
            run("act", E)

        @block.vector
        def _(E):
            run("dve", E)

        @block.gpsimd
        def _(E):
            run("pool", E)

        @block.sync
        def _(E):
            run("sp", E)

        return stack


def slopes(cfg):
    return [2.0 ** (-8.0 * (h + 1) / cfg.H) for h in range(cfg.H)]


def build(cfg):
    D, S, T, H, KC, DFF, FC = cfg.D, cfg.S, cfg.T, cfg.H, cfg.KC, cfg.DFF, cfg.FC
    NKT, NTALL, NT, NQB, TB = cfg.NKT, cfg.NTALL, cfg.NT, cfg.NQB, cfg.TB
    nc = bass.Bass("TRN2", target_bir_lowering=False)
    P = Prog()

    def din(name, shape, dt=F32):
        return nc.dram_tensor(name, list(shape), dt, kind="ExternalInput")

    x_rot = din("x_rot", [S, D])
    meta = din("meta", [N_META, D])
    gvec = din("gvec", [3, D])
    w_in = din("w_in", [D, 9 * D])
    w_br = din("w_br", [D, D])
    w_bd = din("w_bd", [D, D])
    w_out = din("w_out", [D, D])
    w_g = din("w_g", [D, DFF])
    w_u = din("w_u", [D, DFF])
    w_d = din("w_d", [DFF, D])
    rdecay = din("rdecay", [2, H])
    dlam = din("dlam", [4, 128])
    subln = din("subln", [256])
    c_absd = din("c_absd", [NQB, 128, NKT])
    c_sgn = din("c_sgn", [NQB, 2, NKT * 128], BF16)
    c_rb = din("c_rb", [H, 2, 512], BF16)
    c_babs = din("c_babs", [4, 128, 512])
    c_dist = din("c_dist", [2, 128, NKT])
    c_sq = din("c_sq", [4, 128, 128])
    c_kd = din("c_kd", [128, 2])
    c_ident = din("c_ident", [128, 128], BF16)
    y = nc.dram_tensor("y", [T, D], F32, kind="ExternalOutput")

    def dscr(name, shape, dt=BF16):
        return nc.dram_tensor(name, list(shape), dt)

    wb_in = {g: dscr("wb_in_" + g, [D, D]) for g in ("rq", "rk", "rv", "rg", "dq", "dk", "dv", "ga", "gb")}; wb_br = dscr("wb_br", [D, D]); wb_bd = dscr("wb_bd", [D, D])
    wb_out = dscr("wb_out", [D, D]); wb_g = dscr("wb_g", [D, DFF]); wb_u = dscr("wb_u", [D, DFF])
    wb_d = dscr("wb_d", [DFF, D])
    NR = NKT * 128
    rKtm = dscr("rKtm", [NR, D]); rVtm = dscr("rVtm", [NR, D]); dVtm = dscr("dVtm", [NR, D])
    dKT = dscr("dKT", [D, NR])
    rKT = dscr("rKT", [D, T]); rQT = dscr("rQT", [D, T]); rGT = dscr("rGT", [D, T]); dQT = dscr("dQT", [D, T])
    gaT = dscr("gaT", [D, T]); gbT = dscr("gbT", [D, T]); ZrT = dscr("ZrT", [D, T]); ZdT = dscr("ZdT", [D, T])
    h1 = dscr("h1", [T, D], F32)

    stack = contextlib.ExitStack()
    NB16 = 78 * 1024
    NF32 = 11 * 1024
    A16 = stack.enter_context(nc.sbuf_tensor("A16", [128, NB16], BF16))
    A32 = stack.enter_context(nc.sbuf_tensor("A32", [128, NF32], F32))
    ident = stack.enter_context(nc.sbuf_tensor("ident", [128, 128], BF16))
    ones32 = stack.enter_context(nc.sbuf_tensor("ones32", [128, 128], F32))
    ones16 = stack.enter_context(nc.sbuf_tensor("ones16", [128, 128], BF16))
    gT = stack.enter_context(nc.sbuf_tensor("gT", [128, 3, KC], F32))
    csq = stack.enter_context(nc.sbuf_tensor("csq", [128, 4, 128], F32))
    ckd = stack.enter_context(nc.sbuf_tensor("ckd", [128, 2], F32))
    sm = stack.enter_context(nc.sbuf_tensor("sm", [128, 64], F32))
    epsc = stack.enter_context(nc.sbuf_tensor("epsc", [128, 2], F32))
    trigt = stack.enter_context(nc.sbuf_tensor("trigt", [128, 2], F32))
    PS = [stack.enter_context(nc.psum_tensor("ps%d" % i, [128, 512], F32)) for i in range(8)]
    PB = [Buf("ps%d" % i) for i in range(8)]
    B_const = Buf("const")

    class Arena:
        def __init__(s):
            s.o16 = 0; s.o32 = 0

        def a16(s, n, shape=None):
            v = A16[:, s.o16:s.o16 + n]; s.o16 += n
            assert s.o16 <= NB16, ("bf16 arena overflow", s.o16)
            return v

        def a32(s, n):
            v = A32[:, s.o32:s.o32 + n]; s.o32 += n
            assert s.o32 <= NF32, ("f32 arena overflow", s.o32)
            return v

    evq = [0]

    def ev_eng():
        evq[0] += 1
        return "act" if evq[0] % 2 else "dve"

    def copy_op(eng, out, in_, r, w, scale=None):
        if eng == "act":
            if scale is None:
                P.op("act", lambda E: E.activation(out=out, in_=in_, func=AF.Copy), r=r, w=w)
            else:
                P.op("act", lambda E: E.activation(out=out, in_=in_, func=AF.Copy, scale=scale), r=r, w=w)
        else:
            if scale is None:
                P.op("dve", lambda E: E.tensor_copy(out=out, in_=in_), r=r, w=w)
            else:
                P.op("dve", lambda E: E.tensor_scalar(out=out, in0=in_, scalar1=scale, scalar2=None, op0=ALU.mult), r=r, w=w)

    ckey = [0]

    def dma(out, in_, r, w, key, eng="sp"):
        if key == "c0":
            ckey[0] += 1
            key = "c%d" % ckey[0]
        P.op(eng, lambda E: E.dma_start(out=out, in_=in_), r=r, w=w, dma=key)

    B_x = Buf("x_in"); B_w = Buf("w_f32")
    dma(ident[:], c_ident.ap(), [B_x], [B_const], "c0")
    dma(csq[:], c_sq.ap().rearrange("a p f -> p a f"), [B_x], [B_const], "c0")
    dma(ckd[:], c_kd.ap(), [B_x], [B_const], "c0")
    with nc.allow_non_contiguous_dma(reason="tiny gamma relayout"):
        dma(gT[:], gvec.ap().rearrange("a (kc p) -> p a kc", p=128), [B_x], [B_const], "c0")
    P.op("dve", lambda E: E.memset(ones32[:], 1.0), w=[B_const])
    P.op("dve", lambda E: E.memset(ones16[:], 1.0), w=[B_const])
    P.op("dve", lambda E: E.memset(epsc[:], EPS), w=[B_const])

    WB = {}

    pending_casts = []

    def cast_w(name, src, dst, c0, c1, rows, d0=None, defer=False):
        if name not in WB:
            WB[name] = Buf("wb_" + name)
        if defer:
            pending_casts.append((name, src, dst, c0, c1, rows, d0))
            return
        b = WB[name]
        d0 = c0 if d0 is None else d0
        RB = 512 if (c1 - c0) <= 4096 else 128
        trig = Buf("trig_" + name)
        P.op("dve", lambda E: E.memset(trigt[:, 0:1], 0.0), w=[trig])
        for r0 in range(0, rows, RB):
            r1 = min(rows, r0 + RB)
            P.op("pool", lambda E, r0=r0, r1=r1: E.dma_start(out=dst.ap()[r0:r1, d0:d0 + c1 - c0], in_=src.ap()[r0:r1, c0:c1]),
                 r=[B_w, trig], w=[b], dma="cw_" + name)

    def next_cast():
        if pending_casts:
            a = pending_casts.pop(0)
            cast_w(*a, defer=False)

    GR = {"rq": 0, "rk": 1, "rv": 2, "rg": 3, "dq": 4, "dk": 5, "dv": 6, "ga": 7, "gb": 8}
    for g in ("rk", "rv", "dv", "dk", "rq", "dq", "rg", "ga", "gb"):
        cast_w(g, w_in, wb_in[g], GR[g] * D, (GR[g] + 1) * D, D, d0=0, defer=(g not in ("rk", "rv", "dv", "dk")))
    cast_w("br", w_br, wb_br, 0, D, D, defer=True); cast_w("bd", w_bd, wb_bd, 0, D, D, defer=True)
    cast_w("out", w_out, wb_out, 0, D, D, defer=True)
    cast_w("g", w_g, wb_g, 0, DFF, D, defer=True); cast_w("u", w_u, wb_u, 0, DFF, D, defer=True)
    cast_w("d", w_d, wb_d, 0, D, DFF, defer=True)

    psrr = {}

    def next_ps(lo=0, hi=8):
        k = (lo, hi)
        i = psrr.get(k, lo); psrr[k] = lo + ((i - lo + 1) % (hi - lo))
        return i

    def norm_T(ar, src_rows_ap, np_, gi, uT, uTb, tcol, xt, xtb, ub, ubb, ssc, src_buf=None):
        src_buf = src_buf or B_x
        dma(xt[:np_, :], src_rows_ap, [src_buf], [xtb], "xt")
        P.op("dve", lambda E: E.memset(ssc[:, 0:1], 0.0), w=[B_ss])
        P.op("act", lambda E: E.activation(out=ub[:np_, :], in_=xt[:np_, :], func=AF.Square, accum_out=ssc[:np_, 0:1]),
             r=[xtb], w=[ubb, B_ss])
        P.op("act", lambda E: E.activation(out=ssc[:np_, 1:2], in_=ssc[:np_, 0:1], func=AF.Ln, scale=1.0 / D, bias=epsc[:np_, 0:1]),
             r=[B_ss, B_const], w=[B_ss])
        P.op("act", lambda E: E.activation(out=ssc[:np_, 2:3], in_=ssc[:np_, 1:2], func=AF.Exp, scale=-0.5), r=[B_ss], w=[B_ss])
        P.op("act", lambda E: E.activation(out=ub[:np_, :], in_=xt[:np_, :], func=AF.Copy, scale=ssc[:np_, 2:3]),
             r=[xtb, B_ss], w=[ubb])
        for k0 in range(0, KC, 8):
            nk = min(8, KC - k0)
            pi = next_ps()
            pv = PS[pi][:].bitcast(BF16)
            for j in range(nk):
                P.op("pe", lambda E, j=j, k0=k0, pv=pv: E.transpose(out=pv[:, j * 128:j * 128 + np_],
                                                                   in_=ub[:np_, (k0 + j) * 128:(k0 + j + 1) * 128],
                                                                   identity=ident[:np_, :np_]),
                     r=[ubb, B_const], w=[PB[pi]])
            src = pv[:, 0:nk * 128].rearrange("p (k t) -> p k t", t=128)[:, :, 0:np_]
            gsl = gT[:, gi, k0:k0 + nk]
            P.op("dve", lambda E, src=src, gsl=gsl, k0=k0, nk=nk: E.tensor_tensor(
                out=uT[:, k0:k0 + nk, tcol:tcol + np_], in0=src,
                in1=gsl.unsqueeze(2).to_broadcast([128, nk, np_]), op=ALU.mult),
                r=[PB[pi], B_const], w=[uTb])

    B_ss = Buf("ss")

    def proj(actT, actb, ntok, wdram, wbuf, K, c0, ncols, mode, evac, wt, wtb, WC):
        nkc = K // 128
        kgs = [(k, min(nkc, k + 32)) for k in range(0, nkc, 32)]
        wr = [0]
        for cb0 in range(0, ncols, WC):
            wc = min(WC, ncols - cb0)
            tiles = []

            def load(kg):
                i = wr[0] % len(wt); wr[0] += 1
                ka, kb = kgs[kg]
                for k8 in range(ka, kb, 8):
                    k9 = min(kb, k8 + 8)
                    dma(wt[i][:, k8 - ka:k9 - ka, 0:wc],
                        wdram.ap()[k8 * 128:k9 * 128, c0 + cb0:c0 + cb0 + wc].rearrange("(k p) c -> p k c", p=128),
                        [wbuf], [wtb[i]], "ld_" + wtb[i].name)
                return i
            if mode == "tm":
                ntt = (ntok + 127) // 128
                for th0 in range(0, ntt, 4):
                    tts = list(range(th0, min(ntt, th0 + 4)))
                    banks = {tt: next_ps() for tt in tts}
                    for kg, (ka, kb) in enumerate(kgs):
                        if th0 == 0 or len(kgs) > 1:
                            wi = load(kg)
                            if len(kgs) == 1:
                                tiles = [wi]
                        else:
                            wi = tiles[0]
                        for tt in tts:
                            n = min(128, ntok - tt * 128)
                            for kc in range(ka, kb):
                                P.op("pe", lambda E, tt=tt, n=n, kc=kc, ka=ka, wi=wi, b=banks[tt]: E.matmul(
                                    PS[b][:n, 0:wc], lhsT=actT[:, kc, tt * 128:tt * 128 + n], rhs=wt[wi][:, kc - ka, 0:wc],
                                    start=(kc == 0), stop=(kc == nkc - 1)), r=[actb, wtb[wi]], w=[PB[banks[tt]]])
                    for tt in tts:
                        n = min(128, ntok - tt * 128)
                        evac(banks[tt], PS[banks[tt]][:n, 0:wc], tt * 128, n, cb0, wc)
            else:
                wi = load(0)
                for cc in range(0, wc, 128):
                    for tg in range(0, ntok, 512):
                        n = min(512, ntok - tg)
                        b = next_ps()
                        for kc in range(nkc):
                            P.op("pe", lambda E, kc=kc, cc=cc, tg=tg, n=n, b=b, wi=wi: E.matmul(
                                PS[b][:, 0:n], lhsT=wt[wi][:, kc, cc:cc + 128], rhs=actT[:, kc, tg:tg + n],
                                start=(kc == 0), stop=(kc == nkc - 1)), r=[actb, wtb[wi]], w=[PB[b]])
                        evac(b, PS[b][:, 0:n], tg, n, cb0 + cc, 128)

    ar = Arena()
    uT = ar.a16(KC * TB).rearrange("p (k t) -> p k t", t=TB); uTb = Buf("uT")
    WC1 = 512
    wt = [ar.a16(32 * WC1).rearrange("p (k c) -> p k c", c=WC1) for _ in range(2)]
    wtb = [Buf("wt0"), Buf("wt1")]
    ub = ar.a16(D); ubb = Buf("ub")
    ost = [ar.a16(512) for _ in range(4)]; ostb = [Buf("ost%d" % i) for i in range(4)]
    xt = ar.a32(D); xtb = Buf("xt")
    osr = [0]
    B_scr = {n: Buf("scr_" + n) for n in ("rKtm", "rVtm", "dVtm", "dKT", "rKT", "rQT", "rGT", "dQT", "gaT", "gbT", "ZrT", "ZdT", "h1", "y")}

    def mk_evac(dst, dstb, tm, row0, scale=None, func=None):
        def evac(b, pap, t0, n, c0, ncol):
            i = osr[0] % 4; osr[0] += 1
            if tm:
                o = ost[i][:n, 0:ncol]
            else:
                o = ost[i][:, 0:n]
            if func is not None:
                P.op("act", lambda E: E.activation(out=o, in_=pap, func=func), r=[PB[b]], w=[ostb[i]])
            else:
                copy_op(ev_eng(), o, pap, [PB[b]], [ostb[i]], scale)
            if tm:
                dma(dst.ap()[row0 + t0:row0 + t0 + n, c0:c0 + ncol], o, [ostb[i]], [dstb], "st_" + dstb.name, eng="act")
            else:
                dma(dst.ap()[c0:c0 + ncol, row0 + t0:row0 + t0 + n], o, [ostb[i]], [dstb], "st_" + dstb.name, eng="act")
        return evac

    blocks = [(r0, TB) for r0 in range(0, S, TB)] + [(S, N_META)]

    def stage1_block(r0, ntok):
        own = r0 < T
        for t0 in range(0, ntok, 128):
            np_ = min(128, ntok - t0)
            src = (x_rot.ap()[r0 + t0:r0 + t0 + np_, :] if r0 < S else meta.ap()[0:np_, :])
            norm_T(ar, src, np_, 0, uT, uTb, t0, xt, xtb, ub, ubb, sm)
        def P_(g, dst, tm, scale=None, func=None, row0=r0):
            next_cast()
            proj(uT, uTb, ntok, wb_in[g], WB[g], D, 0, D, "tm" if tm else "fm",
                 mk_evac(dst, B_scr[dst.name], tm, row0, scale, func), wt, wtb, WC1)
        P_("rk", rKtm, True, scale=256.0 ** -0.5)
        P_("rv", rVtm, True)
        P_("dv", dVtm, True)
        P_("dk", dKT, False)
        if own:
            P_("rk", rKT, False, scale=256.0 ** -0.5)
            P_("rq", rQT, False)
            P_("dq", dQT, False, scale=128.0 ** -0.5)
            P_("rg", rGT, False, func=AF.Silu)
            P_("ga", gaT, False, func=AF.Sigmoid)
            P_("gb", gbT, False, func=AF.Sigmoid)
    for (r0_, ntok_) in blocks:
        stage1_block(r0_, ntok_)
    while pending_casts:
        next_cast()
    P.barrier()

    ar = Arena()
    rK = ar.a16(NKT * 256).rearrange("p (t c) -> p t c", c=256); rKb = Buf("rK")
    rV = ar.a16(NKT * 256).rearrange("p (t c) -> p t c", c=256); rVb = Buf("rV")
    kT = ar.a16(2 * T).rearrange("p (k t) -> p k t", t=T); kTb = Buf("kT")
    qT = ar.a16(2 * T).rearrange("p (k t) -> p k t", t=T); qTb = Buf("qT")
    gTt = ar.a16(2 * T).rearrange("p (k t) -> p k t", t=T); gTb = Buf("gTt")
    qf = ar.a16(2 * T).rearrange("p (k t) -> p k t", t=T); qfb = Buf("qf")
    qb_ = ar.a16(2 * T).rearrange("p (k t) -> p k t", t=T); qbb = Buf("qb")
    Sbs = ar.a16(NT * 512).rearrange("p (c k v) -> p c k v", k=2, v=256); Sbsb = Buf("Sbs")
    Sf16 = ar.a16(512).rearrange("p (k v) -> p k v", v=256); Sf16b = Buf("Sf16")
    kw = [ar.a16(256) for _ in range(2)]; kwb = [Buf("kw0"), Buf("kw1")]
    aT = [ar.a16(128) for _ in range(2)]; aTb = [Buf("aT0"), Buf("aT1")]
    zo = [ar.a16(512) for _ in range(2)]; zob = [Buf("zo0"), Buf("zo1")]
    Mh = ar.a32(128); Mhb = Buf("Mh")
    Mt = ar.a32(128)
    QD = ar.a32(256).rearrange("p (a t) -> p a t", t=128); QDb = Buf("QD")
    wfb_ = ar.a32(2 * NKT).rearrange("p (a t) -> p a t", t=NKT); wfbb = Buf("wfb")
    cdist = ar.a32(2 * NKT).rearrange("p (a t) -> p a t", t=NKT)
    kdc = ar.a32(4); kdcb = Buf("kdc")
    Sm = ar.a32(1024).rearrange("p (d k v) -> p d k v", k=2, v=256); Smb = [Buf("Smf"), Buf("Smb")]
    osq = ar.a32(1024).rearrange("p (k t) -> p k t", t=512); osqb = Buf("osq")
    orr = ar.a32(512); orrb = Buf("orr")
    ld_raw = ar.a32(2 * H).rearrange("p (a h) -> p a h", h=H); ldb = Buf("ld")
    dma(cdist[:], c_dist.ap().rearrange("a p t -> p a t"), [B_x], [B_const], "c0")
    dma(ld_raw[:].rearrange("p a h -> p (a h)"), rdecay.ap().rearrange("a h -> (a h)").partition_broadcast(128), [B_x], [ldb], "c0")
    P.op("act", lambda E: E.activation(out=ld_raw[:], in_=ld_raw[:], func=AF.Exp), r=[ldb], w=[ldb])
    P.op("dve", lambda E: E.tensor_scalar(out=ld_raw[:], in0=ld_raw[:], scalar1=-1.0, scalar2=None, op0=ALU.mult),
         r=[ldb], w=[ldb])

    def tile_np(kt):
        return N_META if kt == NKT - 1 else 128

    def flat(a):
        return a.rearrange("p k v -> p (k v)")

    def ret_upd_state(d, c):
        i = (c + d) % 2
        P.op("dve", lambda E: E.tensor_scalar(out=kw[i][:, :], in0=rK[:, c, :], scalar1=kdc[:, d:d + 1], scalar2=None,
                                               op0=ALU.mult), r=[rKb, kdcb], w=[kwb[i]])
        b = next_ps(4, 8)
        for dk in range(2):
            P.op("pe", lambda E, dk=dk: E.matmul(PS[b][:, dk * 256:(dk + 1) * 256], lhsT=kw[i][:, dk * 128:(dk + 1) * 128],
                                                 rhs=rV[:, c, :], start=(dk == 0), stop=(dk == 1)),
                 r=[kwb[i], rVb], w=[PB[b]])
        P.op("dve", lambda E: E.scalar_tensor_tensor(
            out=flat(Sm[:, d, :, :]), in0=flat(Sm[:, d, :, :]),
            scalar=kdc[:, 2 + d:3 + d], in1=PS[b][:, :], op0=ALU.mult, op1=ALU.add),
            r=[Smb[d], PB[b], kdcb], w=[Smb[d]])

    def ret_chunk(h, c, c4, po):
        cs = slice(c * 128, (c + 1) * 128)
        P.op("act", lambda E: E.activation(out=flat(Sf16[:]), in_=flat(Sm[:, 0, :, :]), func=AF.Copy),
             r=[Smb[0]], w=[Sf16b])
        bs = next_ps(4, 8)
        for dk in range(2):
            P.op("pe", lambda E, dk=dk: E.matmul(PS[bs][:, 0:128], lhsT=kT[:, dk, cs], rhs=qT[:, dk, cs],
                                                 start=(dk == 0), stop=(dk == 1)), r=[kTb, qTb], w=[PB[bs]])
        ia = c % 2
        P.op("dve", lambda E: E.tensor_tensor(out=aT[ia][:], in0=PS[bs][:, 0:128], in1=Mh[:], op=ALU.mult),
             r=[PB[bs], Mhb], w=[aTb[ia]])
        for vh in range(2):
            vs = slice(vh * 128, (vh + 1) * 128)
            oc = slice((c - c4) * 128, (c - c4 + 1) * 128)
            seq = [(rV[:, c, vs], aT[ia][:], [rVb, aTb[ia]])]
            for dk in range(2):
                seq.append((Sf16[:, dk, vs], qf[:, dk, cs], [Sf16b, qfb]))
                seq.append((Sbs[:, c, dk, vs], qb_[:, dk, cs], [Sbsb, qbb]))
            ns = len(seq)
            for j, (l, r_, rb_) in enumerate(seq):
                P.op("pe", lambda E, l=l, r_=r_, j=j, vh=vh, oc=oc: E.matmul(
                    PS[po[vh]][:, oc], lhsT=l, rhs=r_, start=(j == 0), stop=(j == ns - 1)),
                    r=rb_, w=[PB[po[vh]]])
        if c < NT - 1:
            ret_upd_state(0, c)

    def ret_group(h, c4):
        po = [2, 3]
        c5 = min(NT, c4 + 4)
        for c in range(c4, c5):
            ret_chunk(h, c, c4, po)
        n = (c5 - c4) * 128
        ts_ = slice(c4 * 128, c4 * 128 + n)
        for vh in range(2):
            P.op("act", lambda E, vh=vh: E.activation(out=osq[:, vh, 0:n], in_=PS[po[vh]][:, 0:n], func=AF.Square),
                 r=[PB[po[vh]]], w=[osqb])
        br = next_ps(4, 8)
        for vh in range(2):
            P.op("pe", lambda E, vh=vh: E.matmul(PS[br][:, 0:n], lhsT=ones32[:], rhs=osq[:, vh, 0:n],
                                                 start=(vh == 0), stop=(vh == 1)), r=[osqb, B_const], w=[PB[br]])
        P.op("act", lambda E: E.activation(out=orr[:, 0:n], in_=PS[br][:, 0:n], func=AF.Ln, scale=1.0 / 256, bias=epsc[:, 0:1]),
             r=[PB[br], B_const], w=[orrb])
        P.op("act", lambda E: E.activation(out=orr[:, 0:n], in_=orr[:, 0:n], func=AF.Exp, scale=-0.5), r=[orrb], w=[orrb])
        for vh in range(2):
            P.op("dve", lambda E, vh=vh: E.tensor_tensor(out=osq[:, vh, 0:n], in0=PS[po[vh]][:, 0:n], in1=orr[:, 0:n],
                                                         op=ALU.mult), r=[PB[po[vh]], orrb], w=[osqb])
            P.op("dve", lambda E, vh=vh: E.tensor_tensor(out=zo[vh][:, 0:n], in0=osq[:, vh, 0:n], in1=gTt[:, vh, ts_],
                                                         op=ALU.mult), r=[osqb, gTb], w=[zob[vh]])
            dma(ZrT.ap()[h * 256 + vh * 128:h * 256 + (vh + 1) * 128, ts_], zo[vh][:, 0:n], [zob[vh]], [B_scr["ZrT"]], "st_ZrT", eng="act")

    def ret_head(h):
        hc = slice(h * 256, (h + 1) * 256)
        for t8 in range(0, NTALL, 16):
            t9 = min(NTALL, t8 + 16)
            dma(rK[:, t8:t9, :], rKtm.ap()[t8 * 128:t9 * 128, hc].rearrange("(t p) c -> p t c", p=128), [B_scr["rKtm"]], [rKb], "rK")
        dma(rK[:N_META, NTALL, :], rKtm.ap()[S:S + N_META, hc], [B_scr["rKtm"]], [rKb], "rK")
        for t8 in range(0, NTALL, 16):
            t9 = min(NTALL, t8 + 16)
            dma(rV[:, t8:t9, :], rVtm.ap()[t8 * 128:t9 * 128, hc].rearrange("(t p) c -> p t c", p=128), [B_scr["rVtm"]], [rVb], "rV")
        dma(rV[:N_META, NTALL, :], rVtm.ap()[S:S + N_META, hc], [B_scr["rVtm"]], [rVb], "rV")
        for (dst, dstb, srcT, nm) in ((kT, kTb, rKT, "rKT"), (qT, qTb, rQT, "rQT"), (gTt, gTb, rGT, "rGT")):
            dma(dst[:], srcT.ap()[hc, :].rearrange("(k p) t -> p k t", p=128), [B_scr[nm]], [dstb], "r" + nm)
        for d in range(2):
            lg = ld_raw[:, d, h:h + 1]
            P.op("act", lambda E, d=d, lg=lg: E.activation(out=wfb_[:, d, :], in_=cdist[:, d, :], func=AF.Exp, scale=lg),
                 r=[B_const, ldb], w=[wfbb])
            P.op("act", lambda E, d=d, lg=lg: E.activation(out=QD[:, d, :], in_=csq[:, 2 + d, :], func=AF.Exp, scale=lg),
                 r=[B_const, ldb], w=[QDb])
            P.op("act", lambda E, d=d, lg=lg: E.activation(out=kdc[:, d:d + 1], in_=ckd[:, d:d + 1], func=AF.Exp, scale=lg),
                 r=[B_const, ldb], w=[kdcb])
            P.op("act", lambda E, d=d, lg=lg: E.activation(out=kdc[:, 2 + d:3 + d], in_=lg, func=AF.Exp, scale=128.0),
                 r=[B_const, ldb], w=[kdcb])
        P.op("act", lambda E: E.activation(out=Mh[:], in_=csq[:, 0, :], func=AF.Exp, scale=ld_raw[:, 0, h:h + 1]),
             r=[B_const, ldb], w=[Mhb])
        P.op("act", lambda E: E.activation(out=Mt[:], in_=csq[:, 1, :], func=AF.Exp, scale=ld_raw[:, 1, h:h + 1]),
             r=[B_const, ldb, Mhb], w=[Mhb])
        P.op("dve", lambda E: E.tensor_tensor(out=Mh[:], in0=Mh[:], in1=Mt[:], op=ALU.add), r=[Mhb], w=[Mhb])
        for d, (dst, dstb) in enumerate(((qf, qfb), (qb_, qbb))):
            P.op("dve", lambda E, d=d, dst=dst: E.tensor_tensor(
                out=dst[:].rearrange("p k (c i) -> p (k c) i", i=128),
                in0=qT[:].rearrange("p k (c i) -> p (k c) i", i=128),
                in1=QD[:, d, :].unsqueeze(1).to_broadcast([128, 2 * NT, 128]), op=ALU.mult),
                r=[qTb, QDb], w=[dstb])
        pin = [0, 1]
        for d in range(2):
            for kt in range(NT, NKT):
                np_ = tile_np(kt)
                i = (kt + d) % 2
                P.op("dve", lambda E, i=i, kt=kt, d=d, np_=np_: E.tensor_scalar(
                    out=kw[i][:np_, :], in0=rK[:np_, kt, :], scalar1=wfb_[:np_, d, kt:kt + 1], scalar2=None, op0=ALU.mult),
                    r=[rKb, wfbb], w=[kwb[i]])
                for dk in range(2):
                    P.op("pe", lambda E, i=i, kt=kt, d=d, dk=dk, np_=np_: E.matmul(
                        PS[pin[d]][:, dk * 256:(dk + 1) * 256], lhsT=kw[i][:np_, dk * 128:(dk + 1) * 128],
                        rhs=rV[:np_, kt, :], start=(kt == NT and dk == 0), stop=(kt == NKT - 1 and dk == 1),
                        skip_group_check=True),
                        r=[kwb[i], rVb], w=[PB[pin[d]]])
            P.op("dve", lambda E, d=d: E.tensor_copy(out=flat(Sm[:, d, :, :]), in_=PS[pin[d]][:, :]),
                 r=[PB[pin[d]]], w=[Smb[d]])
        for c in range(NT - 1, -1, -1):
            P.op("act", lambda E, c=c: E.activation(out=flat(Sbs[:, c, :, :]), in_=flat(Sm[:, 1, :, :]), func=AF.Copy),
                 r=[Smb[1]], w=[Sbsb])
            if c > 0:
                ret_upd_state(1, c)
        for c4 in range(0, NT, 4):
            ret_group(h, c4)

    for h in range(H):
        ret_head(h)
    P.barrier()

    ar = Arena()
    k1 = ar.a16(NKT * 128); k2 = ar.a16(NKT * 128); kkb = Buf("kk")
    vv = ar.a16(NKT * 256).rearrange("p (t c) -> p t c", c=256); vvb = Buf("vv")
    sg = [ar.a16(NKT * 128) for _ in range(2)]; sgb = [Buf("sg0"), Buf("sg1")]
    qq = [ar.a16(1024).rearrange("p (m t) -> p m t", t=512) for _ in range(2)]; qqb = [Buf("qq0"), Buf("qq1")]
    rbt = ar.a16(512); rbb = Buf("rb")
    ee = [ar.a16(512) for _ in range(4)]; eeb = [Buf("ee%d" % i) for i in range(4)]
    zd = [ar.a16(512) for _ in range(2)]; zdb = [Buf("zd0"), Buf("zd1")]
    babs = ar.a32(4 * 512).rearrange("p (a t) -> p a t", t=512)
    absd = ar.a32(NQB * NKT).rearrange("p (a t) -> p a t", t=NKT)
    bcol = ar.a32(NKT); bcolb = Buf("bcol")
    zacc = [ar.a32(512) for _ in range(2)]; zaccb = [Buf("zacc0"), Buf("zacc1")]
    sp_ = [ar.a32(512) for _ in range(2)]; spb = [Buf("sp0"), Buf("sp1")]
    o32 = ar.a32(1024).rearrange("p (k t) -> p k t", t=512); o32b = Buf("o32")
    t32 = ar.a32(1024).rearrange("p (k t) -> p k t", t=512); t32b = Buf("t32")
    rz = ar.a32(1024).rearrange("p (k t) -> p k t", t=512); rzb = Buf("rz")
    lam = ar.a32(8); lamb = Buf("lam")
    sgc = ar.a32(2); sgcb = Buf("sgc")
    lraw = ar.a32(512)
    dma(babs[:], c_babs.ap().rearrange("a p t -> p a t"), [B_x], [B_const], "c0")
    dma(absd[:], c_absd.ap().rearrange("a p t -> p a t"), [B_x], [B_const], "c0")
    dma(sgc[:], subln.ap().rearrange("(k p) -> p k", p=128), [B_x], [sgcb], "c0")
    dma(lraw[:], dlam.ap().rearrange("a f -> (a f)").partition_broadcast(128), [B_x], [lamb], "c0")
    P.op("dve", lambda E: E.tensor_tensor(out=lraw[:, 0:128], in0=lraw[:, 0:128], in1=lraw[:, 128:256], op=ALU.mult), r=[lamb], w=[lamb])
    P.op("dve", lambda E: E.tensor_tensor(out=lraw[:, 256:384], in0=lraw[:, 256:384], in1=lraw[:, 384:512], op=ALU.mult), r=[lamb], w=[lamb])
    P.op("dve", lambda E: E.reduce_sum(out=lam[:, 0:1], in_=lraw[:, 0:128], axis=mybir.AxisListType.X), r=[lamb], w=[lamb])
    P.op("dve", lambda E: E.reduce_sum(out=lam[:, 1:2], in_=lraw[:, 256:384], axis=mybir.AxisListType.X), r=[lamb], w=[lamb])
    P.op("act", lambda E: E.activation(out=lam[:, 2:4], in_=lam[:, 0:2], func=AF.Exp), r=[lamb], w=[lamb])
    P.op("dve", lambda E: E.tensor_tensor(out=lam[:, 4:5], in0=lam[:, 3:4], in1=lam[:, 2:3], op=ALU.subtract), r=[lamb], w=[lamb])
    P.op("dve", lambda E: E.tensor_scalar(out=lam[:, 4:5], in0=lam[:, 4:5], scalar1=-cfg.LAM_INIT, scalar2=None, op0=ALU.add), r=[lamb], w=[lamb])
    P.op("dve", lambda E: E.tensor_scalar(out=sgc[:], in0=sgc[:], scalar1=1.0 - cfg.LAM_INIT, scalar2=None, op0=ALU.mult),
         r=[sgcb], w=[sgcb])
    SL = slopes(cfg)

    def diff_S(h, qb, qi, kt):
        np_ = tile_np(kt)
        diag = (qb * 4 <= kt < qb * 4 + 4)
        for m, kk in enumerate((k1, k2)):
            bS = next_ps(4, 8)
            ie = (2 * kt + m) % 4
            P.op("pe", lambda E, kk=kk, bS=bS, m=m: E.matmul(
                PS[bS][:np_, :], lhsT=kk[:, kt * 128:kt * 128 + np_], rhs=qq[qi][:, m, :], start=True, stop=diag),
                r=[kkb, qqb[qi]], w=[PB[bS]])
            if not diag:
                P.op("pe", lambda E, bS=bS: E.matmul(
                    PS[bS][:np_, :], lhsT=sg[qi][0:2, kt * 128:kt * 128 + np_], rhs=rbt[0:2, :], start=False, stop=True),
                    r=[sgb[qi], rbb], w=[PB[bS]])
                P.op("act", lambda E, bS=bS, ie=ie: E.activation(
                    out=ee[ie][:np_, :], in_=PS[bS][:np_, :], func=AF.Exp, bias=bcol[:np_, kt:kt + 1]),
                    r=[PB[bS], bcolb], w=[eeb[ie]])
            else:
                ci = kt - qb * 4
                P.op("dve", lambda E, bS=bS, m=m: E.scalar_tensor_tensor(
                    out=sp_[m][:], in0=babs[:, ci, :], scalar=-SL[h], in1=PS[bS][:, :], op0=ALU.mult, op1=ALU.add),
                    r=[PB[bS], B_const], w=[spb[m]])
                P.op("act", lambda E, m=m, ie=ie: E.activation(out=ee[ie][:], in_=sp_[m][:], func=AF.Exp),
                     r=[spb[m]], w=[eeb[ie]])

    def diff_AV(h, qb, qi, kt, pO):
        np_ = tile_np(kt)
        for m in range(2):
            ie = (2 * kt + m) % 4
            if kt == 0:
                P.op("dve", lambda E, m=m, ie=ie: E.tensor_copy(out=zacc[m][:], in_=ee[ie][:]), r=[eeb[ie]], w=[zaccb[m]])
            else:
                P.op("dve", lambda E, m=m, ie=ie: E.tensor_tensor(out=zacc[m][:np_, :], in0=zacc[m][:np_, :],
                                                                 in1=ee[ie][:np_, :], op=ALU.add),
                     r=[eeb[ie], zaccb[m]], w=[zaccb[m]])
            for vh in range(2):
                P.op("pe", lambda E, vh=vh, m=m, ie=ie: E.matmul(
                    PS[pO[m][vh]][:, :], lhsT=vv[:np_, kt, vh * 128:(vh + 1) * 128], rhs=ee[ie][:np_, :],
                    start=(kt == 0), stop=(kt == NKT - 1)), r=[vvb, eeb[ie]], w=[PB[pO[m][vh]]])

    def diff_iter(h, qb, qi):
        q0 = qb * 512
        dma(qq[qi][:], dQT.ap()[h * 256:(h + 1) * 256, q0:q0 + 512].rearrange("(m p) t -> p m t", p=128),
            [B_scr["dQT"]], [qqb[qi]], "qq%d" % qi)
        dma(sg[qi][0:2, :], c_sgn.ap()[qb], [B_x], [sgb[qi]], "sg%d" % qi)
        P.op("dve", lambda E: E.tensor_scalar(out=bcol[:], in0=absd[:, qb, :], scalar1=-SL[h], scalar2=None,
                                               op0=ALU.mult), r=[B_const], w=[bcolb])
        pO = [[0, 1], [2, 3]]
        for kt in range(NKT + 1):
            if kt < NKT:
                diff_S(h, qb, qi, kt)
            if kt >= 1:
                diff_AV(h, qb, qi, kt - 1, pO)
        for m in range(2):
            bz = next_ps(4, 8)
            P.op("pe", lambda E, m=m, bz=bz: E.matmul(PS[bz][:, :], lhsT=ones32[:], rhs=zacc[m][:], start=True, stop=True),
                 r=[zaccb[m], B_const], w=[PB[bz]])
            P.op("dve", lambda E, m=m, bz=bz: E.reciprocal(out=rz[:, m, :], in_=PS[bz][:, :]), r=[PB[bz]], w=[rzb])
        for vh in range(2):
            P.op("dve", lambda E, vh=vh: E.tensor_tensor(out=t32[:, vh, :], in0=PS[pO[1][vh]][:, :], in1=rz[:, 1, :], op=ALU.mult),
                 r=[PB[pO[1][vh]], rzb], w=[t32b])
            P.op("dve", lambda E, vh=vh: E.tensor_tensor(out=o32[:, vh, :], in0=PS[pO[0][vh]][:, :], in1=rz[:, 0, :], op=ALU.mult),
                 r=[PB[pO[0][vh]], rzb], w=[o32b])
            P.op("dve", lambda E, vh=vh: E.scalar_tensor_tensor(out=o32[:, vh, :], in0=t32[:, vh, :], scalar=lam[:, 4:5],
                                                                in1=o32[:, vh, :], op0=ALU.mult, op1=ALU.add),
                 r=[t32b, o32b, lamb], w=[o32b])
            P.op("act", lambda E, vh=vh: E.activation(out=t32[:, vh, :], in_=o32[:, vh, :], func=AF.Square), r=[o32b, t32b], w=[t32b])
        br = next_ps(4, 8)
        for vh in range(2):
            P.op("pe", lambda E, vh=vh: E.matmul(PS[br][:, :], lhsT=ones32[:], rhs=t32[:, vh, :], start=(vh == 0), stop=(vh == 1)),
                 r=[t32b, B_const], w=[PB[br]])
        P.op("act", lambda E: E.activation(out=rz[:, 0, :], in_=PS[br][:, :], func=AF.Ln, scale=1.0 / 256, bias=epsc[:, 0:1]),
             r=[PB[br], rzb, B_const], w=[rzb])
        P.op("act", lambda E: E.activation(out=rz[:, 0, :], in_=rz[:, 0, :], func=AF.Exp, scale=-0.5), r=[rzb], w=[rzb])
        for vh in range(2):
            P.op("dve", lambda E, vh=vh: E.scalar_tensor_tensor(out=zd[vh][:], in0=o32[:, vh, :], scalar=sgc[:, vh:vh + 1],
                                                                in1=rz[:, 0, :], op0=ALU.mult, op1=ALU.mult),
                 r=[o32b, rzb, sgcb], w=[zdb[vh]])
            dma(ZdT.ap()[h * 256 + vh * 128:h * 256 + (vh + 1) * 128, q0:q0 + 512], zd[vh][:], [zdb[vh]], [B_scr["ZdT"]], "st_ZdT", eng="act")

    def diff_head(h, it0):
        dma(k1[:, 0:S + N_META], dKT.ap()[h * 256:h * 256 + 128, 0:S + N_META], [B_scr["dKT"]], [kkb], "kk")
        dma(k2[:, 0:S + N_META], dKT.ap()[h * 256 + 128:h * 256 + 256, 0:S + N_META], [B_scr["dKT"]], [kkb], "kk")
        for t8 in range(0, NTALL, 16):
            t9 = min(NTALL, t8 + 16)
            dma(vv[:, t8:t9, :], dVtm.ap()[t8 * 128:t9 * 128, h * 256:(h + 1) * 256].rearrange("(t p) c -> p t c", p=128), [B_scr["dVtm"]], [vvb], "vv")
        dma(vv[:N_META, NTALL, :], dVtm.ap()[S:S + N_META, h * 256:(h + 1) * 256], [B_scr["dVtm"]], [vvb], "vv")
        dma(rbt[0:2, :], c_rb.ap()[h], [B_x], [rbb], "rb")
        for qb in range(NQB):
            diff_iter(h, qb, (it0 + qb) % 2)

    for h in range(H):
        diff_head(h, h * NQB)
    P.barrier()

    ar = Arena()
    TF = 512
    WC4 = 512
    zT = ar.a16(KC * TF).rearrange("p (k t) -> p k t", t=TF); zTb = Buf("zT")
    mxbase = ar.a16(max(KC, 32) * TF).rearrange("p (k t) -> p k t", t=TF); mxTb = Buf("mxT")
    mxT = mxbase[:, 0:KC, :]
    aTf = mxbase; aTfb = mxTb
    wt4 = [ar.a16(32 * WC4).rearrange("p (k c) -> p k c", c=WC4) for _ in range(2)]
    wt4b = [Buf("w40"), Buf("w41")]
    gg = [ar.a16(TF) for _ in range(2)]; ggb = [Buf("gg0"), Buf("gg1")]
    ub4 = ar.a16(D); ub4b = Buf("ub4")
    gsb = ar.a16(TF); gsbb = Buf("gsb")
    xt4 = ar.a32(D); xt4b = Buf("xt4")
    gfin = ar.a32(D)
    hst = [ar.a32(WC4) for _ in range(2)]; hstb = [Buf("hst0"), Buf("hst1")]
    hrow = [ar.a32(WC4) for _ in range(2)]; hrowb = [Buf("hrow0"), Buf("hrow1")]
    B_h1 = [Buf("h1_%d" % i) for i in range(4)]
    rr = [0]
    dma(gfin[:], gvec.ap()[2].partition_broadcast(128), [B_x], [B_const], "c0")

    def s4_mix(tb, pas, Zs, nm, wdr, wn, gsrc, gn):
        for k8 in range(0, KC, 8):
            k9 = min(KC, k8 + 8)
            dma(zT[:, k8:k9, :], Zs.ap()[k8 * 128:k9 * 128, tb:tb + TF].rearrange("(k p) t -> p k t", p=128), [B_scr[nm]], [zTb], "zT")

        def ev_mix(b, pap, t0, n, c0, ncol):
            i = rr[0] % 2; rr[0] += 1
            kc = c0 // 128
            dma(gg[i][:, 0:n], gsrc.ap()[c0:c0 + 128, tb + t0:tb + t0 + n], [B_scr[gn]], [ggb[i]], "gg%d" % i)
            if pas == 0:
                P.op("dve", lambda E: E.tensor_tensor(out=mxT[:, kc, t0:t0 + n], in0=pap, in1=gg[i][:, 0:n], op=ALU.mult),
                     r=[PB[b], ggb[i]], w=[mxTb])
            else:
                P.op("dve", lambda E: E.tensor_tensor(out=gsb[:, 0:n], in0=pap, in1=gg[i][:, 0:n], op=ALU.mult),
                     r=[PB[b], ggb[i]], w=[gsbb])
                P.op("dve", lambda E: E.tensor_tensor(out=mxT[:, kc, t0:t0 + n], in0=mxT[:, kc, t0:t0 + n], in1=gsb[:, 0:n], op=ALU.add),
                     r=[gsbb, mxTb], w=[mxTb])
        proj(zT, zTb, TF, wdr, WB[wn], D, 0, D, "fm", ev_mix, wt4, wt4b, WC4)

    def s4_seg(tb, f0, f1):
        def ev_gate(b, pap, t0, n, c0, ncol):
            kc = (c0 // 128)
            P.op("act", lambda E: E.activation(out=aTf[:, kc, t0:t0 + n], in_=pap, func=AF.Silu), r=[PB[b]], w=[aTfb])

        def ev_up(b, pap, t0, n, c0, ncol):
            kc = (c0 // 128)
            P.op("dve", lambda E: E.tensor_tensor(out=aTf[:, kc, t0:t0 + n], in0=pap, in1=aTf[:, kc, t0:t0 + n], op=ALU.mult),
                 r=[PB[b], aTfb], w=[aTfb])

        class _W:
            def __init__(s, a):
                s.a = a

            def ap(s):
                return s.a
        proj(zT, zTb, TF, _W(wb_g.ap()[:, f0 * 128:f1 * 128]), WB["g"], D, 0, (f1 - f0) * 128, "fm", ev_gate, wt4, wt4b, WC4)
        proj(zT, zTb, TF, _W(wb_u.ap()[:, f0 * 128:f1 * 128]), WB["u"], D, 0, (f1 - f0) * 128, "fm", ev_up, wt4, wt4b, WC4)

        def ev_dn(b, pap, t0, n, c0, ncol):
            i = rr[0] % 2; rr[0] += 1
            hb = B_h1[t0 // 128]
            dma(hrow[i][:n, 0:ncol], h1.ap()[tb + t0:tb + t0 + n, c0:c0 + ncol], [hb], [hrowb[i]], "hrow%d" % i)
            P.op("dve", lambda E: E.tensor_tensor(out=hst[i][:n, 0:ncol], in0=pap, in1=hrow[i][:n, 0:ncol], op=ALU.add),
                 r=[PB[b], hrowb[i]], w=[hstb[i]])
            dma(h1.ap()[tb + t0:tb + t0 + n, c0:c0 + ncol], hst[i][:n, 0:ncol], [hstb[i]], [hb], "st_h1_%d" % (t0 // 128), eng="act")
        proj(aTf, aTfb, TF, _W(wb_d.ap()[f0 * 128:f1 * 128, :]), WB["d"], (f1 - f0) * 128, 0, D, "tm", ev_dn, wt4, wt4b, WC4)

    def s4_block(tb):
        s4_mix(tb, 0, ZrT, "ZrT", wb_br, "br", gaT, "gaT")
        s4_mix(tb, 1, ZdT, "ZdT", wb_bd, "bd", gbT, "gbT")

        def ev_h1(b, pap, t0, n, c0, ncol):
            i = rr[0] % 2; rr[0] += 1
            hb = B_h1[t0 // 128]
            dma(hrow[i][:n, 0:ncol], x_rot.ap()[tb + t0:tb + t0 + n, c0:c0 + ncol], [B_x], [hrowb[i]], "hrow%d" % i)
            P.op("dve", lambda E: E.tensor_tensor(out=hst[i][:n, 0:ncol], in0=pap, in1=hrow[i][:n, 0:ncol], op=ALU.add),
                 r=[PB[b], hrowb[i]], w=[hstb[i]])
            dma(h1.ap()[tb + t0:tb + t0 + n, c0:c0 + ncol], hst[i][:n, 0:ncol], [hstb[i]], [hb], "st_h1_%d" % (t0 // 128), eng="act")
        proj(mxT, mxTb, TF, wb_out, WB["out"], D, 0, D, "tm", ev_h1, wt4, wt4b, WC4)
        for t0 in range(0, TF, 128):
            norm_T(ar, h1.ap()[tb + t0:tb + t0 + 128, :], 128, 1, zT, zTb, t0, xt4, xt4b, ub4, ub4b, sm, src_buf=B_h1[t0 // 128])
        for f0 in range(0, FC, 32):
            s4_seg(tb, f0, min(FC, f0 + 32))
        for t0 in range(0, TF, 128):
            s4_fin(tb, t0)

    def s4_fin(tb, t0):
        dma(xt4[:, :], h1.ap()[tb + t0:tb + t0 + 128, :], [B_h1[t0 // 128]], [xt4b], "xt")
        P.op("dve", lambda E: E.memset(sm[:, 0:1], 0.0), w=[B_ss])
        P.op("act", lambda E: E.activation(out=ub4[:, :], in_=xt4[:, :], func=AF.Square, accum_out=sm[:, 0:1]),
             r=[xt4b], w=[ub4b, B_ss])
        P.op("act", lambda E: E.activation(out=sm[:, 1:2], in_=sm[:, 0:1], func=AF.Ln, scale=1.0 / D, bias=epsc[:, 0:1]),
             r=[B_ss, B_const], w=[B_ss])
        P.op("act", lambda E: E.activation(out=sm[:, 2:3], in_=sm[:, 1:2], func=AF.Exp, scale=-0.5), r=[B_ss], w=[B_ss])
        P.op("dve", lambda E: E.scalar_tensor_tensor(out=xt4[:, :], in0=xt4[:, :], scalar=sm[:, 2:3], in1=gfin[:, :],
                                                     op0=ALU.mult, op1=ALU.mult), r=[xt4b, B_ss, B_const], w=[xt4b])
        dma(y.ap()[tb + t0:tb + t0 + 128, :], xt4[:, :], [xt4b], [B_scr["y"]], "st_y", eng="act")

    for tb in range(0, T, TF):
        s4_block(tb)

    with nc.allow_non_contiguous_dma(reason="small strided constant/layout loads"):
        with nc.Block() as block:
            sems = P.emit(nc, block)
        sems.close()
    stack.close()
    return nc


def host_consts(cfg, qt):
    S, T, H, NKT, NQB, NTALL = cfg.S, cfg.T, cfg.H, cfg.NKT, cfg.NQB, cfg.NTALL
    own0 = qt * T
    perm = np.concatenate([np.arange(own0, own0 + T), np.arange(0, own0), np.arange(own0 + T, S)])
    kpos = np.full((NKT, 128), 1e9, np.float64)
    kpos[:NTALL] = (N_META + perm).reshape(NTALL, 128)
    kpos[NTALL, :N_META] = np.arange(N_META)
    absd = np.zeros((NQB, 128, NKT), np.float32)
    sgn = np.zeros((NQB, 2, NKT * 128), np.float32)
    for qb in range(NQB):
        qc = N_META + own0 + qb * 512 + 255.5
        absd[qb] = np.abs(kpos - qc).T
        sg = np.where(kpos < qc, 1.0, -1.0).reshape(-1)
        sgn[qb, 0] = sg; sgn[qb, 1] = sg
    absd = np.minimum(absd, 1e6).astype(np.float32)
    sl = np.array(slopes(cfg), np.float64)
    qrel = np.arange(512) - 255.5
    val = (-sl[:, None] * qrel[None, :]).astype(np.float32)
    hi = val.astype(ml_dtypes.bfloat16)
    lo = (val - hi.astype(np.float32)).astype(ml_dtypes.bfloat16)
    rb = np.stack([hi, lo], axis=1)
    p = np.arange(128)[:, None]; f = np.arange(512)[None, :]
    babs = np.stack([np.abs(ci * 128 + p - f) for ci in range(4)]).astype(np.float32)
    p0 = N_META + own0; p1 = p0 + T
    distf = np.where(kpos < p0, p0 - 1 - kpos, BIG)
    distb = np.where((kpos >= p1) & (kpos < 1e8), kpos - p1, BIG)
    dist = np.stack([distf.T, distb.T]).astype(np.float32)
    s_ = np.arange(128)[:, None]; t_ = np.arange(128)[None, :]
    dpos = np.where(t_ >= s_, t_ - s_, BIG)
    dneg = np.where(s_ > t_, s_ - t_, BIG)
    qdf = np.broadcast_to(t_ + 1.0, (128, 128))
    qdb = np.broadcast_to(128.0 - t_, (128, 128))
    sq = np.stack([dpos, dneg, qdf, qdb]).astype(np.float32)
    kd = np.stack([127.0 - np.arange(128), np.arange(128) * 1.0], axis=1).astype(np.float32)
    return perm, dict(c_absd=absd, c_sgn=sgn.astype(ml_dtypes.bfloat16), c_rb=rb, c_babs=babs, c_dist=dist,
                      c_sq=sq, c_kd=kd, c_ident=np.eye(128, dtype=np.float32).astype(ml_dtypes.bfloat16))


_NC_CACHE = {}


def run(cfg, inp):
    key = (cfg.D, cfg.S)
    if key not in _NC_CACHE:
        _NC_CACHE[key] = build(cfg)
    nc = _NC_CACHE[key]
    f32 = lambda a: np.ascontiguousarray(np.asarray(a, dtype=np.float32))
    x = f32(inp["x"])
    shared = dict(
        meta=f32(inp["meta_tokens"]),
        gvec=np.stack([f32(inp["norm_mix_g"])[0], f32(inp["norm_ffn_g"])[0], f32(inp["norm_final_g"])]),
        w_in=f32(inp["w_in"])[0], w_br=f32(inp["w_branch_ret"])[0], w_bd=f32(inp["w_branch_diff"])[0],
        w_out=f32(inp["w_out"])[0], w_g=f32(inp["w_ffn_gate"])[0], w_u=f32(inp["w_ffn_up"])[0],
        w_d=f32(inp["w_ffn_down"])[0], rdecay=f32(inp["ret_log_decay"])[0], dlam=f32(inp["diff_lambda"])[0],
        subln=f32(inp["diff_subln_g"])[0])
    in_maps = []
    for c in range(8):
        b, qt = c // 4, c % 4
        perm, consts = host_consts(cfg, qt)
        m = dict(shared)
        m["x_rot"] = np.ascontiguousarray(x[b][perm])
        m.update(consts)
        in_maps.append(m)
    res = run_bass_kernel_spmd(nc, in_maps, core_ids=list(range(8)))
    out = np.zeros((2, cfg.S, cfg.D), np.float32)
    for c in range(8):
        b, qt = c // 4, c % 4
        out[b, qt * cfg.T:(qt + 1) * cfg.T] = res.results[c]["y"]
    return out


def kernel(**inputs):
    return run(Cfg(4096, 8192), inputs)
```

```python
import math, contextlib
import numpy as np
import ml_dtypes
import concourse.bass as bass
import concourse.mybir as mybir
from concourse.bass_utils import run_bass_kernel_spmd

F32 = mybir.dt.float32
BF16 = mybir.dt.bfloat16
AF = mybir.ActivationFunctionType
ALU = mybir.AluOpType
BIG = 1.0e30
EPS = 1e-6
N_META = 16


class Cfg:
    def __init__(s, D=4096, S=8192):
        s.D, s.S = D, S
        s.T = S // 4
        s.H = D // 256
        s.DIN = 9 * D
        s.DFF = ((8 * D + 3 * 256 - 1) // (3 * 256)) * 256
        s.KC = D // 128
        s.FC = s.DFF // 128
        s.NT = s.T // 128
        s.NTALL = S // 128
        s.NKT = s.NTALL + 1
        s.NQB = s.T // 512
        s.TB = min(1024, s.T)
        s.LAM_INIT = 0.8 - 0.6 * math.exp(-0.3 * 0)


class Buf:
    __slots__ = ("name",)

    def __init__(s, name):
        s.name = name


class Prog:
    COMPUTE = ("pe", "act", "dve", "pool")

    def __init__(s):
        s.ops = []
        s.lastw = {}
        s.rd_eng = {}
        s.rd_dma = {}
        s.dmacnt = {}
        s.bar = None
        s.bar_done = set()

    def op(s, eng, fn, r=(), w=(), dma=None):
        i = len(s.ops)
        deps = set()
        for b in tuple(r) + tuple(w):
            lw = s.lastw.get(b)
            if lw is not None:
                deps.add(lw)
        for b in w:
            for x in s.rd_eng.get(b, {}).values():
                deps.add(x)
            for x in s.rd_dma.get(b, ()):
                deps.add(x)
        o = dict(eng=eng, fn=fn, deps=deps, dma=dma, mark=False, bar=None)
        if s.bar is not None and eng not in s.bar_done:
            o["bar"] = s.bar
            s.bar_done.add(eng)
        if dma is not None:
            s.dmacnt[dma] = s.dmacnt.get(dma, 0) + 1
            o["dval"] = 16 * s.dmacnt[dma]
        s.ops.append(o)
        for b in w:
            s.lastw[b] = i
            s.rd_eng[b] = {}
            s.rd_dma[b] = []
        for b in r:
            if dma is not None:
                s.rd_dma.setdefault(b, []).append(i)
            else:
                s.rd_eng.setdefault(b, {})[eng] = i
        return i

    def barrier(s):
        last = {}
        for i, o in enumerate(s.ops):
            if o["dma"] is None:
                last[o["eng"]] = i
        s.bar = (dict(last), dict(s.dmacnt))
        s.bar_done = set()

    def emit(s, nc, block):
        ops = s.ops
        for o in ops:
            for d in o["deps"]:
                if ops[d]["dma"] is None:
                    ops[d]["mark"] = True
            if o["bar"] is not None:
                for e, d in o["bar"][0].items():
                    ops[d]["mark"] = True
        cnt = {e: 0 for e in ("pe", "act", "dve", "pool", "sp")}
        for o in ops:
            if o["dma"] is None:
                if o["mark"]:
                    cnt[o["eng"]] += 1
                o["sval"] = cnt[o["eng"]]
        stack = contextlib.ExitStack()
        esem = {e: stack.enter_context(nc.semaphore("s_" + e)) for e in cnt}
        dsem = {k: stack.enter_context(nc.semaphore("d_" + str(k))) for k in s.dmacnt}
        byeng = {e: [] for e in cnt}
        for o in ops:
            byeng[o["eng"]].append(o)

        def run(eng_name, E):
            known = {}

            def need(sem, val):
                if known.get(sem.name if hasattr(sem, "name") else id(sem), 0) < val:
                    E.wait_ge(sem, val)
                    known[sem.name if hasattr(sem, "name") else id(sem)] = val

            for o in byeng[eng_name]:
                if o["bar"] is not None:
                    lastc, dcnt = o["bar"]
                    for e, d in lastc.items():
                        if e != eng_name or e != "pe":
                            need(esem[e], ops[d]["sval"])
                    for k, c in dcnt.items():
                        need(dsem[k], 16 * c)
                for d in sorted(o["deps"]):
                    do = ops[d]
                    if do["dma"] is not None:
                        need(dsem[do["dma"]], do["dval"])
                    else:
                        if do["eng"] == "pe" and eng_name == "pe":
                            continue
                        need(esem[do["eng"]], do["sval"])
                ins = o["fn"](E)
                if o["dma"] is not None:
                    ins.then_inc(dsem[o["dma"]], 16)
                elif o["mark"]:
                    ins.then_inc(esem[eng_name], 1)
            if eng_name == "sp":
                for k, c in s.dmacnt.items():
                    need(dsem[k], 16 * c)

        @block.tensor
        def _(E):
            run("pe", E)

        @block.scalar
        def _(E):
            run("act", E)

        @block.vector
        def _(E):
            run("dve", E)

        @block.gpsimd
        def _(E):
            run("pool", E)

        @block.sync
        def _(E):
            run("sp", E)

        return stack


def slopes(cfg):
    return [2.0 ** (-8.0 * (h + 1) / cfg.H) for h in range(cfg.H)]


def build(cfg):
    D, S, T, H, KC, DFF, FC = cfg.D, cfg.S, cfg.T, cfg.H, cfg.KC, cfg.DFF, cfg.FC
    NKT, NTALL, NT, NQB, TB = cfg.NKT, cfg.NTALL, cfg.NT, cfg.NQB, cfg.TB
    nc = bass.Bass("TRN2", target_bir_lowering=False)
    P = Prog()

    def din(name, shape, dt=F32):
        return nc.dram_tensor(name, list(shape), dt, kind="ExternalInput")

    x_rot = din("x_rot", [S, D])
    meta = din("meta", [N_META, D])
    gvec = din("gvec", [3, D])
    w_in = din("w_in", [D, 9 * D])
    w_br = din("w_br", [D, D])
    w_bd = din("w_bd", [D, D])
    w_out = din("w_out", [D, D])
    w_g = din("w_g", [D, DFF])
    w_u = din("w_u", [D, DFF])
    w_d = din("w_d", [DFF, D])
    rdecay = din("rdecay", [2, H])
    dlam = din("dlam", [4, 128])
    subln = din("subln", [256])
    c_absd = din("c_absd", [NQB, 128, NKT])
    c_sgn = din("c_sgn", [NQB, 2, NKT * 128], BF16)
    c_rb = din("c_rb", [H, 2, 512], BF16)
    c_babs = din("c_babs", [4, 128, 512])
    c_dist = din("c_dist", [2, 128, NKT])
    c_sq = din("c_sq", [4, 128, 128])
    c_kd = din("c_kd", [128, 2])
    c_ident = din("c_ident", [128, 128], BF16)
    y = nc.dram_tensor("y", [T, D], F32, kind="ExternalOutput")

    def dscr(name, shape, dt=BF16):
        return nc.dram_tensor(name, list(shape), dt)

    wb_in = {g: dscr("wb_in_" + g, [D, D]) for g in ("rq", "rk", "rv", "rg", "dq", "dk", "dv", "ga", "gb")}; wb_br = dscr("wb_br", [D, D]); wb_bd = dscr("wb_bd", [D, D])
    wb_out = dscr("wb_out", [D, D]); wb_g = dscr("wb_g", [D, DFF]); wb_u = dscr("wb_u", [D, DFF])
    wb_d = dscr("wb_d", [DFF, D])
    NR = NKT * 128
    rKtm = dscr("rKtm", [NR, D]); rVtm = dscr("rVtm", [NR, D]); dVtm = dscr("dVtm", [NR, D])
    dKT = dscr("dKT", [D, NR])
    rKT = dscr("rKT", [D, T]); rQT = dscr("rQT", [D, T]); rGT = dscr("rGT", [D, T]); dQT = dscr("dQT", [D, T])
    gaT = dscr("gaT", [D, T]); gbT = dscr("gbT", [D, T]); ZrT = dscr("ZrT", [D, T]); ZdT = dscr("ZdT", [D, T])
    h1 = dscr("h1", [T, D], F32)

    stack = contextlib.ExitStack()
    NB16 = 78 * 1024
    NF32 = 11 * 1024
    A16 = stack.enter_context(nc.sbuf_tensor("A16", [128, NB16], BF16))
    A32 = stack.enter_context(nc.sbuf_tensor("A32", [128, NF32], F32))
    ident = stack.enter_context(nc.sbuf_tensor("ident", [128, 128], BF16))
    ones32 = stack.enter_context(nc.sbuf_tensor("ones32", [128, 128], F32))
    ones16 = stack.enter_context(nc.sbuf_tensor("ones16", [128, 128], BF16))
    gT = stack.enter_context(nc.sbuf_tensor("gT", [128, 3, KC], F32))
    csq = stack.enter_context(nc.sbuf_tensor("csq", [128, 4, 128], F32))
    ckd = stack.enter_context(nc.sbuf_tensor("ckd", [128, 2], F32))
    sm = stack.enter_context(nc.sbuf_tensor("sm", [128, 64], F32))
    epsc = stack.enter_context(nc.sbuf_tensor("epsc", [128, 2], F32))
    trigt = stack.enter_context(nc.sbuf_tensor("trigt", [128, 2], F32))
    PS = [stack.enter_context(nc.psum_tensor("ps%d" % i, [128, 512], F32)) for i in range(8)]
    PB = [Buf("ps%d" % i) for i in range(8)]
    B_const = Buf("const")

    class Arena:
        def __init__(s):
            s.o16 = 0; s.o32 = 0

        def a16(s, n, shape=None):
            v = A16[:, s.o16:s.o16 + n]; s.o16 += n
            assert s.o16 <= NB16, ("bf16 arena overflow", s.o16)
            return v

        def a32(s, n):
            v = A32[:, s.o32:s.o32 + n]; s.o32 += n
            assert s.o32 <= NF32, ("f32 arena overflow", s.o32)
            return v

    evq = [0]

    def ev_eng():
        evq[0] += 1
        return "act" if evq[0] % 2 else "dve"

    def copy_op(eng, out, in_, r, w, scale=None):
        if eng == "act":
            if scale is None:
                P.op("act", lambda E: E.activation(out=out, in_=in_, func=AF.Copy), r=r, w=w)
            else:
                P.op("act", lambda E: E.activation(out=out, in_=in_, func=AF.Copy, scale=scale), r=r, w=w)
        else:
            if scale is None:
                P.op("dve", lambda E: E.tensor_copy(out=out, in_=in_), r=r, w=w)
            else:
                P.op("dve", lambda E: E.tensor_scalar(out=out, in0=in_, scalar1=scale, scalar2=None, op0=ALU.mult), r=r, w=w)

    ckey = [0]

    def dma(out, in_, r, w, key, eng="sp"):
        if key == "c0":
            ckey[0] += 1
            key = "c%d" % ckey[0]
        P.op(eng, lambda E: E.dma_start(out=out, in_=in_), r=r, w=w, dma=key)

    B_x = Buf("x_in"); B_w = Buf("w_f32")
    dma(ident[:], c_ident.ap(), [B_x], [B_const], "c0")
    dma(csq[:], c_sq.ap().rearrange("a p f -> p a f"), [B_x], [B_const], "c0")
    dma(ckd[:], c_kd.ap(), [B_x], [B_const], "c0")
    with nc.allow_non_contiguous_dma(reason="tiny gamma relayout"):
        dma(gT[:], gvec.ap().rearrange("a (kc p) -> p a kc", p=128), [B_x], [B_const], "c0")
    P.op("dve", lambda E: E.memset(ones32[:], 1.0), w=[B_const])
    P.op("dve", lambda E: E.memset(ones16[:], 1.0), w=[B_const])
    P.op("dve", lambda E: E.memset(epsc[:], EPS), w=[B_const])

    WB = {}

    pending_casts = []
    pcnt = [0]

    def cast_w(name, src, dst, c0, c1, rows, d0=None, defer=False):
        if name not in WB:
            WB[name] = Buf("wb_" + name)
        if defer:
            pending_casts.append((name, src, dst, c0, c1, rows, d0))
            return
        b = WB[name]
        d0 = c0 if d0 is None else d0
        RB = 512 if (c1 - c0) <= 4096 else 128
        trig = Buf("trig_" + name)
        P.op("dve", lambda E: E.memset(trigt[:, 0:1], 0.0), w=[trig])
        for r0 in range(0, rows, RB):
            r1 = min(rows, r0 + RB)
            P.op("pool", lambda E, r0=r0, r1=r1: E.dma_start(out=dst.ap()[r0:r1, d0:d0 + c1 - c0], in_=src.ap()[r0:r1, c0:c1]),
                 r=[B_w, trig], w=[b], dma="cw_" + name)

    def next_cast():
        if pending_casts:
            a = pending_casts.pop(0)
            cast_w(*a, defer=False)

    GR = {"rq": 0, "rk": 1, "rv": 2, "rg": 3, "dq": 4, "dk": 5, "dv": 6, "ga": 7, "gb": 8}
    for g in ("rk", "rv", "dv", "dk", "rq", "dq", "rg", "ga", "gb"):
        cast_w(g, w_in, wb_in[g], GR[g] * D, (GR[g] + 1) * D, D, d0=0, defer=(g not in ("rk", "rv", "dv", "dk")))
    cast_w("br", w_br, wb_br, 0, D, D, defer=True); cast_w("bd", w_bd, wb_bd, 0, D, D, defer=True)
    cast_w("out", w_out, wb_out, 0, D, D, defer=True)
    cast_w("g", w_g, wb_g, 0, DFF, D, defer=True); cast_w("u", w_u, wb_u, 0, DFF, D, defer=True)
    cast_w("d", w_d, wb_d, 0, D, DFF, defer=True)

    psrr = {}

    def next_ps(lo=0, hi=8):
        k = (lo, hi)
        i = psrr.get(k, lo); psrr[k] = lo + ((i - lo + 1) % (hi - lo))
        return i

    def norm_T(ar, src_rows_ap, np_, gi, uT, uTb, tcol, xt, xtb, ub, ubb, ssc, src_buf=None):
        src_buf = src_buf or B_x
        dma(xt[:np_, :], src_rows_ap, [src_buf], [xtb], "xt")
        P.op("dve", lambda E: E.memset(ssc[:, 0:1], 0.0), w=[B_ss])
        P.op("act", lambda E: E.activation(out=ub[:np_, :], in_=xt[:np_, :], func=AF.Square, accum_out=ssc[:np_, 0:1]),
             r=[xtb], w=[ubb, B_ss])
        P.op("act", lambda E: E.activation(out=ssc[:np_, 1:2], in_=ssc[:np_, 0:1], func=AF.Ln, scale=1.0 / D, bias=epsc[:np_, 0:1]),
             r=[B_ss, B_const], w=[B_ss])
        P.op("act", lambda E: E.activation(out=ssc[:np_, 2:3], in_=ssc[:np_, 1:2], func=AF.Exp, scale=-0.5), r=[B_ss], w=[B_ss])
        P.op("act", lambda E: E.activation(out=ub[:np_, :], in_=xt[:np_, :], func=AF.Copy, scale=ssc[:np_, 2:3]),
             r=[xtb, B_ss], w=[ubb])
        for k0 in range(0, KC, 8):
            nk = min(8, KC - k0)
            pi = next_ps()
            pv = PS[pi][:].bitcast(BF16)
            for j in range(nk):
                P.op("pe", lambda E, j=j, k0=k0, pv=pv: E.transpose(out=pv[:, j * 128:j * 128 + np_],
                                                                   in_=ub[:np_, (k0 + j) * 128:(k0 + j + 1) * 128],
                                                                   identity=ident[:np_, :np_]),
                     r=[ubb, B_const], w=[PB[pi]])
            src = pv[:, 0:nk * 128].rearrange("p (k t) -> p k t", t=128)[:, :, 0:np_]
            gsl = gT[:, gi, k0:k0 + nk]
            P.op("dve", lambda E, src=src, gsl=gsl, k0=k0, nk=nk: E.tensor_tensor(
                out=uT[:, k0:k0 + nk, tcol:tcol + np_], in0=src,
                in1=gsl.unsqueeze(2).to_broadcast([128, nk, np_]), op=ALU.mult),
                r=[PB[pi], B_const], w=[uTb])

    B_ss = Buf("ss")

    def proj(actT, actb, ntok, wdram, wbuf, K, c0, ncols, mode, evac, wt, wtb, WC):
        nkc = K // 128
        kgs = [(k, min(nkc, k + 32)) for k in range(0, nkc, 32)]
        wr = [0]
        for cb0 in range(0, ncols, WC):
            wc = min(WC, ncols - cb0)
            tiles = []

            def load(kg):
                i = wr[0] % len(wt); wr[0] += 1
                ka, kb = kgs[kg]
                for k8 in range(ka, kb, 8):
                    k9 = min(kb, k8 + 8)
                    dma(wt[i][:, k8 - ka:k9 - ka, 0:wc],
                        wdram.ap()[k8 * 128:k9 * 128, c0 + cb0:c0 + cb0 + wc].rearrange("(k p) c -> p k c", p=128),
                        [wbuf], [wtb[i]], "ld_" + wtb[i].name)
                return i
            if mode == "tm":
                ntt = (ntok + 127) // 128
                for th0 in range(0, ntt, 4):
                    tts = list(range(th0, min(ntt, th0 + 4)))
                    banks = {tt: next_ps() for tt in tts}
                    for kg, (ka, kb) in enumerate(kgs):
                        if th0 == 0 or len(kgs) > 1:
                            wi = load(kg)
                            if len(kgs) == 1:
                                tiles = [wi]
                        else:
                            wi = tiles[0]
                        for tt in tts:
                            n = min(128, ntok - tt * 128)
                            for kc in range(ka, kb):
                                P.op("pe", lambda E, tt=tt, n=n, kc=kc, ka=ka, wi=wi, b=banks[tt]: E.matmul(
                                    PS[b][:n, 0:wc], lhsT=actT[:, kc, tt * 128:tt * 128 + n], rhs=wt[wi][:, kc - ka, 0:wc],
                                    start=(kc == 0), stop=(kc == nkc - 1)), r=[actb, wtb[wi]], w=[PB[banks[tt]]])
                    for tt in tts:
                        n = min(128, ntok - tt * 128)
                        evac(banks[tt], PS[banks[tt]][:n, 0:wc], tt * 128, n, cb0, wc)
            else:
                wi = load(0)
                for cc in range(0, wc, 128):
                    for tg in range(0, ntok, 512):
                        n = min(512, ntok - tg)
                        b = next_ps()
                        for kc in range(nkc):
                            P.op("pe", lambda E, kc=kc, cc=cc, tg=tg, n=n, b=b, wi=wi: E.matmul(
                                PS[b][:, 0:n], lhsT=wt[wi][:, kc, cc:cc + 128], rhs=actT[:, kc, tg:tg + n],
                                start=(kc == 0), stop=(kc == nkc - 1)), r=[actb, wtb[wi]], w=[PB[b]])
                        evac(b, PS[b][:, 0:n], tg, n, cb0 + cc, 128)

    ar = Arena()
    uT = ar.a16(KC * TB).rearrange("p (k t) -> p k t", t=TB); uTb = Buf("uT")
    WC1 = 512
    wt = [ar.a16(32 * WC1).rearrange("p (k c) -> p k c", c=WC1) for _ in range(2)]
    wtb = [Buf("wt0"), Buf("wt1")]
    ub = ar.a16(D); ubb = Buf("ub")
    ost = [ar.a16(512) for _ in range(4)]; ostb = [Buf("ost%d" % i) for i in range(4)]
    xt = ar.a32(D); xtb = Buf("xt")
    osr = [0]
    B_scr = {n: Buf("scr_" + n) for n in ("rKtm", "rVtm", "dVtm", "dKT", "rKT", "rQT", "rGT", "dQT", "gaT", "gbT", "ZrT", "ZdT", "h1", "y")}

    def mk_evac(dst, dstb, tm, row0, scale=None, func=None):
        def evac(b, pap, t0, n, c0, ncol):
            i = osr[0] % 4; osr[0] += 1
            if tm:
                o = ost[i][:n, 0:ncol]
            else:
                o = ost[i][:, 0:n]
            if func is not None:
                P.op("act", lambda E: E.activation(out=o, in_=pap, func=func), r=[PB[b]], w=[ostb[i]])
            else:
                copy_op(ev_eng(), o, pap, [PB[b]], [ostb[i]], scale)
            if tm:
                dma(dst.ap()[row0 + t0:row0 + t0 + n, c0:c0 + ncol], o, [ostb[i]], [dstb], "st_" + dstb.name, eng="act")
            else:
                dma(dst.ap()[c0:c0 + ncol, row0 + t0:row0 + t0 + n], o, [ostb[i]], [dstb], "st_" + dstb.name, eng="act")
        return evac

    blocks = [(r0, TB) for r0 in range(0, S, TB)] + [(S, N_META)]

    def stage1_block(r0, ntok):
        own = r0 < T
        for t0 in range(0, ntok, 128):
            np_ = min(128, ntok - t0)
            src = (x_rot.ap()[r0 + t0:r0 + t0 + np_, :] if r0 < S else meta.ap()[0:np_, :])
            norm_T(ar, src, np_, 0, uT, uTb, t0, xt, xtb, ub, ubb, sm)
        def P_(g, dst, tm, scale=None, func=None, row0=r0):
            pcnt[0] += 1
            if pcnt[0] % 2 == 0 and len(pending_casts) > 6:
                next_cast()
            if own:
                next_cast()
            proj(uT, uTb, ntok, wb_in[g], WB[g], D, 0, D, "tm" if tm else "fm",
                 mk_evac(dst, B_scr[dst.name], tm, row0, scale, func), wt, wtb, WC1)
        P_("rk", rKtm, True, scale=256.0 ** -0.5)
        P_("rv", rVtm, True)
        P_("dv", dVtm, True)
        P_("dk", dKT, False)
        if own:
            P_("rk", rKT, False, scale=256.0 ** -0.5)
            P_("rq", rQT, False)
            P_("dq", dQT, False, scale=128.0 ** -0.5)
            P_("rg", rGT, False, func=AF.Silu)
            P_("ga", gaT, False, func=AF.Sigmoid)
            P_("gb", gbT, False, func=AF.Sigmoid)
    blocks = [b_ for b_ in blocks if b_[0] >= T] + [b_ for b_ in blocks if b_[0] < T]
    for (r0_, ntok_) in blocks:
        stage1_block(r0_, ntok_)
    while pending_casts:
        next_cast()
    P.barrier()

    ar = Arena()
    rK = ar.a16(NKT * 256).rearrange("p (t c) -> p t c", c=256); rKb = Buf("rK")
    rV = ar.a16(NKT * 256).rearrange("p (t c) -> p t c", c=256); rVb = Buf("rV")
    kT = ar.a16(2 * T).rearrange("p (k t) -> p k t", t=T); kTb = Buf("kT")
    qT = ar.a16(2 * T).rearrange("p (k t) -> p k t", t=T); qTb = Buf("qT")
    gTt = ar.a16(2 * T).rearrange("p (k t) -> p k t", t=T); gTb = Buf("gTt")
    qf = ar.a16(2 * T).rearrange("p (k t) -> p k t", t=T); qfb = Buf("qf")
    qb_ = ar.a16(2 * T).rearrange("p (k t) -> p k t", t=T); qbb = Buf("qb")
    Sbs = ar.a16(NT * 512).rearrange("p (c k v) -> p c k v", k=2, v=256); Sbsb = Buf("Sbs")
    Sf16 = ar.a16(512).rearrange("p (k v) -> p k v", v=256); Sf16b = Buf("Sf16")
    kw = [ar.a16(256) for _ in range(2)]; kwb = [Buf("kw0"), Buf("kw1")]
    aT = [ar.a16(128) for _ in range(2)]; aTb = [Buf("aT0"), Buf("aT1")]
    zo = [ar.a16(512) for _ in range(2)]; zob = [Buf("zo0"), Buf("zo1")]
    Mh = ar.a32(128); Mhb = Buf("Mh")
    Mt = ar.a32(128)
    QD = ar.a32(256).rearrange("p (a t) -> p a t", t=128); QDb = Buf("QD")
    wfb_ = ar.a32(2 * NKT).rearrange("p (a t) -> p a t", t=NKT); wfbb = Buf("wfb")
    cdist = ar.a32(2 * NKT).rearrange("p (a t) -> p a t", t=NKT)
    kdc = ar.a32(4); kdcb = Buf("kdc")
    Sm = ar.a32(1024).rearrange("p (d k v) -> p d k v", k=2, v=256); Smb = [Buf("Smf"), Buf("Smb")]
    osq = ar.a32(1024).rearrange("p (k t) -> p k t", t=512); osqb = Buf("osq")
    orr = ar.a32(512); orrb = Buf("orr")
    ld_raw = ar.a32(2 * H).rearrange("p (a h) -> p a h", h=H); ldb = Buf("ld")
    dma(cdist[:], c_dist.ap().rearrange("a p t -> p a t"), [B_x], [B_const], "c0")
    dma(ld_raw[:].rearrange("p a h -> p (a h)"), rdecay.ap().rearrange("a h -> (a h)").partition_broadcast(128), [B_x], [ldb], "c0")
    P.op("act", lambda E: E.activation(out=ld_raw[:], in_=ld_raw[:], func=AF.Exp), r=[ldb], w=[ldb])
    P.op("dve", lambda E: E.tensor_scalar(out=ld_raw[:], in0=ld_raw[:], scalar1=-1.0, scalar2=None, op0=ALU.mult),
         r=[ldb], w=[ldb])

    def tile_np(kt):
        return N_META if kt == NKT - 1 else 128

    def flat(a):
        return a.rearrange("p k v -> p (k v)")

    def ret_upd_state(d, c):
        i = (c + d) % 2
        P.op("dve", lambda E: E.tensor_scalar(out=kw[i][:, :], in0=rK[:, c, :], scalar1=kdc[:, d:d + 1], scalar2=None,
                                               op0=ALU.mult), r=[rKb, kdcb], w=[kwb[i]])
        b = next_ps(4, 8)
        for dk in range(2):
            P.op("pe", lambda E, dk=dk: E.matmul(PS[b][:, dk * 256:(dk + 1) * 256], lhsT=kw[i][:, dk * 128:(dk + 1) * 128],
                                                 rhs=rV[:, c, :], start=(dk == 0), stop=(dk == 1)),
                 r=[kwb[i], rVb], w=[PB[b]])
        P.op("dve", lambda E: E.scalar_tensor_tensor(
            out=flat(Sm[:, d, :, :]), in0=flat(Sm[:, d, :, :]),
            scalar=kdc[:, 2 + d:3 + d], in1=PS[b][:, :], op0=ALU.mult, op1=ALU.add),
            r=[Smb[d], PB[b], kdcb], w=[Smb[d]])

    def ret_chunk(h, c, c4, po):
        cs = slice(c * 128, (c + 1) * 128)
        P.op("act", lambda E: E.activation(out=flat(Sf16[:]), in_=flat(Sm[:, 0, :, :]), func=AF.Copy),
             r=[Smb[0]], w=[Sf16b])
        bs = next_ps(4, 8)
        for dk in range(2):
            P.op("pe", lambda E, dk=dk: E.matmul(PS[bs][:, 0:128], lhsT=kT[:, dk, cs], rhs=qT[:, dk, cs],
                                                 start=(dk == 0), stop=(dk == 1)), r=[kTb, qTb], w=[PB[bs]])
        ia = c % 2
        P.op("dve", lambda E: E.tensor_tensor(out=aT[ia][:], in0=PS[bs][:, 0:128], in1=Mh[:], op=ALU.mult),
             r=[PB[bs], Mhb], w=[aTb[ia]])
        for vh in range(2):
            vs = slice(vh * 128, (vh + 1) * 128)
            oc = slice((c - c4) * 128, (c - c4 + 1) * 128)
            seq = [(rV[:, c, vs], aT[ia][:], [rVb, aTb[ia]])]
            for dk in range(2):
                seq.append((Sf16[:, dk, vs], qf[:, dk, cs], [Sf16b, qfb]))
                seq.append((Sbs[:, c, dk, vs], qb_[:, dk, cs], [Sbsb, qbb]))
            ns = len(seq)
            for j, (l, r_, rb_) in enumerate(seq):
                P.op("pe", lambda E, l=l, r_=r_, j=j, vh=vh, oc=oc: E.matmul(
                    PS[po[vh]][:, oc], lhsT=l, rhs=r_, start=(j == 0), stop=(j == ns - 1)),
                    r=rb_, w=[PB[po[vh]]])
        if c < NT - 1:
            ret_upd_state(0, c)

    def ret_group(h, c4):
        po = [2, 3]
        c5 = min(NT, c4 + 4)
        for c in range(c4, c5):
            ret_chunk(h, c, c4, po)
        n = (c5 - c4) * 128
        ts_ = slice(c4 * 128, c4 * 128 + n)
        for vh in range(2):
            P.op("act", lambda E, vh=vh: E.activation(out=osq[:, vh, 0:n], in_=PS[po[vh]][:, 0:n], func=AF.Square),
                 r=[PB[po[vh]]], w=[osqb])
        br = next_ps(4, 8)
        for vh in range(2):
            P.op("pe", lambda E, vh=vh: E.matmul(PS[br][:, 0:n], lhsT=ones32[:], rhs=osq[:, vh, 0:n],
                                                 start=(vh == 0), stop=(vh == 1)), r=[osqb, B_const], w=[PB[br]])
        P.op("act", lambda E: E.activation(out=orr[:, 0:n], in_=PS[br][:, 0:n], func=AF.Ln, scale=1.0 / 256, bias=epsc[:, 0:1]),
             r=[PB[br], B_const], w=[orrb])
        P.op("act", lambda E: E.activation(out=orr[:, 0:n], in_=orr[:, 0:n], func=AF.Exp, scale=-0.5), r=[orrb], w=[orrb])
        for vh in range(2):
            P.op("dve", lambda E, vh=vh: E.tensor_tensor(out=osq[:, vh, 0:n], in0=PS[po[vh]][:, 0:n], in1=orr[:, 0:n],
                                                         op=ALU.mult), r=[PB[po[vh]], orrb], w=[osqb])
            P.op("dve", lambda E, vh=vh: E.tensor_tensor(out=zo[vh][:, 0:n], in0=osq[:, vh, 0:n], in1=gTt[:, vh, ts_],
                                                         op=ALU.mult), r=[osqb, gTb], w=[zob[vh]])
            dma(ZrT.ap()[h * 256 + vh * 128:h * 256 + (vh + 1) * 128, ts_], zo[vh][:, 0:n], [zob[vh]], [B_scr["ZrT"]], "st_ZrT", eng="act")

    def ret_head(h):
        hc = slice(h * 256, (h + 1) * 256)
        for t8 in range(0, NTALL, 16):
            t9 = min(NTALL, t8 + 16)
            dma(rK[:, t8:t9, :], rKtm.ap()[t8 * 128:t9 * 128, hc].rearrange("(t p) c -> p t c", p=128), [B_scr["rKtm"]], [rKb], "rK")
        dma(rK[:N_META, NTALL, :], rKtm.ap()[S:S + N_META, hc], [B_scr["rKtm"]], [rKb], "rK")
        for t8 in range(0, NTALL, 16):
            t9 = min(NTALL, t8 + 16)
            dma(rV[:, t8:t9, :], rVtm.ap()[t8 * 128:t9 * 128, hc].rearrange("(t p) c -> p t c", p=128), [B_scr["rVtm"]], [rVb], "rV")
        dma(rV[:N_META, NTALL, :], rVtm.ap()[S:S + N_META, hc], [B_scr["rVtm"]], [rVb], "rV")
        for (dst, dstb, srcT, nm) in ((kT, kTb, rKT, "rKT"), (qT, qTb, rQT, "rQT"), (gTt, gTb, rGT, "rGT")):
            dma(dst[:], srcT.ap()[hc, :].rearrange("(k p) t -> p k t", p=128), [B_scr[nm]], [dstb], "r" + nm)
        for d in range(2):
            lg = ld_raw[:, d, h:h + 1]
            P.op("act", lambda E, d=d, lg=lg: E.activation(out=wfb_[:, d, :], in_=cdist[:, d, :], func=AF.Exp, scale=lg),
                 r=[B_const, ldb], w=[wfbb])
            P.op("act", lambda E, d=d, lg=lg: E.activation(out=QD[:, d, :], in_=csq[:, 2 + d, :], func=AF.Exp, scale=lg),
                 r=[B_const, ldb], w=[QDb])
            P.op("act", lambda E, d=d, lg=lg: E.activation(out=kdc[:, d:d + 1], in_=ckd[:, d:d + 1], func=AF.Exp, scale=lg),
                 r=[B_const, ldb], w=[kdcb])
            P.op("act", lambda E, d=d, lg=lg: E.activation(out=kdc[:, 2 + d:3 + d], in_=lg, func=AF.Exp, scale=128.0),
                 r=[B_const, ldb], w=[kdcb])
        P.op("act", lambda E: E.activation(out=Mh[:], in_=csq[:, 0, :], func=AF.Exp, scale=ld_raw[:, 0, h:h + 1]),
             r=[B_const, ldb], w=[Mhb])
        P.op("act", lambda E: E.activation(out=Mt[:], in_=csq[:, 1, :], func=AF.Exp, scale=ld_raw[:, 1, h:h + 1]),
             r=[B_const, ldb, Mhb], w=[Mhb])
        P.op("dve", lambda E: E.tensor_tensor(out=Mh[:], in0=Mh[:], in1=Mt[:], op=ALU.add), r=[Mhb], w=[Mhb])
        for d, (dst, dstb) in enumerate(((qf, qfb), (qb_, qbb))):
            P.op("dve", lambda E, d=d, dst=dst: E.tensor_tensor(
                out=dst[:].rearrange("p k (c i) -> p (k c) i", i=128),
                in0=qT[:].rearrange("p k (c i) -> p (k c) i", i=128),
                in1=QD[:, d, :].unsqueeze(1).to_broadcast([128, 2 * NT, 128]), op=ALU.mult),
                r=[qTb, QDb], w=[dstb])
        pin = [0, 1]
        for d in range(2):
            for kt in range(NT, NKT):
                np_ = tile_np(kt)
                i = (kt + d) % 2
                P.op("dve", lambda E, i=i, kt=kt, d=d, np_=np_: E.tensor_scalar(
                    out=kw[i][:np_, :], in0=rK[:np_, kt, :], scalar1=wfb_[:np_, d, kt:kt + 1], scalar2=None, op0=ALU.mult),
                    r=[rKb, wfbb], w=[kwb[i]])
                for dk in range(2):
                    P.op("pe", lambda E, i=i, kt=kt, d=d, dk=dk, np_=np_: E.matmul(
                        PS[pin[d]][:, dk * 256:(dk + 1) * 256], lhsT=kw[i][:np_, dk * 128:(dk + 1) * 128],
                        rhs=rV[:np_, kt, :], start=(kt == NT and dk == 0), stop=(kt == NKT - 1 and dk == 1),
                        skip_group_check=True),
                        r=[kwb[i], rVb], w=[PB[pin[d]]])
            P.op("dve", lambda E, d=d: E.tensor_copy(out=flat(Sm[:, d, :, :]), in_=PS[pin[d]][:, :]),
                 r=[PB[pin[d]]], w=[Smb[d]])
        for c in range(NT - 1, -1, -1):
            P.op("act", lambda E, c=c: E.activation(out=flat(Sbs[:, c, :, :]), in_=flat(Sm[:, 1, :, :]), func=AF.Copy),
                 r=[Smb[1]], w=[Sbsb])
            if c > 0:
                ret_upd_state(1, c)
        for c4 in range(0, NT, 4):
            ret_group(h, c4)

    for h in range(H):
        ret_head(h)
    P.barrier()

    ar = Arena()
    k1 = ar.a16(NKT * 128); k2 = ar.a16(NKT * 128); kkb = Buf("kk")
    vv = ar.a16(NKT * 256).rearrange("p (t c) -> p t c", c=256); vvb = Buf("vv")
    sg = [ar.a16(NKT * 128) for _ in range(2)]; sgb = [Buf("sg0"), Buf("sg1")]
    qq = [ar.a16(1024).rearrange("p (m t) -> p m t", t=512) for _ in range(2)]; qqb = [Buf("qq0"), Buf("qq1")]
    rbt = ar.a16(512); rbb = Buf("rb")
    ee = [ar.a16(512) for _ in range(4)]; eeb = [Buf("ee%d" % i) for i in range(4)]
    zd = [ar.a16(512) for _ in range(2)]; zdb = [Buf("zd0"), Buf("zd1")]
    babs = ar.a32(4 * 512).rearrange("p (a t) -> p a t", t=512)
    absd = ar.a32(NQB * NKT).rearrange("p (a t) -> p a t", t=NKT)
    bcol = ar.a32(NKT); bcolb = Buf("bcol")
    zacc = [ar.a32(512) for _ in range(2)]; zaccb = [Buf("zacc0"), Buf("zacc1")]
    sp_ = [ar.a32(512) for _ in range(2)]; spb = [Buf("sp0"), Buf("sp1")]
    o32 = ar.a32(1024).rearrange("p (k t) -> p k t", t=512); o32b = Buf("o32")
    t32 = ar.a32(1024).rearrange("p (k t) -> p k t", t=512); t32b = Buf("t32")
    rz = ar.a32(1024).rearrange("p (k t) -> p k t", t=512); rzb = Buf("rz")
    lam = ar.a32(8); lamb = Buf("lam")
    sgc = ar.a32(2); sgcb = Buf("sgc")
    lraw = ar.a32(512)
    dma(babs[:], c_babs.ap().rearrange("a p t -> p a t"), [B_x], [B_const], "c0")
    dma(absd[:], c_absd.ap().rearrange("a p t -> p a t"), [B_x], [B_const], "c0")
    dma(sgc[:], subln.ap().rearrange("(k p) -> p k", p=128), [B_x], [sgcb], "c0")
    dma(lraw[:], dlam.ap().rearrange("a f -> (a f)").partition_broadcast(128), [B_x], [lamb], "c0")
    P.op("dve", lambda E: E.tensor_tensor(out=lraw[:, 0:128], in0=lraw[:, 0:128], in1=lraw[:, 128:256], op=ALU.mult), r=[lamb], w=[lamb])
    P.op("dve", lambda E: E.tensor_tensor(out=lraw[:, 256:384], in0=lraw[:, 256:384], in1=lraw[:, 384:512], op=ALU.mult), r=[lamb], w=[lamb])
    P.op("dve", lambda E: E.reduce_sum(out=lam[:, 0:1], in_=lraw[:, 0:128], axis=mybir.AxisListType.X), r=[lamb], w=[lamb])
    P.op("dve", lambda E: E.reduce_sum(out=lam[:, 1:2], in_=lraw[:, 256:384], axis=mybir.AxisListType.X), r=[lamb], w=[lamb])
    P.op("act", lambda E: E.activation(out=lam[:, 2:4], in_=lam[:, 0:2], func=AF.Exp), r=[lamb], w=[lamb])
    P.op("dve", lambda E: E.tensor_tensor(out=lam[:, 4:5], in0=lam[:, 3:4], in1=lam[:, 2:3], op=ALU.subtract), r=[lamb], w=[lamb])
    P.op("dve", lambda E: E.tensor_scalar(out=lam[:, 4:5], in0=lam[:, 4:5], scalar1=-cfg.LAM_INIT, scalar2=None, op0=ALU.add), r=[lamb], w=[lamb])
    P.op("dve", lambda E: E.tensor_scalar(out=sgc[:], in0=sgc[:], scalar1=1.0 - cfg.LAM_INIT, scalar2=None, op0=ALU.mult),
         r=[sgcb], w=[sgcb])
    SL = slopes(cfg)

    def diff_S(h, qb, qi, kt):
        np_ = tile_np(kt)
        diag = (qb * 4 <= kt < qb * 4 + 4)
        for m, kk in enumerate((k1, k2)):
            bS = next_ps(4, 8)
            ie = (2 * kt + m) % 4
            P.op("pe", lambda E, kk=kk, bS=bS, m=m: E.matmul(
                PS[bS][:np_, :], lhsT=kk[:, kt * 128:kt * 128 + np_], rhs=qq[qi][:, m, :], start=True, stop=diag),
                r=[kkb, qqb[qi]], w=[PB[bS]])
            if not diag:
                P.op("pe", lambda E, bS=bS: E.matmul(
                    PS[bS][:np_, :], lhsT=sg[qi][0:2, kt * 128:kt * 128 + np_], rhs=rbt[0:2, :], start=False, stop=True),
                    r=[sgb[qi], rbb], w=[PB[bS]])
                P.op("act", lambda E, bS=bS, ie=ie: E.activation(
                    out=ee[ie][:np_, :], in_=PS[bS][:np_, :], func=AF.Exp, bias=bcol[:np_, kt:kt + 1]),
                    r=[PB[bS], bcolb], w=[eeb[ie]])
            else:
                ci = kt - qb * 4
                P.op("dve", lambda E, bS=bS, m=m: E.scalar_tensor_tensor(
                    out=sp_[m][:], in0=babs[:, ci, :], scalar=-SL[h], in1=PS[bS][:, :], op0=ALU.mult, op1=ALU.add),
                    r=[PB[bS], B_const], w=[spb[m]])
                P.op("act", lambda E, m=m, ie=ie: E.activation(out=ee[ie][:], in_=sp_[m][:], func=AF.Exp),
                     r=[spb[m]], w=[eeb[ie]])

    def diff_AV(h, qb, qi, kt, pO):
        np_ = tile_np(kt)
        for m in range(2):
            ie = (2 * kt + m) % 4
            if kt == 0:
                P.op("dve", lambda E, m=m, ie=ie: E.tensor_copy(out=zacc[m][:], in_=ee[ie][:]), r=[eeb[ie]], w=[zaccb[m]])
            else:
                P.op("dve", lambda E, m=m, ie=ie: E.tensor_tensor(out=zacc[m][:np_, :], in0=zacc[m][:np_, :],
                                                                 in1=ee[ie][:np_, :], op=ALU.add),
                     r=[eeb[ie], zaccb[m]], w=[zaccb[m]])
            for vh in range(2):
                P.op("pe", lambda E, vh=vh, m=m, ie=ie: E.matmul(
                    PS[pO[m][vh]][:, :], lhsT=vv[:np_, kt, vh * 128:(vh + 1) * 128], rhs=ee[ie][:np_, :],
                    start=(kt == 0), stop=(kt == NKT - 1)), r=[vvb, eeb[ie]], w=[PB[pO[m][vh]]])

    def diff_iter(h, qb, qi):
        q0 = qb * 512
        dma(qq[qi][:], dQT.ap()[h * 256:(h + 1) * 256, q0:q0 + 512].rearrange("(m p) t -> p m t", p=128),
            [B_scr["dQT"]], [qqb[qi]], "qq%d" % qi)
        dma(sg[qi][0:2, :], c_sgn.ap()[qb], [B_x], [sgb[qi]], "sg%d" % qi)
        P.op("dve", lambda E: E.tensor_scalar(out=bcol[:], in0=absd[:, qb, :], scalar1=-SL[h], scalar2=None,
                                               op0=ALU.mult), r=[B_const], w=[bcolb])
        pO = [[0, 1], [2, 3]]
        for kt in range(NKT + 1):
            if kt < NKT:
                diff_S(h, qb, qi, kt)
            if kt >= 1:
                diff_AV(h, qb, qi, kt - 1, pO)
        for m in range(2):
            bz = next_ps(4, 8)
            P.op("pe", lambda E, m=m, bz=bz: E.matmul(PS[bz][:, :], lhsT=ones32[:], rhs=zacc[m][:], start=True, stop=True),
                 r=[zaccb[m], B_const], w=[PB[bz]])
            P.op("dve", lambda E, m=m, bz=bz: E.reciprocal(out=rz[:, m, :], in_=PS[bz][:, :]), r=[PB[bz]], w=[rzb])
        for vh in range(2):
            P.op("dve", lambda E, vh=vh: E.tensor_tensor(out=t32[:, vh, :], in0=PS[pO[1][vh]][:, :], in1=rz[:, 1, :], op=ALU.mult),
                 r=[PB[pO[1][vh]], rzb], w=[t32b])
            P.op("dve", lambda E, vh=vh: E.tensor_tensor(out=o32[:, vh, :], in0=PS[pO[0][vh]][:, :], in1=rz[:, 0, :], op=ALU.mult),
                 r=[PB[pO[0][vh]], rzb], w=[o32b])
            P.op("dve", lambda E, vh=vh: E.scalar_tensor_tensor(out=o32[:, vh, :], in0=t32[:, vh, :], scalar=lam[:, 4:5],
                                                                in1=o32[:, vh, :], op0=ALU.mult, op1=ALU.add),
                 r=[t32b, o32b, lamb], w=[o32b])
            P.op("act", lambda E, vh=vh: E.activation(out=t32[:, vh, :], in_=o32[:, vh, :], func=AF.Square), r=[o32b, t32b], w=[t32b])
        br = next_ps(4, 8)
        for vh in range(2):
            P.op("pe", lambda E, vh=vh: E.matmul(PS[br][:, :], lhsT=ones32[:], rhs=t32[:, vh, :], start=(vh == 0), stop=(vh == 1)),
                 r=[t32b, B_const], w=[PB[br]])
        P.op("act", lambda E: E.activation(out=rz[:, 0, :], in_=PS[br][:, :], func=AF.Ln, scale=1.0 / 256, bias=epsc[:, 0:1]),
             r=[PB[br], rzb, B_const], w=[rzb])
        P.op("act", lambda E: E.activation(out=rz[:, 0, :], in_=rz[:, 0, :], func=AF.Exp, scale=-0.5), r=[rzb], w=[rzb])
        for vh in range(2):
            P.op("dve", lambda E, vh=vh: E.scalar_tensor_tensor(out=zd[vh][:], in0=o32[:, vh, :], scalar=sgc[:, vh:vh + 1],
                                                                in1=rz[:, 0, :], op0=ALU.mult, op1=ALU.mult),
                 r=[o32b, rzb, sgcb], w=[zdb[vh]])
            dma(ZdT.ap()[h * 256 + vh * 128:h * 256 + (vh + 1) * 128, q0:q0 + 512], zd[vh][:], [zdb[vh]], [B_scr["ZdT"]], "st_ZdT", eng="act")

    def diff_head(h, it0):
        dma(k1[:, 0:S + N_META], dKT.ap()[h * 256:h * 256 + 128, 0:S + N_META], [B_scr["dKT"]], [kkb], "kk")
        dma(k2[:, 0:S + N_META], dKT.ap()[h * 256 + 128:h * 256 + 256, 0:S + N_META], [B_scr["dKT"]], [kkb], "kk")
        for t8 in range(0, NTALL, 16):
            t9 = min(NTALL, t8 + 16)
            dma(vv[:, t8:t9, :], dVtm.ap()[t8 * 128:t9 * 128, h * 256:(h + 1) * 256].rearrange("(t p) c -> p t c", p=128), [B_scr["dVtm"]], [vvb], "vv")
        dma(vv[:N_META, NTALL, :], dVtm.ap()[S:S + N_META, h * 256:(h + 1) * 256], [B_scr["dVtm"]], [vvb], "vv")
        dma(rbt[0:2, :], c_rb.ap()[h], [B_x], [rbb], "rb")
        for qb in range(NQB):
            diff_iter(h, qb, (it0 + qb) % 2)

    for h in range(H):
        diff_head(h, h * NQB)
    P.barrier()

    ar = Arena()
    TF = 512
    WC4 = 512
    zT = ar.a16(KC * TF).rearrange("p (k t) -> p k t", t=TF); zTb = Buf("zT")
    mxbase = ar.a16(max(KC, 32) * TF).rearrange("p (k t) -> p k t", t=TF); mxTb = Buf("mxT")
    mxT = mxbase[:, 0:KC, :]
    aTf = mxbase; aTfb = mxTb
    wt4 = [ar.a16(32 * WC4).rearrange("p (k c) -> p k c", c=WC4) for _ in range(2)]
    wt4b = [Buf("w40"), Buf("w41")]
    gg = [ar.a16(TF) for _ in range(4)]; ggb = [Buf("gg%d" % i) for i in range(4)]
    ub4 = ar.a16(D); ub4b = Buf("ub4")
    gsb = ar.a16(TF); gsbb = Buf("gsb")
    xt4 = ar.a32(D); xt4b = Buf("xt4")
    gfin = ar.a32(D)
    hst = [ar.a32(WC4) for _ in range(3)]; hstb = [Buf("hst%d" % i) for i in range(3)]
    hrow = [ar.a32(WC4) for _ in range(3)]; hrowb = [Buf("hrow%d" % i) for i in range(3)]
    B_h1 = [Buf("h1_%d" % i) for i in range(4)]
    rr = [0]
    dma(gfin[:], gvec.ap()[2].partition_broadcast(128), [B_x], [B_const], "c0")

    def s4_mix(tb, pas, Zs, nm, wdr, wn, gsrc, gn):
        for k8 in range(0, KC, 8):
            k9 = min(KC, k8 + 8)
            dma(zT[:, k8:k9, :], Zs.ap()[k8 * 128:k9 * 128, tb:tb + TF].rearrange("(k p) t -> p k t", p=128), [B_scr[nm]], [zTb], "zT")

        def ev_mix(b, pap, t0, n, c0, ncol):
            i = rr[0] % 4; rr[0] += 1
            kc = c0 // 128
            dma(gg[i][:, 0:n], gsrc.ap()[c0:c0 + 128, tb + t0:tb + t0 + n], [B_scr[gn]], [ggb[i]], "gg%d" % i)
            if pas == 0:
                P.op("dve", lambda E: E.tensor_tensor(out=mxT[:, kc, t0:t0 + n], in0=pap, in1=gg[i][:, 0:n], op=ALU.mult),
                     r=[PB[b], ggb[i]], w=[mxTb])
            else:
                P.op("dve", lambda E: E.tensor_tensor(out=gsb[:, 0:n], in0=pap, in1=gg[i][:, 0:n], op=ALU.mult),
                     r=[PB[b], ggb[i]], w=[gsbb])
                P.op("dve", lambda E: E.tensor_tensor(out=mxT[:, kc, t0:t0 + n], in0=mxT[:, kc, t0:t0 + n], in1=gsb[:, 0:n], op=ALU.add),
                     r=[gsbb, mxTb], w=[mxTb])
        proj(zT, zTb, TF, wdr, WB[wn], D, 0, D, "fm", ev_mix, wt4, wt4b, WC4)

    def s4_seg(tb, f0, f1):
        def ev_gate(b, pap, t0, n, c0, ncol):
            kc = (c0 // 128)
            P.op("act", lambda E: E.activation(out=aTf[:, kc, t0:t0 + n], in_=pap, func=AF.Silu), r=[PB[b]], w=[aTfb])

        def ev_up(b, pap, t0, n, c0, ncol):
            kc = (c0 // 128)
            P.op("dve", lambda E: E.tensor_tensor(out=aTf[:, kc, t0:t0 + n], in0=pap, in1=aTf[:, kc, t0:t0 + n], op=ALU.mult),
                 r=[PB[b], aTfb], w=[aTfb])

        class _W:
            def __init__(s, a):
                s.a = a

            def ap(s):
                return s.a
        proj(zT, zTb, TF, _W(wb_g.ap()[:, f0 * 128:f1 * 128]), WB["g"], D, 0, (f1 - f0) * 128, "fm", ev_gate, wt4, wt4b, WC4)
        proj(zT, zTb, TF, _W(wb_u.ap()[:, f0 * 128:f1 * 128]), WB["u"], D, 0, (f1 - f0) * 128, "fm", ev_up, wt4, wt4b, WC4)

        def ev_dn(b, pap, t0, n, c0, ncol):
            i = rr[0] % 3; rr[0] += 1
            hb = B_h1[t0 // 128]
            dma(hrow[i][:n, 0:ncol], h1.ap()[tb + t0:tb + t0 + n, c0:c0 + ncol], [hb], [hrowb[i]], "hrow%d" % i)
            P.op("dve", lambda E: E.tensor_tensor(out=hst[i][:n, 0:ncol], in0=pap, in1=hrow[i][:n, 0:ncol], op=ALU.add),
                 r=[PB[b], hrowb[i]], w=[hstb[i]])
            dma(h1.ap()[tb + t0:tb + t0 + n, c0:c0 + ncol], hst[i][:n, 0:ncol], [hstb[i]], [hb], "st_h1_%d" % (t0 // 128), eng="act")
        proj(aTf, aTfb, TF, _W(wb_d.ap()[f0 * 128:f1 * 128, :]), WB["d"], (f1 - f0) * 128, 0, D, "tm", ev_dn, wt4, wt4b, WC4)

    def s4_block(tb):
        s4_mix(tb, 0, ZrT, "ZrT", wb_br, "br", gaT, "gaT")
        s4_mix(tb, 1, ZdT, "ZdT", wb_bd, "bd", gbT, "gbT")

        def ev_h1(b, pap, t0, n, c0, ncol):
            i = rr[0] % 3; rr[0] += 1
            hb = B_h1[t0 // 128]
            dma(hrow[i][:n, 0:ncol], x_rot.ap()[tb + t0:tb + t0 + n, c0:c0 + ncol], [B_x], [hrowb[i]], "hrow%d" % i)
            P.op("dve", lambda E: E.tensor_tensor(out=hst[i][:n, 0:ncol], in0=pap, in1=hrow[i][:n, 0:ncol], op=ALU.add),
                 r=[PB[b], hrowb[i]], w=[hstb[i]])
            dma(h1.ap()[tb + t0:tb + t0 + n, c0:c0 + ncol], hst[i][:n, 0:ncol], [hstb[i]], [hb], "st_h1_%d" % (t0 // 128), eng="act")
        proj(mxT, mxTb, TF, wb_out, WB["out"], D, 0, D, "tm", ev_h1, wt4, wt4b, WC4)
        for t0 in range(0, TF, 128):
            norm_T(ar, h1.ap()[tb + t0:tb + t0 + 128, :], 128, 1, zT, zTb, t0, xt4, xt4b, ub4, ub4b, sm, src_buf=B_h1[t0 // 128])
        for f0 in range(0, FC, 32):
            s4_seg(tb, f0, min(FC, f0 + 32))
        for t0 in range(0, TF, 128):
            s4_fin(tb, t0)

    def s4_fin(tb, t0):
        dma(xt4[:, :], h1.ap()[tb + t0:tb + t0 + 128, :], [B_h1[t0 // 128]], [xt4b], "xt")
        P.op("dve", lambda E: E.memset(sm[:, 0:1], 0.0), w=[B_ss])
        P.op("act", lambda E: E.activation(out=ub4[:, :], in_=xt4[:, :], func=AF.Square, accum_out=sm[:, 0:1]),
             r=[xt4b], w=[ub4b, B_ss])
        P.op("act", lambda E: E.activation(out=sm[:, 1:2], in_=sm[:, 0:1], func=AF.Ln, scale=1.0 / D, bias=epsc[:, 0:1]),
             r=[B_ss, B_const], w=[B_ss])
        P.op("act", lambda E: E.activation(out=sm[:, 2:3], in_=sm[:, 1:2], func=AF.Exp, scale=-0.5), r=[B_ss], w=[B_ss])
        P.op("dve", lambda E: E.scalar_tensor_tensor(out=xt4[:, :], in0=xt4[:, :], scalar=sm[:, 2:3], in1=gfin[:, :],
                                                     op0=ALU.mult, op1=ALU.mult), r=[xt4b, B_ss, B_const], w=[xt4b])
        dma(y.ap()[tb + t0:tb + t0 + 128, :], xt4[:, :], [xt4b], [B_scr["y"]], "st_y", eng="act")

    for tb in range(0, T, TF):
        s4_block(tb)

    with nc.allow_non_contiguous_dma(reason="small strided constant/layout loads"):
        with nc.Block() as block:
            sems = P.emit(nc, block)
        sems.close()
    stack.close()
    return nc


def host_consts(cfg, qt):
    S, T, H, NKT, NQB, NTALL = cfg.S, cfg.T, cfg.H, cfg.NKT, cfg.NQB, cfg.NTALL
    own0 = qt * T
    perm = np.concatenate([np.arange(own0, own0 + T), np.arange(0, own0), np.arange(own0 + T, S)])
    kpos = np.full((NKT, 128), 1e9, np.float64)
    kpos[:NTALL] = (N_META + perm).reshape(NTALL, 128)
    kpos[NTALL, :N_META] = np.arange(N_META)
    absd = np.zeros((NQB, 128, NKT), np.float32)
    sgn = np.zeros((NQB, 2, NKT * 128), np.float32)
    for qb in range(NQB):
        qc = N_META + own0 + qb * 512 + 255.5
        absd[qb] = np.abs(kpos - qc).T
        sg = np.where(kpos < qc, 1.0, -1.0).reshape(-1)
        sgn[qb, 0] = sg; sgn[qb, 1] = sg
    absd = np.minimum(absd, 1e6).astype(np.float32)
    sl = np.array(slopes(cfg), np.float64)
    qrel = np.arange(512) - 255.5
    val = (-sl[:, None] * qrel[None, :]).astype(np.float32)
    hi = val.astype(ml_dtypes.bfloat16)
    lo = (val - hi.astype(np.float32)).astype(ml_dtypes.bfloat16)
    rb = np.stack([hi, lo], axis=1)
    p = np.arange(128)[:, None]; f = np.arange(512)[None, :]
    babs = np.stack([np.abs(ci * 128 + p - f) for ci in range(4)]).astype(np.float32)
    p0 = N_META + own0; p1 = p0 + T
    distf = np.where(kpos < p0, p0 - 1 - kpos, BIG)
    distb = np.where((kpos >= p1) & (kpos < 1e8), kpos - p1, BIG)
    dist = np.stack([distf.T, distb.T]).astype(np.float32)
    s_ = np.arange(128)[:, None]; t_ = np.arange(128)[None, :]
    dpos = np.where(t_ >= s_, t_ - s_, BIG)
    dneg = np.where(s_ > t_, s_ - t_, BIG)
    qdf = np.broadcast_to(t_ + 1.0, (128, 128))
    qdb = np.broadcast_to(128.0 - t_, (128, 128))
    sq = np.stack([dpos, dneg, qdf, qdb]).astype(np.float32)
    kd = np.stack([127.0 - np.arange(128), np.arange(128) * 1.0], axis=1).astype(np.float32)
    return perm, dict(c_absd=absd, c_sgn=sgn.astype(ml_dtypes.bfloat16), c_rb=rb, c_babs=babs, c_dist=dist,
                      c_sq=sq, c_kd=kd, c_ident=np.eye(128, dtype=np.float32).astype(ml_dtypes.bfloat16))


_NC_CACHE = {}


def run(cfg, inp):
    key = (cfg.D, cfg.S)
    if key not in _NC_CACHE:
        _NC_CACHE[key] = build(cfg)
    nc = _NC_CACHE[key]
    f32 = lambda a: np.ascontiguousarray(np.asarray(a, dtype=np.float32))
    x = f32(inp["x"])
    shared = dict(
        meta=f32(inp["meta_tokens"]),
        gvec=np.stack([f32(inp["norm_mix_g"])[0], f32(inp["norm_ffn_g"])[0], f32(inp["norm_final_g"])]),
        w_in=f32(inp["w_in"])[0], w_br=f32(inp["w_branch_ret"])[0], w_bd=f32(inp["w_branch_diff"])[0],
        w_out=f32(inp["w_out"])[0], w_g=f32(inp["w_ffn_gate"])[0], w_u=f32(inp["w_ffn_up"])[0],
        w_d=f32(inp["w_ffn_down"])[0], rdecay=f32(inp["ret_log_decay"])[0], dlam=f32(inp["diff_lambda"])[0],
        subln=f32(inp["diff_subln_g"])[0])
    in_maps = []
    for c in range(8):
        b, qt = c // 4, c % 4
        perm, consts = host_consts(cfg, qt)
        m = dict(shared)
        m["x_rot"] = np.ascontiguousarray(x[b][perm])
        m.update(consts)
        in_maps.append(m)
    res = run_bass_kernel_spmd(nc, in_maps, core_ids=list(range(8)))
    out = np.zeros((2, cfg.S, cfg.D), np.float32)
    for c in range(8):
        b, qt = c // 4, c % 4
        out[b, qt * cfg.T:(qt + 1) * cfg.T] = res.results[c]["y"]
    return out


def kernel(**inputs):
    return run(Cfg(4096, 8192), inputs)
```

```python
import math, contextlib
import numpy as np
import ml_dtypes
import concourse.bass as bass
import concourse.mybir as mybir
from concourse.bass_utils import run_bass_kernel_spmd

F32 = mybir.dt.float32
BF16 = mybir.dt.bfloat16
AF = mybir.ActivationFunctionType
ALU = mybir.AluOpType
BIG = 1.0e30
EPS = 1e-6
N_META = 16


class Cfg:
    def __init__(s, D=4096, S=8192):
        s.D, s.S = D, S
        s.T = S // 4
        s.H = D // 256
        s.DIN = 9 * D
        s.DFF = ((8 * D + 3 * 256 - 1) // (3 * 256)) * 256
        s.KC = D // 128
        s.FC = s.DFF // 128
        s.NT = s.T // 128
        s.NTALL = S // 128
        s.NKT = s.NTALL + 1
        s.NQB = s.T // 512
        s.TB = min(1024, s.T)
        s.LAM_INIT = 0.8 - 0.6 * math.exp(-0.3 * 0)


class Buf:
    __slots__ = ("name",)

    def __init__(s, name):
        s.name = name


class Prog:
    COMPUTE = ("pe", "act", "dve", "pool")

    def __init__(s):
        s.ops = []
        s.lastw = {}
        s.rd_eng = {}
        s.rd_dma = {}
        s.dmacnt = {}
        s.bar = None
        s.bar_done = set()

    def op(s, eng, fn, r=(), w=(), dma=None):
        i = len(s.ops)
        deps = set()
        for b in tuple(r) + tuple(w):
            lw = s.lastw.get(b)
            if lw is not None:
                deps.add(lw)
        for b in w:
            for x in s.rd_eng.get(b, {}).values():
                deps.add(x)
            for x in s.rd_dma.get(b, ()):
                deps.add(x)
        o = dict(eng=eng, fn=fn, deps=deps, dma=dma, mark=False, bar=None)
        if s.bar is not None and eng not in s.bar_done:
            o["bar"] = s.bar
            s.bar_done.add(eng)
        if dma is not None:
            s.dmacnt[dma] = s.dmacnt.get(dma, 0) + 1
            o["dval"] = 16 * s.dmacnt[dma]
        s.ops.append(o)
        for b in w:
            s.lastw[b] = i
            s.rd_eng[b] = {}
            s.rd_dma[b] = []
        for b in r:
            if dma is not None:
                s.rd_dma.setdefault(b, []).append(i)
            else:
                s.rd_eng.setdefault(b, {})[eng] = i
        return i

    def barrier(s):
        last = {}
        for i, o in enumerate(s.ops):
            if o["dma"] is None:
                last[o["eng"]] = i
        s.bar = (dict(last), dict(s.dmacnt))
        s.bar_done = set()

    def emit(s, nc, block):
        ops = s.ops
        for o in ops:
            for d in o["deps"]:
                if ops[d]["dma"] is None:
                    ops[d]["mark"] = True
            if o["bar"] is not None:
                for e, d in o["bar"][0].items():
                    ops[d]["mark"] = True
        cnt = {e: 0 for e in ("pe", "act", "dve", "pool", "sp")}
        for o in ops:
            if o["dma"] is None:
                if o["mark"]:
                    cnt[o["eng"]] += 1
                o["sval"] = cnt[o["eng"]]
        stack = contextlib.ExitStack()
        esem = {e: stack.enter_context(nc.semaphore("s_" + e)) for e in cnt}
        dsem = {k: stack.enter_context(nc.semaphore("d_" + str(k))) for k in s.dmacnt}
        byeng = {e: [] for e in cnt}
        for o in ops:
            byeng[o["eng"]].append(o)

        def run(eng_name, E):
            known = {}

            def need(sem, val):
                if known.get(sem.name if hasattr(sem, "name") else id(sem), 0) < val:
                    E.wait_ge(sem, val)
                    known[sem.name if hasattr(sem, "name") else id(sem)] = val

            for o in byeng[eng_name]:
                if o["bar"] is not None:
                    lastc, dcnt = o["bar"]
                    for e, d in lastc.items():
                        if e != eng_name or e != "pe":
                            need(esem[e], ops[d]["sval"])
                    for k, c in dcnt.items():
                        need(dsem[k], 16 * c)
                for d in sorted(o["deps"]):
                    do = ops[d]
                    if do["dma"] is not None:
                        need(dsem[do["dma"]], do["dval"])
                    else:
                        if do["eng"] == "pe" and eng_name == "pe":
                            continue
                        need(esem[do["eng"]], do["sval"])
                ins = o["fn"](E)
                if o["dma"] is not None:
                    ins.then_inc(dsem[o["dma"]], 16)
                elif o["mark"]:
                    ins.then_inc(esem[eng_name], 1)
            if eng_name == "sp":
                for k, c in s.dmacnt.items():
                    need(dsem[k], 16 * c)

        @block.tensor
        def _(E):
            run("pe", E)

        @block.scalar
        def _(E):
            run("act", E)

        @block.vector
        def _(E):
            run("dve", E)

        @block.gpsimd
        def _(E):
            run("pool", E)

        @block.sync
        def _(E):
            run("sp", E)

        return stack


def slopes(cfg):
    return [2.0 ** (-8.0 * (h + 1) / cfg.H) for h in range(cfg.H)]


def build(cfg):
    D, S, T, H, KC, DFF, FC = cfg.D, cfg.S, cfg.T, cfg.H, cfg.KC, cfg.DFF, cfg.FC
    NKT, NTALL, NT, NQB, TB = cfg.NKT, cfg.NTALL, cfg.NT, cfg.NQB, cfg.TB
    nc = bass.Bass("TRN2", target_bir_lowering=False)
    P = Prog()

    def din(name, shape, dt=F32):
        return nc.dram_tensor(name, list(shape), dt, kind="ExternalInput")

    x_rot = din("x_rot", [S, D])
    meta = din("meta", [N_META, D])
    gvec = din("gvec", [3, D])
    w_in = din("w_in", [D, 9 * D])
    w_br = din("w_br", [D, D])
    w_bd = din("w_bd", [D, D])
    w_out = din("w_out", [D, D])
    w_g = din("w_g", [D, DFF])
    w_u = din("w_u", [D, DFF])
    w_d = din("w_d", [DFF, D])
    rdecay = din("rdecay", [2, H])
    dlam = din("dlam", [4, 128])
    subln = din("subln", [256])
    c_absd = din("c_absd", [NQB, 128, NKT])
    c_sgn = din("c_sgn", [NQB, 2, NKT * 128], BF16)
    c_rb = din("c_rb", [H, 2, 512], BF16)
    c_babs = din("c_babs", [4, 128, 512])
    c_dist = din("c_dist", [2, 128, NKT])
    c_sq = din("c_sq", [4, 128, 128])
    c_kd = din("c_kd", [128, 2])
    c_ident = din("c_ident", [128, 128], BF16)
    y = nc.dram_tensor("y", [T, D], F32, kind="ExternalOutput")

    def dscr(name, shape, dt=BF16):
        return nc.dram_tensor(name, list(shape), dt)

    wb_in = {g: dscr("wb_in_" + g, [D, D]) for g in ("rq", "rk", "rv", "rg", "dq", "dk", "dv", "ga", "gb")}; wb_br = dscr("wb_br", [D, D]); wb_bd = dscr("wb_bd", [D, D])
    wb_out = dscr("wb_out", [D, D]); wb_g = dscr("wb_g", [D, DFF]); wb_u = dscr("wb_u", [D, DFF])
    wb_d = dscr("wb_d", [DFF, D])
    NR = NKT * 128
    rKtm = dscr("rKtm", [NR, D]); rVtm = dscr("rVtm", [NR, D]); dVtm = dscr("dVtm", [NR, D])
    dKT = dscr("dKT", [D, NR])
    rKT = dscr("rKT", [D, T]); rQT = dscr("rQT", [D, T]); rGT = dscr("rGT", [D, T]); dQT = dscr("dQT", [D, T])
    gaT = dscr("gaT", [D, T]); gbT = dscr("gbT", [D, T]); ZrT = dscr("ZrT", [D, T]); ZdT = dscr("ZdT", [D, T])
    h1 = dscr("h1", [T, D], F32)

    stack = contextlib.ExitStack()
    NB16 = 78 * 1024
    NF32 = 11 * 1024
    A16 = stack.enter_context(nc.sbuf_tensor("A16", [128, NB16], BF16))
    A32 = stack.enter_context(nc.sbuf_tensor("A32", [128, NF32], F32))
    ident = stack.enter_context(nc.sbuf_tensor("ident", [128, 128], BF16))
    ones32 = stack.enter_context(nc.sbuf_tensor("ones32", [128, 128], F32))
    ones16 = stack.enter_context(nc.sbuf_tensor("ones16", [128, 128], BF16))
    gT = stack.enter_context(nc.sbuf_tensor("gT", [128, 3, KC], F32))
    csq = stack.enter_context(nc.sbuf_tensor("csq", [128, 4, 128], F32))
    ckd = stack.enter_context(nc.sbuf_tensor("ckd", [128, 2], F32))
    sm = stack.enter_context(nc.sbuf_tensor("sm", [128, 64], F32))
    epsc = stack.enter_context(nc.sbuf_tensor("epsc", [128, 2], F32))
    trigt = stack.enter_context(nc.sbuf_tensor("trigt", [128, 2], F32))
    PS = [stack.enter_context(nc.psum_tensor("ps%d" % i, [128, 512], F32)) for i in range(8)]
    PB = [Buf("ps%d" % i) for i in range(8)]
    B_const = Buf("const")

    class Arena:
        def __init__(s):
            s.o16 = 0; s.o32 = 0

        def a16(s, n, shape=None):
            v = A16[:, s.o16:s.o16 + n]; s.o16 += n
            assert s.o16 <= NB16, ("bf16 arena overflow", s.o16)
            return v

        def a32(s, n):
            v = A32[:, s.o32:s.o32 + n]; s.o32 += n
            assert s.o32 <= NF32, ("f32 arena overflow", s.o32)
            return v

    evq = [0]

    def ev_eng():
        evq[0] += 1
        return "act" if evq[0] % 2 else "dve"

    def copy_op(eng, out, in_, r, w, scale=None):
        if eng == "act":
            if scale is None:
                P.op("act", lambda E: E.activation(out=out, in_=in_, func=AF.Copy), r=r, w=w)
            else:
                P.op("act", lambda E: E.activation(out=out, in_=in_, func=AF.Copy, scale=scale), r=r, w=w)
        else:
            if scale is None:
                P.op("dve", lambda E: E.tensor_copy(out=out, in_=in_), r=r, w=w)
            else:
                P.op("dve", lambda E: E.tensor_scalar(out=out, in0=in_, scalar1=scale, scalar2=None, op0=ALU.mult), r=r, w=w)

    ckey = [0]

    def dma(out, in_, r, w, key, eng="sp"):
        if key == "c0":
            ckey[0] += 1
            key = "c%d" % ckey[0]
        P.op(eng, lambda E: E.dma_start(out=out, in_=in_), r=r, w=w, dma=key)

    B_x = Buf("x_in"); B_w = Buf("w_f32")
    dma(ident[:], c_ident.ap(), [B_x], [B_const], "c0")
    dma(csq[:], c_sq.ap().rearrange("a p f -> p a f"), [B_x], [B_const], "c0")
    dma(ckd[:], c_kd.ap(), [B_x], [B_const], "c0")
    with nc.allow_non_contiguous_dma(reason="tiny gamma relayout"):
        dma(gT[:], gvec.ap().rearrange("a (kc p) -> p a kc", p=128), [B_x], [B_const], "c0")
    P.op("dve", lambda E: E.memset(ones32[:], 1.0), w=[B_const])
    P.op("dve", lambda E: E.memset(ones16[:], 1.0), w=[B_const])
    P.op("dve", lambda E: E.memset(epsc[:], EPS), w=[B_const])

    WB = {}

    pending_casts = []
    pcnt = [0]

    def cast_w(name, src, dst, c0, c1, rows, d0=None, defer=False):
        if name not in WB:
            WB[name] = Buf("wb_" + name)
        if defer:
            pending_casts.append((name, src, dst, c0, c1, rows, d0))
            return
        b = WB[name]
        d0 = c0 if d0 is None else d0
        RB = 512 if (c1 - c0) <= 4096 else 128
        trig = Buf("trig_" + name)
        P.op("dve", lambda E: E.memset(trigt[:, 0:1], 0.0), w=[trig])
        for r0 in range(0, rows, RB):
            r1 = min(rows, r0 + RB)
            P.op("pool", lambda E, r0=r0, r1=r1: E.dma_start(out=dst.ap()[r0:r1, d0:d0 + c1 - c0], in_=src.ap()[r0:r1, c0:c1]),
                 r=[B_w, trig], w=[b], dma="cw_" + name)

    def next_cast():
        if pending_casts:
            a = pending_casts.pop(0)
            cast_w(*a, defer=False)

    GR = {"rq": 0, "rk": 1, "rv": 2, "rg": 3, "dq": 4, "dk": 5, "dv": 6, "ga": 7, "gb": 8}
    for g in ("rk", "rv", "dv", "dk", "rq", "dq", "rg", "ga", "gb"):
        cast_w(g, w_in, wb_in[g], GR[g] * D, (GR[g] + 1) * D, D, d0=0, defer=(g not in ("rk", "rv", "dv", "dk")))
    cast_w("br", w_br, wb_br, 0, D, D, defer=True); cast_w("bd", w_bd, wb_bd, 0, D, D, defer=True)
    cast_w("out", w_out, wb_out, 0, D, D, defer=True)
    cast_w("g", w_g, wb_g, 0, DFF, D, defer=True); cast_w("u", w_u, wb_u, 0, DFF, D, defer=True)
    cast_w("d", w_d, wb_d, 0, D, DFF, defer=True)

    psrr = {}

    def next_ps(lo=0, hi=8):
        k = (lo, hi)
        i = psrr.get(k, lo); psrr[k] = lo + ((i - lo + 1) % (hi - lo))
        return i

    def norm_T(ar, src_rows_ap, np_, gi, uT, uTb, tcol, xt, xtb, ub, ubb, ssc, src_buf=None):
        src_buf = src_buf or B_x
        dma(xt[:np_, :], src_rows_ap, [src_buf], [xtb], "xt")
        P.op("dve", lambda E: E.memset(ssc[:, 0:1], 0.0), w=[B_ss])
        P.op("act", lambda E: E.activation(out=ub[:np_, :], in_=xt[:np_, :], func=AF.Square, accum_out=ssc[:np_, 0:1]),
             r=[xtb], w=[ubb, B_ss])
        P.op("act", lambda E: E.activation(out=ssc[:np_, 1:2], in_=ssc[:np_, 0:1], func=AF.Ln, scale=1.0 / D, bias=epsc[:np_, 0:1]),
             r=[B_ss, B_const], w=[B_ss])
        P.op("act", lambda E: E.activation(out=ssc[:np_, 2:3], in_=ssc[:np_, 1:2], func=AF.Exp, scale=-0.5), r=[B_ss], w=[B_ss])
        P.op("act", lambda E: E.activation(out=ub[:np_, :], in_=xt[:np_, :], func=AF.Copy, scale=ssc[:np_, 2:3]),
             r=[xtb, B_ss], w=[ubb])
        for k0 in range(0, KC, 8):
            nk = min(8, KC - k0)
            pi = next_ps()
            pv = PS[pi][:].bitcast(BF16)
            for j in range(nk):
                P.op("pe", lambda E, j=j, k0=k0, pv=pv: E.transpose(out=pv[:, j * 128:j * 128 + np_],
                                                                   in_=ub[:np_, (k0 + j) * 128:(k0 + j + 1) * 128],
                                                                   identity=ident[:np_, :np_]),
                     r=[ubb, B_const], w=[PB[pi]])
            src = pv[:, 0:nk * 128].rearrange("p (k t) -> p k t", t=128)[:, :, 0:np_]
            gsl = gT[:, gi, k0:k0 + nk]
            P.op("dve", lambda E, src=src, gsl=gsl, k0=k0, nk=nk: E.tensor_tensor(
                out=uT[:, k0:k0 + nk, tcol:tcol + np_], in0=src,
                in1=gsl.unsqueeze(2).to_broadcast([128, nk, np_]), op=ALU.mult),
                r=[PB[pi], B_const], w=[uTb])

    B_ss = Buf("ss")

    def proj(actT, actb, ntok, wdram, wbuf, K, c0, ncols, mode, evac, wt, wtb, WC):
        nkc = K // 128
        kgs = [(k, min(nkc, k + 32)) for k in range(0, nkc, 32)]
        wr = [0]
        for cb0 in range(0, ncols, WC):
            wc = min(WC, ncols - cb0)
            tiles = []

            def load(kg):
                i = wr[0] % len(wt); wr[0] += 1
                ka, kb = kgs[kg]
                for k8 in range(ka, kb, 8):
                    k9 = min(kb, k8 + 8)
                    dma(wt[i][:, k8 - ka:k9 - ka, 0:wc],
                        wdram.ap()[k8 * 128:k9 * 128, c0 + cb0:c0 + cb0 + wc].rearrange("(k p) c -> p k c", p=128),
                        [wbuf], [wtb[i]], "ld_" + wtb[i].name)
                return i
            if mode == "tm":
                ntt = (ntok + 127) // 128
                for th0 in range(0, ntt, 4):
                    tts = list(range(th0, min(ntt, th0 + 4)))
                    banks = {tt: next_ps() for tt in tts}
                    for kg, (ka, kb) in enumerate(kgs):
                        if th0 == 0 or len(kgs) > 1:
                            wi = load(kg)
                            if len(kgs) == 1:
                                tiles = [wi]
                        else:
                            wi = tiles[0]
                        for tt in tts:
                            n = min(128, ntok - tt * 128)
                            for kc in range(ka, kb):
                                P.op("pe", lambda E, tt=tt, n=n, kc=kc, ka=ka, wi=wi, b=banks[tt]: E.matmul(
                                    PS[b][:n, 0:wc], lhsT=actT[:, kc, tt * 128:tt * 128 + n], rhs=wt[wi][:, kc - ka, 0:wc],
                                    start=(kc == 0), stop=(kc == nkc - 1)), r=[actb, wtb[wi]], w=[PB[banks[tt]]])
                    for tt in tts:
                        n = min(128, ntok - tt * 128)
                        evac(banks[tt], PS[banks[tt]][:n, 0:wc], tt * 128, n, cb0, wc)
            else:
                wi = load(0)
                for cc in range(0, wc, 128):
                    for tg in range(0, ntok, 512):
                        n = min(512, ntok - tg)
                        b = next_ps()
                        for kc in range(nkc):
                            P.op("pe", lambda E, kc=kc, cc=cc, tg=tg, n=n, b=b, wi=wi: E.matmul(
                                PS[b][:, 0:n], lhsT=wt[wi][:, kc, cc:cc + 128], rhs=actT[:, kc, tg:tg + n],
                                start=(kc == 0), stop=(kc == nkc - 1)), r=[actb, wtb[wi]], w=[PB[b]])
                        evac(b, PS[b][:, 0:n], tg, n, cb0 + cc, 128)

    ar = Arena()
    uT = ar.a16(KC * TB).rearrange("p (k t) -> p k t", t=TB); uTb = Buf("uT")
    WC1 = 512
    wt = [ar.a16(32 * WC1).rearrange("p (k c) -> p k c", c=WC1) for _ in range(2)]
    wtb = [Buf("wt0"), Buf("wt1")]
    ub = ar.a16(D); ubb = Buf("ub")
    ost = [ar.a16(512) for _ in range(4)]; ostb = [Buf("ost%d" % i) for i in range(4)]
    xt = ar.a32(D); xtb = Buf("xt")
    osr = [0]
    B_scr = {n: Buf("scr_" + n) for n in ("rKtm", "rVtm", "dVtm", "dKT", "rKT", "rQT", "rGT", "dQT", "gaT", "gbT", "ZrT", "ZdT", "h1", "y")}

    def mk_evac(dst, dstb, tm, row0, scale=None, func=None):
        def evac(b, pap, t0, n, c0, ncol):
            i = osr[0] % 4; osr[0] += 1
            if tm:
                o = ost[i][:n, 0:ncol]
            else:
                o = ost[i][:, 0:n]
            if func is not None:
                P.op("act", lambda E: E.activation(out=o, in_=pap, func=func), r=[PB[b]], w=[ostb[i]])
            else:
                copy_op(ev_eng(), o, pap, [PB[b]], [ostb[i]], scale)
            if tm:
                dma(dst.ap()[row0 + t0:row0 + t0 + n, c0:c0 + ncol], o, [ostb[i]], [dstb], "st_" + dstb.name, eng="act")
            else:
                dma(dst.ap()[c0:c0 + ncol, row0 + t0:row0 + t0 + n], o, [ostb[i]], [dstb], "st_" + dstb.name, eng="act")
        return evac

    blocks = [(r0, TB) for r0 in range(0, S, TB)] + [(S, N_META)]

    def stage1_block(r0, ntok):
        own = r0 < T
        for t0 in range(0, ntok, 128):
            np_ = min(128, ntok - t0)
            src = (x_rot.ap()[r0 + t0:r0 + t0 + np_, :] if r0 < S else meta.ap()[0:np_, :])
            norm_T(ar, src, np_, 0, uT, uTb, t0, xt, xtb, ub, ubb, sm)
        def P_(g, dst, tm, scale=None, func=None, row0=r0):
            pcnt[0] += 1
            if pcnt[0] % 4 == 0:
                next_cast()
            proj(uT, uTb, ntok, wb_in[g], WB[g], D, 0, D, "tm" if tm else "fm",
                 mk_evac(dst, B_scr[dst.name], tm, row0, scale, func), wt, wtb, WC1)
        P_("rk", rKtm, True, scale=256.0 ** -0.5)
        P_("rv", rVtm, True)
        P_("dv", dVtm, True)
        P_("dk", dKT, False)
        if own:
            P_("rk", rKT, False, scale=256.0 ** -0.5)
            P_("rq", rQT, False)
            P_("dq", dQT, False, scale=128.0 ** -0.5)
            P_("rg", rGT, False, func=AF.Silu)
            P_("ga", gaT, False, func=AF.Sigmoid)
            P_("gb", gbT, False, func=AF.Sigmoid)
    blocks = [b_ for b_ in blocks if b_[0] >= T] + [b_ for b_ in blocks if b_[0] < T]
    for (r0_, ntok_) in blocks:
        stage1_block(r0_, ntok_)
    while pending_casts:
        next_cast()
    P.barrier()

    ar = Arena()
    rK = ar.a16(NKT * 256).rearrange("p (t c) -> p t c", c=256); rKb = Buf("rK")
    rV = ar.a16(NKT * 256).rearrange("p (t c) -> p t c", c=256); rVb = Buf("rV")
    kT = ar.a16(2 * T).rearrange("p (k t) -> p k t", t=T); kTb = Buf("kT")
    qT = ar.a16(2 * T).rearrange("p (k t) -> p k t", t=T); qTb = Buf("qT")
    gTt = ar.a16(2 * T).rearrange("p (k t) -> p k t", t=T); gTb = Buf("gTt")
    qf = ar.a16(2 * T).rearrange("p (k t) -> p k t", t=T); qfb = Buf("qf")
    qb_ = ar.a16(2 * T).rearrange("p (k t) -> p k t", t=T); qbb = Buf("qb")
    Sbs = ar.a16(NT * 512).rearrange("p (c k v) -> p c k v", k=2, v=256); Sbsb = Buf("Sbs")
    Sf16 = ar.a16(512).rearrange("p (k v) -> p k v", v=256); Sf16b = Buf("Sf16")
    kw = [ar.a16(256) for _ in range(2)]; kwb = [Buf("kw0"), Buf("kw1")]
    aT = [ar.a16(128) for _ in range(2)]; aTb = [Buf("aT0"), Buf("aT1")]
    zo = [ar.a16(512) for _ in range(2)]; zob = [Buf("zo0"), Buf("zo1")]
    Mh = ar.a32(128); Mhb = Buf("Mh")
    Mt = ar.a32(128)
    QD = ar.a32(256).rearrange("p (a t) -> p a t", t=128); QDb = Buf("QD")
    wfb_ = ar.a32(2 * NKT).rearrange("p (a t) -> p a t", t=NKT); wfbb = Buf("wfb")
    cdist = ar.a32(2 * NKT).rearrange("p (a t) -> p a t", t=NKT)
    kdc = ar.a32(4); kdcb = Buf("kdc")
    Sm = ar.a32(1024).rearrange("p (d k v) -> p d k v", k=2, v=256); Smb = [Buf("Smf"), Buf("Smb")]
    osq = ar.a32(1024).rearrange("p (k t) -> p k t", t=512); osqb = Buf("osq")
    orr = ar.a32(512); orrb = Buf("orr")
    ld_raw = ar.a32(2 * H).rearrange("p (a h) -> p a h", h=H); ldb = Buf("ld")
    dma(cdist[:], c_dist.ap().rearrange("a p t -> p a t"), [B_x], [B_const], "c0")
    dma(ld_raw[:].rearrange("p a h -> p (a h)"), rdecay.ap().rearrange("a h -> (a h)").partition_broadcast(128), [B_x], [ldb], "c0")
    P.op("act", lambda E: E.activation(out=ld_raw[:], in_=ld_raw[:], func=AF.Exp), r=[ldb], w=[ldb])
    P.op("dve", lambda E: E.tensor_scalar(out=ld_raw[:], in0=ld_raw[:], scalar1=-1.0, scalar2=None, op0=ALU.mult),
         r=[ldb], w=[ldb])

    def tile_np(kt):
        return N_META if kt == NKT - 1 else 128

    def flat(a):
        return a.rearrange("p k v -> p (k v)")

    def ret_upd_state(d, c):
        i = (c + d) % 2
        P.op("dve", lambda E: E.tensor_scalar(out=kw[i][:, :], in0=rK[:, c, :], scalar1=kdc[:, d:d + 1], scalar2=None,
                                               op0=ALU.mult), r=[rKb, kdcb], w=[kwb[i]])
        b = next_ps(4, 8)
        for dk in range(2):
            P.op("pe", lambda E, dk=dk: E.matmul(PS[b][:, dk * 256:(dk + 1) * 256], lhsT=kw[i][:, dk * 128:(dk + 1) * 128],
                                                 rhs=rV[:, c, :], start=(dk == 0), stop=(dk == 1)),
                 r=[kwb[i], rVb], w=[PB[b]])
        P.op("dve", lambda E: E.scalar_tensor_tensor(
            out=flat(Sm[:, d, :, :]), in0=flat(Sm[:, d, :, :]),
            scalar=kdc[:, 2 + d:3 + d], in1=PS[b][:, :], op0=ALU.mult, op1=ALU.add),
            r=[Smb[d], PB[b], kdcb], w=[Smb[d]])

    def ret_chunk(h, c, c4, po):
        cs = slice(c * 128, (c + 1) * 128)
        P.op("act", lambda E: E.activation(out=flat(Sf16[:]), in_=flat(Sm[:, 0, :, :]), func=AF.Copy),
             r=[Smb[0]], w=[Sf16b])
        bs = next_ps(4, 8)
        for dk in range(2):
            P.op("pe", lambda E, dk=dk: E.matmul(PS[bs][:, 0:128], lhsT=kT[:, dk, cs], rhs=qT[:, dk, cs],
                                                 start=(dk == 0), stop=(dk == 1)), r=[kTb, qTb], w=[PB[bs]])
        ia = c % 2
        P.op("dve", lambda E: E.tensor_tensor(out=aT[ia][:], in0=PS[bs][:, 0:128], in1=Mh[:], op=ALU.mult),
             r=[PB[bs], Mhb], w=[aTb[ia]])
        for vh in range(2):
            vs = slice(vh * 128, (vh + 1) * 128)
            oc = slice((c - c4) * 128, (c - c4 + 1) * 128)
            seq = [(rV[:, c, vs], aT[ia][:], [rVb, aTb[ia]])]
            for dk in range(2):
                seq.append((Sf16[:, dk, vs], qf[:, dk, cs], [Sf16b, qfb]))
                seq.append((Sbs[:, c, dk, vs], qb_[:, dk, cs], [Sbsb, qbb]))
            ns = len(seq)
            for j, (l, r_, rb_) in enumerate(seq):
                P.op("pe", lambda E, l=l, r_=r_, j=j, vh=vh, oc=oc: E.matmul(
                    PS[po[vh]][:, oc], lhsT=l, rhs=r_, start=(j == 0), stop=(j == ns - 1)),
                    r=rb_, w=[PB[po[vh]]])
        if c < NT - 1:
            ret_upd_state(0, c)

    def ret_group(h, c4):
        po = [2, 3]
        c5 = min(NT, c4 + 4)
        for c in range(c4, c5):
            ret_chunk(h, c, c4, po)
        n = (c5 - c4) * 128
        ts_ = slice(c4 * 128, c4 * 128 + n)
        for vh in range(2):
            P.op("act", lambda E, vh=vh: E.activation(out=osq[:, vh, 0:n], in_=PS[po[vh]][:, 0:n], func=AF.Square),
                 r=[PB[po[vh]]], w=[osqb])
        br = next_ps(4, 8)
        for vh in range(2):
            P.op("pe", lambda E, vh=vh: E.matmul(PS[br][:, 0:n], lhsT=ones32[:], rhs=osq[:, vh, 0:n],
                                                 start=(vh == 0), stop=(vh == 1)), r=[osqb, B_const], w=[PB[br]])
        P.op("act", lambda E: E.activation(out=orr[:, 0:n], in_=PS[br][:, 0:n], func=AF.Ln, scale=1.0 / 256, bias=epsc[:, 0:1]),
             r=[PB[br], B_const], w=[orrb])
        P.op("act", lambda E: E.activation(out=orr[:, 0:n], in_=orr[:, 0:n], func=AF.Exp, scale=-0.5), r=[orrb], w=[orrb])
        for vh in range(2):
            P.op("dve", lambda E, vh=vh: E.tensor_tensor(out=osq[:, vh, 0:n], in0=PS[po[vh]][:, 0:n], in1=orr[:, 0:n],
                                                         op=ALU.mult), r=[PB[po[vh]], orrb], w=[osqb])
            P.op("dve", lambda E, vh=vh: E.tensor_tensor(out=zo[vh][:, 0:n], in0=osq[:, vh, 0:n], in1=gTt[:, vh, ts_],
                                                         op=ALU.mult), r=[osqb, gTb], w=[zob[vh]])
            dma(ZrT.ap()[h * 256 + vh * 128:h * 256 + (vh + 1) * 128, ts_], zo[vh][:, 0:n], [zob[vh]], [B_scr["ZrT"]], "st_ZrT", eng="act")

    def ret_head(h):
        hc = slice(h * 256, (h + 1) * 256)
        for t8 in range(0, NTALL, 16):
            t9 = min(NTALL, t8 + 16)
            dma(rK[:, t8:t9, :], rKtm.ap()[t8 * 128:t9 * 128, hc].rearrange("(t p) c -> p t c", p=128), [B_scr["rKtm"]], [rKb], "rK")
        dma(rK[:N_META, NTALL, :], rKtm.ap()[S:S + N_META, hc], [B_scr["rKtm"]], [rKb], "rK")
        for t8 in range(0, NTALL, 16):
            t9 = min(NTALL, t8 + 16)
            dma(rV[:, t8:t9, :], rVtm.ap()[t8 * 128:t9 * 128, hc].rearrange("(t p) c -> p t c", p=128), [B_scr["rVtm"]], [rVb], "rV")
        dma(rV[:N_META, NTALL, :], rVtm.ap()[S:S + N_META, hc], [B_scr["rVtm"]], [rVb], "rV")
        for (dst, dstb, srcT, nm) in ((kT, kTb, rKT, "rKT"), (qT, qTb, rQT, "rQT"), (gTt, gTb, rGT, "rGT")):
            dma(dst[:], srcT.ap()[hc, :].rearrange("(k p) t -> p k t", p=128), [B_scr[nm]], [dstb], "r" + nm)
        for d in range(2):
            lg = ld_raw[:, d, h:h + 1]
            P.op("act", lambda E, d=d, lg=lg: E.activation(out=wfb_[:, d, :], in_=cdist[:, d, :], func=AF.Exp, scale=lg),
                 r=[B_const, ldb], w=[wfbb])
            P.op("act", lambda E, d=d, lg=lg: E.activation(out=QD[:, d, :], in_=csq[:, 2 + d, :], func=AF.Exp, scale=lg),
                 r=[B_const, ldb], w=[QDb])
            P.op("act", lambda E, d=d, lg=lg: E.activation(out=kdc[:, d:d + 1], in_=ckd[:, d:d + 1], func=AF.Exp, scale=lg),
                 r=[B_const, ldb], w=[kdcb])
            P.op("act", lambda E, d=d, lg=lg: E.activation(out=kdc[:, 2 + d:3 + d], in_=lg, func=AF.Exp, scale=128.0),
                 r=[B_const, ldb], w=[kdcb])
        P.op("act", lambda E: E.activation(out=Mh[:], in_=csq[:, 0, :], func=AF.Exp, scale=ld_raw[:, 0, h:h + 1]),
             r=[B_const, ldb], w=[Mhb])
        P.op("act", lambda E: E.activation(out=Mt[:], in_=csq[:, 1, :], func=AF.Exp, scale=ld_raw[:, 1, h:h + 1]),
             r=[B_const, ldb, Mhb], w=[Mhb])
        P.op("dve", lambda E: E.tensor_tensor(out=Mh[:], in0=Mh[:], in1=Mt[:], op=ALU.add), r=[Mhb], w=[Mhb])
        for d, (dst, dstb) in enumerate(((qf, qfb), (qb_, qbb))):
            P.op("dve", lambda E, d=d, dst=dst: E.tensor_tensor(
                out=dst[:].rearrange("p k (c i) -> p (k c) i", i=128),
                in0=qT[:].rearrange("p k (c i) -> p (k c) i", i=128),
                in1=QD[:, d, :].unsqueeze(1).to_broadcast([128, 2 * NT, 128]), op=ALU.mult),
                r=[qTb, QDb], w=[dstb])
        pin = [0, 1]
        for d in range(2):
            for kt in range(NT, NKT):
                np_ = tile_np(kt)
                i = (kt + d) % 2
                P.op("dve", lambda E, i=i, kt=kt, d=d, np_=np_: E.tensor_scalar(
                    out=kw[i][:np_, :], in0=rK[:np_, kt, :], scalar1=wfb_[:np_, d, kt:kt + 1], scalar2=None, op0=ALU.mult),
                    r=[rKb, wfbb], w=[kwb[i]])
                for dk in range(2):
                    P.op("pe", lambda E, i=i, kt=kt, d=d, dk=dk, np_=np_: E.matmul(
                        PS[pin[d]][:, dk * 256:(dk + 1) * 256], lhsT=kw[i][:np_, dk * 128:(dk + 1) * 128],
                        rhs=rV[:np_, kt, :], start=(kt == NT and dk == 0), stop=(kt == NKT - 1 and dk == 1),
                        skip_group_check=True),
                        r=[kwb[i], rVb], w=[PB[pin[d]]])
            P.op("dve", lambda E, d=d: E.tensor_copy(out=flat(Sm[:, d, :, :]), in_=PS[pin[d]][:, :]),
                 r=[PB[pin[d]]], w=[Smb[d]])
        for c in range(NT - 1, -1, -1):
            P.op("act", lambda E, c=c: E.activation(out=flat(Sbs[:, c, :, :]), in_=flat(Sm[:, 1, :, :]), func=AF.Copy),
                 r=[Smb[1]], w=[Sbsb])
            if c > 0:
                ret_upd_state(1, c)
        for c4 in range(0, NT, 4):
            ret_group(h, c4)

    for h in range(H):
        ret_head(h)
    P.barrier()

    ar = Arena()
    k1s = [ar.a16(NKT * 128) for _ in range(2)]; k2s = [ar.a16(NKT * 128) for _ in range(2)]
    kkbs = [Buf("kk0"), Buf("kk1")]
    vv = ar.a16(NKT * 256).rearrange("p (t c) -> p t c", c=256); vvb = Buf("vv")
    sg = [ar.a16(NKT * 128) for _ in range(2)]; sgb = [Buf("sg0"), Buf("sg1")]
    qq = [ar.a16(1024).rearrange("p (m t) -> p m t", t=512) for _ in range(2)]; qqb = [Buf("qq0"), Buf("qq1")]
    rbt = ar.a16(512); rbb = Buf("rb")
    ee = [ar.a16(512) for _ in range(4)]; eeb = [Buf("ee%d" % i) for i in range(4)]
    zd = [ar.a16(512) for _ in range(2)]; zdb = [Buf("zd0"), Buf("zd1")]
    babs = ar.a32(4 * 512).rearrange("p (a t) -> p a t", t=512)
    absd = ar.a32(NQB * NKT).rearrange("p (a t) -> p a t", t=NKT)
    bcol = ar.a32(NKT); bcolb = Buf("bcol")
    zacc = [ar.a32(512) for _ in range(2)]; zaccb = [Buf("zacc0"), Buf("zacc1")]
    sp_ = [ar.a32(512) for _ in range(2)]; spb = [Buf("sp0"), Buf("sp1")]
    o32 = ar.a32(1024).rearrange("p (k t) -> p k t", t=512); o32b = Buf("o32")
    t32 = ar.a32(1024).rearrange("p (k t) -> p k t", t=512); t32b = Buf("t32")
    rz = ar.a32(1024).rearrange("p (k t) -> p k t", t=512); rzb = Buf("rz")
    lam = ar.a32(8); lamb = Buf("lam")
    sgc = ar.a32(2); sgcb = Buf("sgc")
    lraw = ar.a32(512)
    dma(babs[:], c_babs.ap().rearrange("a p t -> p a t"), [B_x], [B_const], "c0")
    dma(absd[:], c_absd.ap().rearrange("a p t -> p a t"), [B_x], [B_const], "c0")
    dma(sgc[:], subln.ap().rearrange("(k p) -> p k", p=128), [B_x], [sgcb], "c0")
    dma(lraw[:], dlam.ap().rearrange("a f -> (a f)").partition_broadcast(128), [B_x], [lamb], "c0")
    P.op("dve", lambda E: E.tensor_tensor(out=lraw[:, 0:128], in0=lraw[:, 0:128], in1=lraw[:, 128:256], op=ALU.mult), r=[lamb], w=[lamb])
    P.op("dve", lambda E: E.tensor_tensor(out=lraw[:, 256:384], in0=lraw[:, 256:384], in1=lraw[:, 384:512], op=ALU.mult), r=[lamb], w=[lamb])
    P.op("dve", lambda E: E.reduce_sum(out=lam[:, 0:1], in_=lraw[:, 0:128], axis=mybir.AxisListType.X), r=[lamb], w=[lamb])
    P.op("dve", lambda E: E.reduce_sum(out=lam[:, 1:2], in_=lraw[:, 256:384], axis=mybir.AxisListType.X), r=[lamb], w=[lamb])
    P.op("act", lambda E: E.activation(out=lam[:, 2:4], in_=lam[:, 0:2], func=AF.Exp), r=[lamb], w=[lamb])
    P.op("dve", lambda E: E.tensor_tensor(out=lam[:, 4:5], in0=lam[:, 3:4], in1=lam[:, 2:3], op=ALU.subtract), r=[lamb], w=[lamb])
    P.op("dve", lambda E: E.tensor_scalar(out=lam[:, 4:5], in0=lam[:, 4:5], scalar1=-cfg.LAM_INIT, scalar2=None, op0=ALU.add), r=[lamb], w=[lamb])
    P.op("dve", lambda E: E.tensor_scalar(out=sgc[:], in0=sgc[:], scalar1=1.0 - cfg.LAM_INIT, scalar2=None, op0=ALU.mult),
         r=[sgcb], w=[sgcb])
    SL = slopes(cfg)

    def diff_S(h, qb, qi, kt):
        np_ = tile_np(kt)
        k1, k2, kkb = k1s[h % 2], k2s[h % 2], kkbs[h % 2]
        diag = (qb * 4 <= kt < qb * 4 + 4)
        for m, kk in enumerate((k1, k2)):
            bS = next_ps(4, 8)
            ie = (2 * kt + m) % 4
            P.op("pe", lambda E, kk=kk, bS=bS, m=m: E.matmul(
                PS[bS][:np_, :], lhsT=kk[:, kt * 128:kt * 128 + np_], rhs=qq[qi][:, m, :], start=True, stop=diag),
                r=[kkb, qqb[qi]], w=[PB[bS]])
            if not diag:
                P.op("pe", lambda E, bS=bS: E.matmul(
                    PS[bS][:np_, :], lhsT=sg[qi][0:2, kt * 128:kt * 128 + np_], rhs=rbt[0:2, :], start=False, stop=True),
                    r=[sgb[qi], rbb], w=[PB[bS]])
                P.op("act", lambda E, bS=bS, ie=ie: E.activation(
                    out=ee[ie][:np_, :], in_=PS[bS][:np_, :], func=AF.Exp, bias=bcol[:np_, kt:kt + 1]),
                    r=[PB[bS], bcolb], w=[eeb[ie]])
            else:
                ci = kt - qb * 4
                P.op("dve", lambda E, bS=bS, m=m: E.scalar_tensor_tensor(
                    out=sp_[m][:], in0=babs[:, ci, :], scalar=-SL[h], in1=PS[bS][:, :], op0=ALU.mult, op1=ALU.add),
                    r=[PB[bS], B_const], w=[spb[m]])
                P.op("act", lambda E, m=m, ie=ie: E.activation(out=ee[ie][:], in_=sp_[m][:], func=AF.Exp),
                     r=[spb[m]], w=[eeb[ie]])

    def diff_AV(h, qb, qi, kt, pO):
        np_ = tile_np(kt)
        for m in range(2):
            ie = (2 * kt + m) % 4
            if kt == 0:
                P.op("dve", lambda E, m=m, ie=ie: E.tensor_copy(out=zacc[m][:], in_=ee[ie][:]), r=[eeb[ie]], w=[zaccb[m]])
            else:
                P.op("dve", lambda E, m=m, ie=ie: E.tensor_tensor(out=zacc[m][:np_, :], in0=zacc[m][:np_, :],
                                                                 in1=ee[ie][:np_, :], op=ALU.add),
                     r=[eeb[ie], zaccb[m]], w=[zaccb[m]])
            for vh in range(2):
                P.op("pe", lambda E, vh=vh, m=m, ie=ie: E.matmul(
                    PS[pO[m][vh]][:, :], lhsT=vv[:np_, kt, vh * 128:(vh + 1) * 128], rhs=ee[ie][:np_, :],
                    start=(kt == 0), stop=(kt == NKT - 1)), r=[vvb, eeb[ie]], w=[PB[pO[m][vh]]])

    def diff_iter(h, qb, qi):
        q0 = qb * 512
        dma(qq[qi][:], dQT.ap()[h * 256:(h + 1) * 256, q0:q0 + 512].rearrange("(m p) t -> p m t", p=128),
            [B_scr["dQT"]], [qqb[qi]], "qq%d" % qi)
        dma(sg[qi][0:2, :], c_sgn.ap()[qb], [B_x], [sgb[qi]], "sg%d" % qi)
        P.op("dve", lambda E: E.tensor_scalar(out=bcol[:], in0=absd[:, qb, :], scalar1=-SL[h], scalar2=None,
                                               op0=ALU.mult), r=[B_const], w=[bcolb])
        pO = [[0, 1], [2, 3]]
        for kt in range(NKT + 1):
            if kt < NKT:
                diff_S(h, qb, qi, kt)
            if kt >= 1:
                diff_AV(h, qb, qi, kt - 1, pO)
        for m in range(2):
            bz = next_ps(4, 8)
            P.op("pe", lambda E, m=m, bz=bz: E.matmul(PS[bz][:, :], lhsT=ones32[:], rhs=zacc[m][:], start=True, stop=True),
                 r=[zaccb[m], B_const], w=[PB[bz]])
            P.op("dve", lambda E, m=m, bz=bz: E.reciprocal(out=rz[:, m, :], in_=PS[bz][:, :]), r=[PB[bz]], w=[rzb])
        for vh in range(2):
            P.op("dve", lambda E, vh=vh: E.tensor_tensor(out=t32[:, vh, :], in0=PS[pO[1][vh]][:, :], in1=rz[:, 1, :], op=ALU.mult),
                 r=[PB[pO[1][vh]], rzb], w=[t32b])
            P.op("dve", lambda E, vh=vh: E.tensor_tensor(out=o32[:, vh, :], in0=PS[pO[0][vh]][:, :], in1=rz[:, 0, :], op=ALU.mult),
                 r=[PB[pO[0][vh]], rzb], w=[o32b])
            P.op("dve", lambda E, vh=vh: E.scalar_tensor_tensor(out=o32[:, vh, :], in0=t32[:, vh, :], scalar=lam[:, 4:5],
                                                                in1=o32[:, vh, :], op0=ALU.mult, op1=ALU.add),
                 r=[t32b, o32b, lamb], w=[o32b])
            P.op("act", lambda E, vh=vh: E.activation(out=t32[:, vh, :], in_=o32[:, vh, :], func=AF.Square), r=[o32b, t32b], w=[t32b])
        br = next_ps(4, 8)
        for vh in range(2):
            P.op("pe", lambda E, vh=vh: E.matmul(PS[br][:, :], lhsT=ones32[:], rhs=t32[:, vh, :], start=(vh == 0), stop=(vh == 1)),
                 r=[t32b, B_const], w=[PB[br]])
        P.op("act", lambda E: E.activation(out=rz[:, 0, :], in_=PS[br][:, :], func=AF.Ln, scale=1.0 / 256, bias=epsc[:, 0:1]),
             r=[PB[br], rzb, B_const], w=[rzb])
        P.op("act", lambda E: E.activation(out=rz[:, 0, :], in_=rz[:, 0, :], func=AF.Exp, scale=-0.5), r=[rzb], w=[rzb])
        for vh in range(2):
            P.op("dve", lambda E, vh=vh: E.scalar_tensor_tensor(out=zd[vh][:], in0=o32[:, vh, :], scalar=sgc[:, vh:vh + 1],
                                                                in1=rz[:, 0, :], op0=ALU.mult, op1=ALU.mult),
                 r=[o32b, rzb, sgcb], w=[zdb[vh]])
            dma(ZdT.ap()[h * 256 + vh * 128:h * 256 + (vh + 1) * 128, q0:q0 + 512], zd[vh][:], [zdb[vh]], [B_scr["ZdT"]], "st_ZdT", eng="act")

    def diff_head(h, it0):
        k1, k2, kkb = k1s[h % 2], k2s[h % 2], kkbs[h % 2]
        dma(k1[:, 0:S + N_META], dKT.ap()[h * 256:h * 256 + 128, 0:S + N_META], [B_scr["dKT"]], [kkb], "kk%d" % (h % 2))
        dma(k2[:, 0:S + N_META], dKT.ap()[h * 256 + 128:h * 256 + 256, 0:S + N_META], [B_scr["dKT"]], [kkb], "kk%d" % (h % 2))
        for t8 in range(0, NTALL, 16):
            t9 = min(NTALL, t8 + 16)
            dma(vv[:, t8:t9, :], dVtm.ap()[t8 * 128:t9 * 128, h * 256:(h + 1) * 256].rearrange("(t p) c -> p t c", p=128), [B_scr["dVtm"]], [vvb], "vv")
        dma(vv[:N_META, NTALL, :], dVtm.ap()[S:S + N_META, h * 256:(h + 1) * 256], [B_scr["dVtm"]], [vvb], "vv")
        dma(rbt[0:2, :], c_rb.ap()[h], [B_x], [rbb], "rb")
        for qb in range(NQB):
            diff_iter(h, qb, (it0 + qb) % 2)

    for h in range(H):
        diff_head(h, h * NQB)
    P.barrier()

    ar = Arena()
    TF = 512
    WC4 = 512
    zT = ar.a16(KC * TF).rearrange("p (k t) -> p k t", t=TF); zTb = Buf("zT")
    mxbase = ar.a16(max(KC, 32) * TF).rearrange("p (k t) -> p k t", t=TF); mxTb = Buf("mxT")
    mxT = mxbase[:, 0:KC, :]
    aTf = mxbase; aTfb = mxTb
    wt4 = [ar.a16(32 * WC4).rearrange("p (k c) -> p k c", c=WC4) for _ in range(2)]
    wt4b = [Buf("w40"), Buf("w41")]
    gg = [ar.a16(TF) for _ in range(4)]; ggb = [Buf("gg%d" % i) for i in range(4)]
    ub4 = ar.a16(D); ub4b = Buf("ub4")
    gsb = ar.a16(TF); gsbb = Buf("gsb")
    xt4 = ar.a32(D); xt4b = Buf("xt4")
    gfin = ar.a32(D)
    hst = [ar.a32(WC4) for _ in range(3)]; hstb = [Buf("hst%d" % i) for i in range(3)]
    hrow = [ar.a32(WC4) for _ in range(3)]; hrowb = [Buf("hrow%d" % i) for i in range(3)]
    B_h1 = [Buf("h1_%d" % i) for i in range(4)]
    rr = [0]
    dma(gfin[:], gvec.ap()[2].partition_broadcast(128), [B_x], [B_const], "c0")

    def s4_mix(tb, pas, Zs, nm, wdr, wn, gsrc, gn):
        for k8 in range(0, KC, 8):
            k9 = min(KC, k8 + 8)
            dma(zT[:, k8:k9, :], Zs.ap()[k8 * 128:k9 * 128, tb:tb + TF].rearrange("(k p) t -> p k t", p=128), [B_scr[nm]], [zTb], "zT")

        def ev_mix(b, pap, t0, n, c0, ncol):
            i = rr[0] % 4; rr[0] += 1
            kc = c0 // 128
            dma(gg[i][:, 0:n], gsrc.ap()[c0:c0 + 128, tb + t0:tb + t0 + n], [B_scr[gn]], [ggb[i]], "gg%d" % i)
            if pas == 0:
                P.op("dve", lambda E: E.tensor_tensor(out=mxT[:, kc, t0:t0 + n], in0=pap, in1=gg[i][:, 0:n], op=ALU.mult),
                     r=[PB[b], ggb[i]], w=[mxTb])
            else:
                P.op("dve", lambda E: E.tensor_tensor(out=gsb[:, 0:n], in0=pap, in1=gg[i][:, 0:n], op=ALU.mult),
                     r=[PB[b], ggb[i]], w=[gsbb])
                P.op("dve", lambda E: E.tensor_tensor(out=mxT[:, kc, t0:t0 + n], in0=mxT[:, kc, t0:t0 + n], in1=gsb[:, 0:n], op=ALU.add),
                     r=[gsbb, mxTb], w=[mxTb])
        proj(zT, zTb, TF, wdr, WB[wn], D, 0, D, "fm", ev_mix, wt4, wt4b, WC4)

    def s4_seg(tb, f0, f1):
        def ev_gate(b, pap, t0, n, c0, ncol):
            kc = (c0 // 128)
            P.op("act", lambda E: E.activation(out=aTf[:, kc, t0:t0 + n], in_=pap, func=AF.Silu), r=[PB[b]], w=[aTfb])

        def ev_up(b, pap, t0, n, c0, ncol):
            kc = (c0 // 128)
            P.op("dve", lambda E: E.tensor_tensor(out=aTf[:, kc, t0:t0 + n], in0=pap, in1=aTf[:, kc, t0:t0 + n], op=ALU.mult),
                 r=[PB[b], aTfb], w=[aTfb])

        class _W:
            def __init__(s, a):
                s.a = a

            def ap(s):
                return s.a
        proj(zT, zTb, TF, _W(wb_g.ap()[:, f0 * 128:f1 * 128]), WB["g"], D, 0, (f1 - f0) * 128, "fm", ev_gate, wt4, wt4b, WC4)
        proj(zT, zTb, TF, _W(wb_u.ap()[:, f0 * 128:f1 * 128]), WB["u"], D, 0, (f1 - f0) * 128, "fm", ev_up, wt4, wt4b, WC4)

        def ev_dn(b, pap, t0, n, c0, ncol):
            i = rr[0] % 3; rr[0] += 1
            hb = B_h1[t0 // 128]
            dma(hrow[i][:n, 0:ncol], h1.ap()[tb + t0:tb + t0 + n, c0:c0 + ncol], [hb], [hrowb[i]], "hrow%d" % i)
            P.op("dve", lambda E: E.tensor_tensor(out=hst[i][:n, 0:ncol], in0=pap, in1=hrow[i][:n, 0:ncol], op=ALU.add),
                 r=[PB[b], hrowb[i]], w=[hstb[i]])
            dma(h1.ap()[tb + t0:tb + t0 + n, c0:c0 + ncol], hst[i][:n, 0:ncol], [hstb[i]], [hb], "st_h1_%d" % (t0 // 128), eng="act")
        proj(aTf, aTfb, TF, _W(wb_d.ap()[f0 * 128:f1 * 128, :]), WB["d"], (f1 - f0) * 128, 0, D, "tm", ev_dn, wt4, wt4b, WC4)

    def s4_block(tb):
        s4_mix(tb, 0, ZrT, "ZrT", wb_br, "br", gaT, "gaT")
        s4_mix(tb, 1, ZdT, "ZdT", wb_bd, "bd", gbT, "gbT")

        def ev_h1(b, pap, t0, n, c0, ncol):
            i = rr[0] % 3; rr[0] += 1
            hb = B_h1[t0 // 128]
            dma(hrow[i][:n, 0:ncol], x_rot.ap()[tb + t0:tb + t0 + n, c0:c0 + ncol], [B_x], [hrowb[i]], "hrow%d" % i)
            P.op("dve", lambda E: E.tensor_tensor(out=hst[i][:n, 0:ncol], in0=pap, in1=hrow[i][:n, 0:ncol], op=ALU.add),
                 r=[PB[b], hrowb[i]], w=[hstb[i]])
            dma(h1.ap()[tb + t0:tb + t0 + n, c0:c0 + ncol], hst[i][:n, 0:ncol], [hstb[i]], [hb], "st_h1_%d" % (t0 // 128), eng="act")
        proj(mxT, mxTb, TF, wb_out, WB["out"], D, 0, D, "tm", ev_h1, wt4, wt4b, WC4)
        for t0 in range(0, TF, 128):
            norm_T(ar, h1.ap()[tb + t0:tb + t0 + 128, :], 128, 1, zT, zTb, t0, xt4, xt4b, ub4, ub4b, sm, src_buf=B_h1[t0 // 128])
        for f0 in range(0, FC, 32):
            s4_seg(tb, f0, min(FC, f0 + 32))
        for t0 in range(0, TF, 128):
            s4_fin(tb, t0)

    def s4_fin(tb, t0):
        dma(xt4[:, :], h1.ap()[tb + t0:tb + t0 + 128, :], [B_h1[t0 // 128]], [xt4b], "xt")
        P.op("dve", lambda E: E.memset(sm[:, 0:1], 0.0), w=[B_ss])
        P.op("act", lambda E: E.activation(out=ub4[:, :], in_=xt4[:, :], func=AF.Square, accum_out=sm[:, 0:1]),
             r=[xt4b], w=[ub4b, B_ss])
        P.op("act", lambda E: E.activation(out=sm[:, 1:2], in_=sm[:, 0:1], func=AF.Ln, scale=1.0 / D, bias=epsc[:, 0:1]),
             r=[B_ss, B_const], w=[B_ss])
        P.op("act", lambda E: E.activation(out=sm[:, 2:3], in_=sm[:, 1:2], func=AF.Exp, scale=-0.5), r=[B_ss], w=[B_ss])
        P.op("dve", lambda E: E.scalar_tensor_tensor(out=xt4[:, :], in0=xt4[:, :], scalar=sm[:, 2:3], in1=gfin[:, :],
                                                     op0=ALU.mult, op1=ALU.mult), r=[xt4b, B_ss, B_const], w=[xt4b])
        dma(y.ap()[tb + t0:tb + t0 + 128, :], xt4[:, :], [xt4b], [B_scr["y"]], "st_y", eng="act")

    for tb in range(0, T, TF):
        s4_block(tb)

    with nc.allow_non_contiguous_dma(reason="small strided constant/layout loads"):
        with nc.Block() as block:
            sems = P.emit(nc, block)
        sems.close()
    stack.close()
    return nc


def host_consts(cfg, qt):
    S, T, H, NKT, NQB, NTALL = cfg.S, cfg.T, cfg.H, cfg.NKT, cfg.NQB, cfg.NTALL
    own0 = qt * T
    perm = np.concatenate([np.arange(own0, own0 + T), np.arange(0, own0), np.arange(own0 + T, S)])
    kpos = np.full((NKT, 128), 1e9, np.float64)
    kpos[:NTALL] = (N_META + perm).reshape(NTALL, 128)
    kpos[NTALL, :N_META] = np.arange(N_META)
    absd = np.zeros((NQB, 128, NKT), np.float32)
    sgn = np.zeros((NQB, 2, NKT * 128), np.float32)
    for qb in range(NQB):
        qc = N_META + own0 + qb * 512 + 255.5
        absd[qb] = np.abs(kpos - qc).T
        sg = np.where(kpos < qc, 1.0, -1.0).reshape(-1)
        sgn[qb, 0] = sg; sgn[qb, 1] = sg
    absd = np.minimum(absd, 1e6).astype(np.float32)
    sl = np.array(slopes(cfg), np.float64)
    qrel = np.arange(512) - 255.5
    val = (-sl[:, None] * qrel[None, :]).astype(np.float32)
    hi = val.astype(ml_dtypes.bfloat16)
    lo = (val - hi.astype(np.float32)).astype(ml_dtypes.bfloat16)
    rb = np.stack([hi, lo], axis=1)
    p = np.arange(128)[:, None]; f = np.arange(512)[None, :]
    babs = np.stack([np.abs(ci * 128 + p - f) for ci in range(4)]).astype(np.float32)
    p0 = N_META + own0; p1 = p0 + T
    distf = np.where(kpos < p0, p0 - 1 - kpos, BIG)
    distb = np.where((kpos >= p1) & (kpos < 1e8), kpos - p1, BIG)
    dist = np.stack([distf.T, distb.T]).astype(np.float32)
    s_ = np.arange(128)[:, None]; t_ = np.arange(128)[None, :]
    dpos = np.where(t_ >= s_, t_ - s_, BIG)
    dneg = np.where(s_ > t_, s_ - t_, BIG)
    qdf = np.broadcast_to(t_ + 1.0, (128, 128))
    qdb = np.broadcast_to(128.0 - t_, (128, 128))
    sq = np.stack([dpos, dneg, qdf, qdb]).astype(np.float32)
    kd = np.stack([127.0 - np.arange(128), np.arange(128) * 1.0], axis=1).astype(np.float32)
    return perm, dict(c_absd=absd, c_sgn=sgn.astype(ml_dtypes.bfloat16), c_rb=rb, c_babs=babs, c_dist=dist,
                      c_sq=sq, c_kd=kd, c_ident=np.eye(128, dtype=np.float32).astype(ml_dtypes.bfloat16))


_NC_CACHE = {}


def run(cfg, inp):
    key = (cfg.D, cfg.S)
    if key not in _NC_CACHE:
        _NC_CACHE[key] = build(cfg)
    nc = _NC_CACHE[key]
    f32 = lambda a: np.ascontiguousarray(np.asarray(a, dtype=np.float32))
    x = f32(inp["x"])
    shared = dict(
        meta=f32(inp["meta_tokens"]),
        gvec=np.stack([f32(inp["norm_mix_g"])[0], f32(inp["norm_ffn_g"])[0], f32(inp["norm_final_g"])]),
        w_in=f32(inp["w_in"])[0], w_br=f32(inp["w_branch_ret"])[0], w_bd=f32(inp["w_branch_diff"])[0],
        w_out=f32(inp["w_out"])[0], w_g=f32(inp["w_ffn_gate"])[0], w_u=f32(inp["w_ffn_up"])[0],
        w_d=f32(inp["w_ffn_down"])[0], rdecay=f32(inp["ret_log_decay"])[0], dlam=f32(inp["diff_lambda"])[0],
        subln=f32(inp["diff_subln_g"])[0])
    in_maps = []
    for c in range(8):
        b, qt = c // 4, c % 4
        perm, consts = host_consts(cfg, qt)
        m = dict(shared)
        m["x_rot"] = np.ascontiguousarray(x[b][perm])
        m.update(consts)
        in_maps.append(m)
    res = run_bass_kernel_spmd(nc, in_maps, core_ids=list(range(8)))
    out = np.zeros((2, cfg.S, cfg.D), np.float32)
    for c in range(8):
        b, qt = c // 4, c % 4
        out[b, qt * cfg.T:(qt + 1) * cfg.T] = res.results[c]["y"]
    return out


def kernel(**inputs):
    return run(Cfg(4096, 8192), inputs)
```

```python
import math, contextlib
import numpy as np
import ml_dtypes
import concourse.bass as bass
import concourse.mybir as mybir
from concourse.bass_utils import run_bass_kernel_spmd

F32 = mybir.dt.float32
BF16 = mybir.dt.bfloat16
AF = mybir.ActivationFunctionType
ALU = mybir.AluOpType
BIG = 1.0e30
EPS = 1e-6
N_META = 16


class Cfg:
    def __init__(s, D=4096, S=8192):
        s.D, s.S = D, S
        s.T = S // 4
        s.H = D // 256
        s.DIN = 9 * D
        s.DFF = ((8 * D + 3 * 256 - 1) // (3 * 256)) * 256
        s.KC = D // 128
        s.FC = s.DFF // 128
        s.NT = s.T // 128
        s.NTALL = S // 128
        s.NKT = s.NTALL + 1
        s.NQB = s.T // 512
        s.TB = min(1024, s.T)
        s.LAM_INIT = 0.8 - 0.6 * math.exp(-0.3 * 0)


class Buf:
    __slots__ = ("name",)

    def __init__(s, name):
        s.name = name


class Prog:
    COMPUTE = ("pe", "act", "dve", "pool")

    def __init__(s):
        s.ops = []
        s.lastw = {}
        s.rd_eng = {}
        s.rd_dma = {}
        s.dmacnt = {}
        s.bar = None
        s.bar_done = set()

    def op(s, eng, fn, r=(), w=(), dma=None):
        i = len(s.ops)
        deps = set()
        for b in tuple(r) + tuple(w):
            lw = s.lastw.get(b)
            if lw is not None:
                deps.add(lw)
        for b in w:
            for x in s.rd_eng.get(b, {}).values():
                deps.add(x)
            for x in s.rd_dma.get(b, ()):
                deps.add(x)
        o = dict(eng=eng, fn=fn, deps=deps, dma=dma, mark=False, bar=None)
        if s.bar is not None and eng not in s.bar_done:
            o["bar"] = s.bar
            s.bar_done.add(eng)
        if dma is not None:
            s.dmacnt[dma] = s.dmacnt.get(dma, 0) + 1
            o["dval"] = 16 * s.dmacnt[dma]
        s.ops.append(o)
        for b in w:
            s.lastw[b] = i
            s.rd_eng[b] = {}
            s.rd_dma[b] = []
        for b in r:
            if dma is not None:
                s.rd_dma.setdefault(b, []).append(i)
            else:
                s.rd_eng.setdefault(b, {})[eng] = i
        return i

    def barrier(s):
        last = {}
        for i, o in enumerate(s.ops):
            if o["dma"] is None:
                last[o["eng"]] = i
        s.bar = (dict(last), dict(s.dmacnt))
        s.bar_done = set()

    def emit(s, nc, block):
        ops = s.ops
        for o in ops:
            for d in o["deps"]:
                if ops[d]["dma"] is None:
                    ops[d]["mark"] = True
            if o["bar"] is not None:
                for e, d in o["bar"][0].items():
                    ops[d]["mark"] = True
        cnt = {e: 0 for e in ("pe", "act", "dve", "pool", "sp")}
        for o in ops:
            if o["dma"] is None:
                if o["mark"]:
                    cnt[o["eng"]] += 1
                o["sval"] = cnt[o["eng"]]
        stack = contextlib.ExitStack()
        esem = {e: stack.enter_context(nc.semaphore("s_" + e)) for e in cnt}
        dsem = {k: stack.enter_context(nc.semaphore("d_" + str(k))) for k in s.dmacnt}
        byeng = {e: [] for e in cnt}
        for o in ops:
            byeng[o["eng"]].append(o)

        def run(eng_name, E):
            known = {}

            def need(sem, val):
                if known.get(sem.name if hasattr(sem, "name") else id(sem), 0) < val:
                    E.wait_ge(sem, val)
                    known[sem.name if hasattr(sem, "name") else id(sem)] = val

            for o in byeng[eng_name]:
                if o["bar"] is not None:
                    lastc, dcnt = o["bar"]
                    for e, d in lastc.items():
                        if e != eng_name or e != "pe":
                            need(esem[e], ops[d]["sval"])
                    for k, c in dcnt.items():
                        need(dsem[k], 16 * c)
                for d in sorted(o["deps"]):
                    do = ops[d]
                    if do["dma"] is not None:
                        need(dsem[do["dma"]], do["dval"])
                    else:
                        if do["eng"] == "pe" and eng_name == "pe":
                            continue
                        need(esem[do["eng"]], do["sval"])
                ins = o["fn"](E)
                if o["dma"] is not None:
                    ins.then_inc(dsem[o["dma"]], 16)
                elif o["mark"]:
                    ins.then_inc(esem[eng_name], 1)
            if eng_name == "sp":
                for k, c in s.dmacnt.items():
                    need(dsem[k], 16 * c)

        @block.tensor
        def _(E):
            run("pe", E)

        @block.scalar
        def _(E):
            run("act", E)

        @block.vector
        def _(E):
            run("dve", E)

        @block.gpsimd
        def _(E):
            run("pool", E)

        @block.sync
        def _(E):
            run("sp", E)

        return stack


def slopes(cfg):
    return [2.0 ** (-8.0 * (h + 1) / cfg.H) for h in range(cfg.H)]


def build(cfg):
    D, S, T, H, KC, DFF, FC = cfg.D, cfg.S, cfg.T, cfg.H, cfg.KC, cfg.DFF, cfg.FC
    NKT, NTALL, NT, NQB, TB = cfg.NKT, cfg.NTALL, cfg.NT, cfg.NQB, cfg.TB
    nc = bass.Bass("TRN2", target_bir_lowering=False)
    P = Prog()

    def din(name, shape, dt=F32):
        return nc.dram_tensor(name, list(shape), dt, kind="ExternalInput")

    x_rot = din("x_rot", [S, D])
    meta = din("meta", [N_META, D])
    gvec = din("gvec", [3, D])
    w_in = din("w_in", [D, 9 * D])
    w_br = din("w_br", [D, D])
    w_bd = din("w_bd", [D, D])
    w_out = din("w_out", [D, D])
    w_g = din("w_g", [D, DFF])
    w_u = din("w_u", [D, DFF])
    w_d = din("w_d", [DFF, D])
    rdecay = din("rdecay", [2, H])
    dlam = din("dlam", [4, 128])
    subln = din("subln", [256])
    c_absd = din("c_absd", [NQB, 128, NKT])
    c_sgn = din("c_sgn", [NQB, 2, NKT * 128], BF16)
    c_rb = din("c_rb", [H, 2, 512], BF16)
    c_babs = din("c_babs", [4, 128, 512])
    c_dist = din("c_dist", [2, 128, NKT])
    c_sq = din("c_sq", [4, 128, 128])
    c_kd = din("c_kd", [128, 2])
    c_ident = din("c_ident", [128, 128], BF16)
    y = nc.dram_tensor("y", [T, D], F32, kind="ExternalOutput")

    def dscr(name, shape, dt=BF16):
        return nc.dram_tensor(name, list(shape), dt)

    wb_in = {g: dscr("wb_in_" + g, [D, D]) for g in ("rq", "rk", "rv", "rg", "dq", "dk", "dv", "ga", "gb")}; wb_br = dscr("wb_br", [D, D]); wb_bd = dscr("wb_bd", [D, D])
    wb_out = dscr("wb_out", [D, D]); wb_g = dscr("wb_g", [D, DFF]); wb_u = dscr("wb_u", [D, DFF])
    wb_d = dscr("wb_d", [DFF, D])
    NR = NKT * 128
    rKtm = dscr("rKtm", [NR, D]); rVtm = dscr("rVtm", [NR, D]); dVtm = dscr("dVtm", [NR, D])
    dKT = dscr("dKT", [D, NR])
    rKT = dscr("rKT", [D, T]); rQT = dscr("rQT", [D, T]); rGT = dscr("rGT", [D, T]); dQT = dscr("dQT", [D, T])
    gaT = dscr("gaT", [D, T]); gbT = dscr("gbT", [D, T]); ZrT = dscr("ZrT", [D, T]); ZdT = dscr("ZdT", [D, T])
    h1 = dscr("h1", [T, D], F32)

    stack = contextlib.ExitStack()
    NB16 = 78 * 1024
    NF32 = 11 * 1024
    A16 = stack.enter_context(nc.sbuf_tensor("A16", [128, NB16], BF16))
    A32 = stack.enter_context(nc.sbuf_tensor("A32", [128, NF32], F32))
    ident = stack.enter_context(nc.sbuf_tensor("ident", [128, 128], BF16))
    ones32 = stack.enter_context(nc.sbuf_tensor("ones32", [128, 128], F32))
    ones16 = stack.enter_context(nc.sbuf_tensor("ones16", [128, 128], BF16))
    gT = stack.enter_context(nc.sbuf_tensor("gT", [128, 3, KC], F32))
    csq = stack.enter_context(nc.sbuf_tensor("csq", [128, 4, 128], F32))
    ckd = stack.enter_context(nc.sbuf_tensor("ckd", [128, 2], F32))
    sm = stack.enter_context(nc.sbuf_tensor("sm", [128, 64], F32))
    epsc = stack.enter_context(nc.sbuf_tensor("epsc", [128, 2], F32))
    trigt = stack.enter_context(nc.sbuf_tensor("trigt", [128, 2], F32))
    PS = [stack.enter_context(nc.psum_tensor("ps%d" % i, [128, 512], F32)) for i in range(8)]
    PB = [Buf("ps%d" % i) for i in range(8)]
    B_const = Buf("const")

    class Arena:
        def __init__(s):
            s.o16 = 0; s.o32 = 0

        def a16(s, n, shape=None):
            v = A16[:, s.o16:s.o16 + n]; s.o16 += n
            assert s.o16 <= NB16, ("bf16 arena overflow", s.o16)
            return v

        def a32(s, n):
            v = A32[:, s.o32:s.o32 + n]; s.o32 += n
            assert s.o32 <= NF32, ("f32 arena overflow", s.o32)
            return v

    evq = [0]

    def ev_eng():
        evq[0] += 1
        return "act" if evq[0] % 2 else "dve"

    def copy_op(eng, out, in_, r, w, scale=None):
        if eng == "act":
            if scale is None:
                P.op("act", lambda E: E.activation(out=out, in_=in_, func=AF.Copy), r=r, w=w)
            else:
                P.op("act", lambda E: E.activation(out=out, in_=in_, func=AF.Copy, scale=scale), r=r, w=w)
        else:
            if scale is None:
                P.op("dve", lambda E: E.tensor_copy(out=out, in_=in_), r=r, w=w)
            else:
                P.op("dve", lambda E: E.tensor_scalar(out=out, in0=in_, scalar1=scale, scalar2=None, op0=ALU.mult), r=r, w=w)

    ckey = [0]

    def dma(out, in_, r, w, key, eng="sp"):
        if key == "c0":
            ckey[0] += 1
            key = "c%d" % ckey[0]
        P.op(eng, lambda E: E.dma_start(out=out, in_=in_), r=r, w=w, dma=key)

    B_x = Buf("x_in"); B_w = Buf("w_f32")
    dma(ident[:], c_ident.ap(), [B_x], [B_const], "c0")
    dma(csq[:], c_sq.ap().rearrange("a p f -> p a f"), [B_x], [B_const], "c0")
    dma(ckd[:], c_kd.ap(), [B_x], [B_const], "c0")
    with nc.allow_non_contiguous_dma(reason="tiny gamma relayout"):
        dma(gT[:], gvec.ap().rearrange("a (kc p) -> p a kc", p=128), [B_x], [B_const], "c0")
    P.op("dve", lambda E: E.memset(ones32[:], 1.0), w=[B_const])
    P.op("dve", lambda E: E.memset(ones16[:], 1.0), w=[B_const])
    P.op("dve", lambda E: E.memset(epsc[:], EPS), w=[B_const])

    WB = {}

    pending_casts = []
    pcnt = [0]

    def cast_w(name, src, dst, c0, c1, rows, d0=None, defer=False):
        if name not in WB:
            WB[name] = Buf("wb_" + name)
        if defer:
            pending_casts.append((name, src, dst, c0, c1, rows, d0))
            return
        b = WB[name]
        d0 = c0 if d0 is None else d0
        RB = 512 if (c1 - c0) <= 4096 else 128
        trig = Buf("trig_" + name)
        P.op("dve", lambda E: E.memset(trigt[:, 0:1], 0.0), w=[trig])
        for r0 in range(0, rows, RB):
            r1 = min(rows, r0 + RB)
            P.op("pool", lambda E, r0=r0, r1=r1: E.dma_start(out=dst.ap()[r0:r1, d0:d0 + c1 - c0], in_=src.ap()[r0:r1, c0:c1]),
                 r=[B_w, trig], w=[b], dma="cw_" + name)

    def next_cast():
        if pending_casts:
            a = pending_casts.pop(0)
            cast_w(*a, defer=False)

    GR = {"rq": 0, "rk": 1, "rv": 2, "rg": 3, "dq": 4, "dk": 5, "dv": 6, "ga": 7, "gb": 8}
    for g in ("rk", "rv", "dv", "dk", "rq", "dq", "rg", "ga", "gb"):
        cast_w(g, w_in, wb_in[g], GR[g] * D, (GR[g] + 1) * D, D, d0=0, defer=(g not in ("rk", "rv", "dv", "dk")))
    cast_w("br", w_br, wb_br, 0, D, D, defer=True); cast_w("bd", w_bd, wb_bd, 0, D, D, defer=True)
    cast_w("out", w_out, wb_out, 0, D, D, defer=True)
    cast_w("g", w_g, wb_g, 0, DFF, D, defer=True); cast_w("u", w_u, wb_u, 0, DFF, D, defer=True)
    cast_w("d", w_d, wb_d, 0, D, DFF, defer=True)

    psrr = {}

    def next_ps(lo=0, hi=8):
        k = (lo, hi)
        i = psrr.get(k, lo); psrr[k] = lo + ((i - lo + 1) % (hi - lo))
        return i

    def norm_T(ar, src_rows_ap, np_, gi, uT, uTb, tcol, xt, xtb, ub, ubb, ssc, src_buf=None):
        src_buf = src_buf or B_x
        dma(xt[:np_, :], src_rows_ap, [src_buf], [xtb], "xt")
        P.op("dve", lambda E: E.memset(ssc[:, 0:1], 0.0), w=[B_ss])
        P.op("act", lambda E: E.activation(out=ub[:np_, :], in_=xt[:np_, :], func=AF.Square, accum_out=ssc[:np_, 0:1]),
             r=[xtb], w=[ubb, B_ss])
        P.op("act", lambda E: E.activation(out=ssc[:np_, 1:2], in_=ssc[:np_, 0:1], func=AF.Ln, scale=1.0 / D, bias=epsc[:np_, 0:1]),
             r=[B_ss, B_const], w=[B_ss])
        P.op("act", lambda E: E.activation(out=ssc[:np_, 2:3], in_=ssc[:np_, 1:2], func=AF.Exp, scale=-0.5), r=[B_ss], w=[B_ss])
        P.op("act", lambda E: E.activation(out=ub[:np_, :], in_=xt[:np_, :], func=AF.Copy, scale=ssc[:np_, 2:3]),
             r=[xtb, B_ss], w=[ubb])
        for k0 in range(0, KC, 8):
            nk = min(8, KC - k0)
            pi = next_ps()
            pv = PS[pi][:].bitcast(BF16)
            for j in range(nk):
                P.op("pe", lambda E, j=j, k0=k0, pv=pv: E.transpose(out=pv[:, j * 128:j * 128 + np_],
                                                                   in_=ub[:np_, (k0 + j) * 128:(k0 + j + 1) * 128],
                                                                   identity=ident[:np_, :np_]),
                     r=[ubb, B_const], w=[PB[pi]])
            src = pv[:, 0:nk * 128].rearrange("p (k t) -> p k t", t=128)[:, :, 0:np_]
            gsl = gT[:, gi, k0:k0 + nk]
            P.op("dve", lambda E, src=src, gsl=gsl, k0=k0, nk=nk: E.tensor_tensor(
                out=uT[:, k0:k0 + nk, tcol:tcol + np_], in0=src,
                in1=gsl.unsqueeze(2).to_broadcast([128, nk, np_]), op=ALU.mult),
                r=[PB[pi], B_const], w=[uTb])

    B_ss = Buf("ss")

    def proj(actT, actb, ntok, wdram, wbuf, K, c0, ncols, mode, evac, wt, wtb, WC):
        nkc = K // 128
        kgs = [(k, min(nkc, k + 32)) for k in range(0, nkc, 32)]
        wr = [0]
        for cb0 in range(0, ncols, WC):
            wc = min(WC, ncols - cb0)
            tiles = []

            def load(kg):
                i = wr[0] % len(wt); wr[0] += 1
                ka, kb = kgs[kg]
                for k8 in range(ka, kb, 8):
                    k9 = min(kb, k8 + 8)
                    dma(wt[i][:, k8 - ka:k9 - ka, 0:wc],
                        wdram.ap()[k8 * 128:k9 * 128, c0 + cb0:c0 + cb0 + wc].rearrange("(k p) c -> p k c", p=128),
                        [wbuf], [wtb[i]], "ld_" + wtb[i].name)
                return i
            if mode == "tm":
                ntt = (ntok + 127) // 128
                for th0 in range(0, ntt, 4):
                    tts = list(range(th0, min(ntt, th0 + 4)))
                    banks = {tt: next_ps() for tt in tts}
                    for kg, (ka, kb) in enumerate(kgs):
                        if th0 == 0 or len(kgs) > 1:
                            wi = load(kg)
                            if len(kgs) == 1:
                                tiles = [wi]
                        else:
                            wi = tiles[0]
                        for tt in tts:
                            n = min(128, ntok - tt * 128)
                            for kc in range(ka, kb):
                                P.op("pe", lambda E, tt=tt, n=n, kc=kc, ka=ka, wi=wi, b=banks[tt]: E.matmul(
                                    PS[b][:n, 0:wc], lhsT=actT[:, kc, tt * 128:tt * 128 + n], rhs=wt[wi][:, kc - ka, 0:wc],
                                    start=(kc == 0), stop=(kc == nkc - 1)), r=[actb, wtb[wi]], w=[PB[banks[tt]]])
                    for tt in tts:
                        n = min(128, ntok - tt * 128)
                        evac(banks[tt], PS[banks[tt]][:n, 0:wc], tt * 128, n, cb0, wc)
            else:
                wi = load(0)
                for cc in range(0, wc, 128):
                    for tg in range(0, ntok, 512):
                        n = min(512, ntok - tg)
                        b = next_ps()
                        for kc in range(nkc):
                            P.op("pe", lambda E, kc=kc, cc=cc, tg=tg, n=n, b=b, wi=wi: E.matmul(
                                PS[b][:, 0:n], lhsT=wt[wi][:, kc, cc:cc + 128], rhs=actT[:, kc, tg:tg + n],
                                start=(kc == 0), stop=(kc == nkc - 1)), r=[actb, wtb[wi]], w=[PB[b]])
                        evac(b, PS[b][:, 0:n], tg, n, cb0 + cc, 128)

    ar = Arena()
    uT = ar.a16(KC * TB).rearrange("p (k t) -> p k t", t=TB); uTb = Buf("uT")
    WC1 = 512
    wt = [ar.a16(32 * WC1).rearrange("p (k c) -> p k c", c=WC1) for _ in range(2)]
    wtb = [Buf("wt0"), Buf("wt1")]
    ub = ar.a16(D); ubb = Buf("ub")
    ost = [ar.a16(512) for _ in range(4)]; ostb = [Buf("ost%d" % i) for i in range(4)]
    xt = ar.a32(D); xtb = Buf("xt")
    osr = [0]
    B_scr = {n: Buf("scr_" + n) for n in ("rKtm", "rVtm", "dVtm", "dKT", "rKT", "rQT", "rGT", "dQT", "gaT", "gbT", "ZrT", "ZdT", "h1", "y")}

    def mk_evac(dst, dstb, tm, row0, scale=None, func=None):
        def evac(b, pap, t0, n, c0, ncol):
            i = osr[0] % 4; osr[0] += 1
            if tm:
                o = ost[i][:n, 0:ncol]
            else:
                o = ost[i][:, 0:n]
            if func is not None:
                P.op("act", lambda E: E.activation(out=o, in_=pap, func=func), r=[PB[b]], w=[ostb[i]])
            else:
                copy_op(ev_eng(), o, pap, [PB[b]], [ostb[i]], scale)
            if tm:
                dma(dst.ap()[row0 + t0:row0 + t0 + n, c0:c0 + ncol], o, [ostb[i]], [dstb], "st_" + dstb.name, eng="act")
            else:
                dma(dst.ap()[c0:c0 + ncol, row0 + t0:row0 + t0 + n], o, [ostb[i]], [dstb], "st_" + dstb.name, eng="act")
        return evac

    blocks = [(r0, TB) for r0 in range(0, S, TB)] + [(S, N_META)]

    def stage1_block(r0, ntok):
        own = r0 < T
        for t0 in range(0, ntok, 128):
            np_ = min(128, ntok - t0)
            src = (x_rot.ap()[r0 + t0:r0 + t0 + np_, :] if r0 < S else meta.ap()[0:np_, :])
            norm_T(ar, src, np_, 0, uT, uTb, t0, xt, xtb, ub, ubb, sm)
        def P_(g, dst, tm, scale=None, func=None, row0=r0):
            pcnt[0] += 1
            if pcnt[0] % 2 == 0 and len(pending_casts) > 6:
                next_cast()
            if own:
                next_cast()
            proj(uT, uTb, ntok, wb_in[g], WB[g], D, 0, D, "tm" if tm else "fm",
                 mk_evac(dst, B_scr[dst.name], tm, row0, scale, func), wt, wtb, WC1)
        P_("rk", rKtm, True, scale=256.0 ** -0.5)
        P_("rv", rVtm, True)
        P_("dv", dVtm, True)
        P_("dk", dKT, False)
        if own:
            P_("rk", rKT, False, scale=256.0 ** -0.5)
            P_("rq", rQT, False)
            P_("dq", dQT, False, scale=128.0 ** -0.5)
            P_("rg", rGT, False, func=AF.Silu)
            P_("ga", gaT, False, func=AF.Sigmoid)
            P_("gb", gbT, False, func=AF.Sigmoid)
    blocks = [b_ for b_ in blocks if b_[0] >= T] + [b_ for b_ in blocks if b_[0] < T]
    for (r0_, ntok_) in blocks:
        stage1_block(r0_, ntok_)
    while pending_casts:
        next_cast()
    P.barrier()

    ar = Arena()
    rK = ar.a16(NKT * 256).rearrange("p (t c) -> p t c", c=256); rKb = Buf("rK")
    rV = ar.a16(NKT * 256).rearrange("p (t c) -> p t c", c=256); rVb = Buf("rV")
    kT = ar.a16(2 * T).rearrange("p (k t) -> p k t", t=T); kTb = Buf("kT")
    qT = ar.a16(2 * T).rearrange("p (k t) -> p k t", t=T); qTb = Buf("qT")
    gTt = ar.a16(2 * T).rearrange("p (k t) -> p k t", t=T); gTb = Buf("gTt")
    qf = ar.a16(2 * T).rearrange("p (k t) -> p k t", t=T); qfb = Buf("qf")
    qb_ = ar.a16(2 * T).rearrange("p (k t) -> p k t", t=T); qbb = Buf("qb")
    Sbs = ar.a16(NT * 512).rearrange("p (c k v) -> p c k v", k=2, v=256); Sbsb = Buf("Sbs")
    Sf16 = ar.a16(512).rearrange("p (k v) -> p k v", v=256); Sf16b = Buf("Sf16")
    kw = [ar.a16(256) for _ in range(2)]; kwb = [Buf("kw0"), Buf("kw1")]
    aT = [ar.a16(128) for _ in range(2)]; aTb = [Buf("aT0"), Buf("aT1")]
    zo = [ar.a16(512) for _ in range(2)]; zob = [Buf("zo0"), Buf("zo1")]
    Mh = ar.a32(128); Mhb = Buf("Mh")
    Mt = ar.a32(128)
    QD = ar.a32(256).rearrange("p (a t) -> p a t", t=128); QDb = Buf("QD")
    wfb_ = ar.a32(2 * NKT).rearrange("p (a t) -> p a t", t=NKT); wfbb = Buf("wfb")
    cdist = ar.a32(2 * NKT).rearrange("p (a t) -> p a t", t=NKT)
    kdc = ar.a32(4); kdcb = Buf("kdc")
    Sm = ar.a32(1024).rearrange("p (d k v) -> p d k v", k=2, v=256); Smb = [Buf("Smf"), Buf("Smb")]
    osq = ar.a32(1024).rearrange("p (k t) -> p k t", t=512); osqb = Buf("osq")
    orr = ar.a32(512); orrb = Buf("orr")
    ld_raw = ar.a32(2 * H).rearrange("p (a h) -> p a h", h=H); ldb = Buf("ld")
    dma(cdist[:], c_dist.ap().rearrange("a p t -> p a t"), [B_x], [B_const], "c0")
    dma(ld_raw[:].rearrange("p a h -> p (a h)"), rdecay.ap().rearrange("a h -> (a h)").partition_broadcast(128), [B_x], [ldb], "c0")
    P.op("act", lambda E: E.activation(out=ld_raw[:], in_=ld_raw[:], func=AF.Exp), r=[ldb], w=[ldb])
    P.op("dve", lambda E: E.tensor_scalar(out=ld_raw[:], in0=ld_raw[:], scalar1=-1.0, scalar2=None, op0=ALU.mult),
         r=[ldb], w=[ldb])

    def tile_np(kt):
        return N_META if kt == NKT - 1 else 128

    def flat(a):
        return a.rearrange("p k v -> p (k v)")

    def ret_upd_state(d, c):
        i = (c + d) % 2
        P.op("dve", lambda E: E.tensor_scalar(out=kw[i][:, :], in0=rK[:, c, :], scalar1=kdc[:, d:d + 1], scalar2=None,
                                               op0=ALU.mult), r=[rKb, kdcb], w=[kwb[i]])
        b = next_ps(4, 8)
        for dk in range(2):
            P.op("pe", lambda E, dk=dk: E.matmul(PS[b][:, dk * 256:(dk + 1) * 256], lhsT=kw[i][:, dk * 128:(dk + 1) * 128],
                                                 rhs=rV[:, c, :], start=(dk == 0), stop=(dk == 1)),
                 r=[kwb[i], rVb], w=[PB[b]])
        P.op("dve", lambda E: E.scalar_tensor_tensor(
            out=flat(Sm[:, d, :, :]), in0=flat(Sm[:, d, :, :]),
            scalar=kdc[:, 2 + d:3 + d], in1=PS[b][:, :], op0=ALU.mult, op1=ALU.add),
            r=[Smb[d], PB[b], kdcb], w=[Smb[d]])

    def ret_chunk(h, c, c4, po):
        cs = slice(c * 128, (c + 1) * 128)
        P.op("act", lambda E: E.activation(out=flat(Sf16[:]), in_=flat(Sm[:, 0, :, :]), func=AF.Copy),
             r=[Smb[0]], w=[Sf16b])
        bs = next_ps(4, 8)
        for dk in range(2):
            P.op("pe", lambda E, dk=dk: E.matmul(PS[bs][:, 0:128], lhsT=kT[:, dk, cs], rhs=qT[:, dk, cs],
                                                 start=(dk == 0), stop=(dk == 1)), r=[kTb, qTb], w=[PB[bs]])
        ia = c % 2
        P.op("dve", lambda E: E.tensor_tensor(out=aT[ia][:], in0=PS[bs][:, 0:128], in1=Mh[:], op=ALU.mult),
             r=[PB[bs], Mhb], w=[aTb[ia]])
        for vh in range(2):
            vs = slice(vh * 128, (vh + 1) * 128)
            oc = slice((c - c4) * 128, (c - c4 + 1) * 128)
            seq = [(rV[:, c, vs], aT[ia][:], [rVb, aTb[ia]])]
            for dk in range(2):
                seq.append((Sf16[:, dk, vs], qf[:, dk, cs], [Sf16b, qfb]))
                seq.append((Sbs[:, c, dk, vs], qb_[:, dk, cs], [Sbsb, qbb]))
            ns = len(seq)
            for j, (l, r_, rb_) in enumerate(seq):
                P.op("pe", lambda E, l=l, r_=r_, j=j, vh=vh, oc=oc: E.matmul(
                    PS[po[vh]][:, oc], lhsT=l, rhs=r_, start=(j == 0), stop=(j == ns - 1)),
                    r=rb_, w=[PB[po[vh]]])
        if c < NT - 1:
            ret_upd_state(0, c)

    def ret_group(h, c4):
        po = [2, 3]
        c5 = min(NT, c4 + 4)
        for c in range(c4, c5):
            ret_chunk(h, c, c4, po)
        n = (c5 - c4) * 128
        ts_ = slice(c4 * 128, c4 * 128 + n)
        for vh in range(2):
            P.op("act", lambda E, vh=vh: E.activation(out=osq[:, vh, 0:n], in_=PS[po[vh]][:, 0:n], func=AF.Square),
                 r=[PB[po[vh]]], w=[osqb])
        br = next_ps(4, 8)
        for vh in range(2):
            P.op("pe", lambda E, vh=vh: E.matmul(PS[br][:, 0:n], lhsT=ones32[:], rhs=osq[:, vh, 0:n],
                                                 start=(vh == 0), stop=(vh == 1)), r=[osqb, B_const], w=[PB[br]])
        P.op("act", lambda E: E.activation(out=orr[:, 0:n], in_=PS[br][:, 0:n], func=AF.Ln, scale=1.0 / 256, bias=epsc[:, 0:1]),
             r=[PB[br], B_const], w=[orrb])
        P.op("act", lambda E: E.activation(out=orr[:, 0:n], in_=orr[:, 0:n], func=AF.Exp, scale=-0.5), r=[orrb], w=[orrb])
        for vh in range(2):
            P.op("dve", lambda E, vh=vh: E.tensor_tensor(out=osq[:, vh, 0:n], in0=PS[po[vh]][:, 0:n], in1=orr[:, 0:n],
                                                         op=ALU.mult), r=[PB[po[vh]], orrb], w=[osqb])
            P.op("dve", lambda E, vh=vh: E.tensor_tensor(out=zo[vh][:, 0:n], in0=osq[:, vh, 0:n], in1=gTt[:, vh, ts_],
                                                         op=ALU.mult), r=[osqb, gTb], w=[zob[vh]])
            dma(ZrT.ap()[h * 256 + vh * 128:h * 256 + (vh + 1) * 128, ts_], zo[vh][:, 0:n], [zob[vh]], [B_scr["ZrT"]], "st_ZrT", eng="act")

    def ret_head(h):
        hc = slice(h * 256, (h + 1) * 256)
        for t8 in range(0, NTALL, 16):
            t9 = min(NTALL, t8 + 16)
            dma(rK[:, t8:t9, :], rKtm.ap()[t8 * 128:t9 * 128, hc].rearrange("(t p) c -> p t c", p=128), [B_scr["rKtm"]], [rKb], "rK")
        dma(rK[:N_META, NTALL, :], rKtm.ap()[S:S + N_META, hc], [B_scr["rKtm"]], [rKb], "rK")
        for t8 in range(0, NTALL, 16):
            t9 = min(NTALL, t8 + 16)
            dma(rV[:, t8:t9, :], rVtm.ap()[t8 * 128:t9 * 128, hc].rearrange("(t p) c -> p t c", p=128), [B_scr["rVtm"]], [rVb], "rV")
        dma(rV[:N_META, NTALL, :], rVtm.ap()[S:S + N_META, hc], [B_scr["rVtm"]], [rVb], "rV")
        for (dst, dstb, srcT, nm) in ((kT, kTb, rKT, "rKT"), (qT, qTb, rQT, "rQT"), (gTt, gTb, rGT, "rGT")):
            dma(dst[:], srcT.ap()[hc, :].rearrange("(k p) t -> p k t", p=128), [B_scr[nm]], [dstb], "r" + nm)
        for d in range(2):
            lg = ld_raw[:, d, h:h + 1]
            P.op("act", lambda E, d=d, lg=lg: E.activation(out=wfb_[:, d, :], in_=cdist[:, d, :], func=AF.Exp, scale=lg),
                 r=[B_const, ldb], w=[wfbb])
            P.op("act", lambda E, d=d, lg=lg: E.activation(out=QD[:, d, :], in_=csq[:, 2 + d, :], func=AF.Exp, scale=lg),
                 r=[B_const, ldb], w=[QDb])
            P.op("act", lambda E, d=d, lg=lg: E.activation(out=kdc[:, d:d + 1], in_=ckd[:, d:d + 1], func=AF.Exp, scale=lg),
                 r=[B_const, ldb], w=[kdcb])
            P.op("act", lambda E, d=d, lg=lg: E.activation(out=kdc[:, 2 + d:3 + d], in_=lg, func=AF.Exp, scale=128.0),
                 r=[B_const, ldb], w=[kdcb])
        P.op("act", lambda E: E.activation(out=Mh[:], in_=csq[:, 0, :], func=AF.Exp, scale=ld_raw[:, 0, h:h + 1]),
             r=[B_const, ldb], w=[Mhb])
        P.op("act", lambda E: E.activation(out=Mt[:], in_=csq[:, 1, :], func=AF.Exp, scale=ld_raw[:, 1, h:h + 1]),
             r=[B_const, ldb, Mhb], w=[Mhb])
        P.op("dve", lambda E: E.tensor_tensor(out=Mh[:], in0=Mh[:], in1=Mt[:], op=ALU.add), r=[Mhb], w=[Mhb])
        for d, (dst, dstb) in enumerate(((qf, qfb), (qb_, qbb))):
            P.op("dve", lambda E, d=d, dst=dst: E.tensor_tensor(
                out=dst[:].rearrange("p k (c i) -> p (k c) i", i=128),
                in0=qT[:].rearrange("p k (c i) -> p (k c) i", i=128),
                in1=QD[:, d, :].unsqueeze(1).to_broadcast([128, 2 * NT, 128]), op=ALU.mult),
                r=[qTb, QDb], w=[dstb])
        pin = [0, 1]
        for d in range(2):
            for kt in range(NT, NKT):
                np_ = tile_np(kt)
                i = (kt + d) % 2
                P.op("dve", lambda E, i=i, kt=kt, d=d, np_=np_: E.tensor_scalar(
                    out=kw[i][:np_, :], in0=rK[:np_, kt, :], scalar1=wfb_[:np_, d, kt:kt + 1], scalar2=None, op0=ALU.mult),
                    r=[rKb, wfbb], w=[kwb[i]])
                for dk in range(2):
                    P.op("pe", lambda E, i=i, kt=kt, d=d, dk=dk, np_=np_: E.matmul(
                        PS[pin[d]][:, dk * 256:(dk + 1) * 256], lhsT=kw[i][:np_, dk * 128:(dk + 1) * 128],
                        rhs=rV[:np_, kt, :], start=(kt == NT and dk == 0), stop=(kt == NKT - 1 and dk == 1),
                        skip_group_check=True),
                        r=[kwb[i], rVb], w=[PB[pin[d]]])
            P.op("dve", lambda E, d=d: E.tensor_copy(out=flat(Sm[:, d, :, :]), in_=PS[pin[d]][:, :]),
                 r=[PB[pin[d]]], w=[Smb[d]])
        for c in range(NT - 1, -1, -1):
            P.op("act", lambda E, c=c: E.activation(out=flat(Sbs[:, c, :, :]), in_=flat(Sm[:, 1, :, :]), func=AF.Copy),
                 r=[Smb[1]], w=[Sbsb])
            if c > 0:
                ret_upd_state(1, c)
        for c4 in range(0, NT, 4):
            ret_group(h, c4)

    for h in range(H):
        ret_head(h)
    P.barrier()

    ar = Arena()
    k1 = ar.a16(NKT * 128); k2 = ar.a16(NKT * 128); kkb = Buf("kk")
    vv = ar.a16(NKT * 256).rearrange("p (t c) -> p t c", c=256); vvb = Buf("vv")
    sg = [ar.a16(NKT * 128) for _ in range(2)]; sgb = [Buf("sg0"), Buf("sg1")]
    qq = [ar.a16(1024).rearrange("p (m t) -> p m t", t=512) for _ in range(2)]; qqb = [Buf("qq0"), Buf("qq1")]
    rbt = ar.a16(512); rbb = Buf("rb")
    ee = [ar.a16(512) for _ in range(4)]; eeb = [Buf("ee%d" % i) for i in range(4)]
    zd = [ar.a16(512) for _ in range(2)]; zdb = [Buf("zd0"), Buf("zd1")]
    babs = ar.a32(4 * 512).rearrange("p (a t) -> p a t", t=512)
    absd = ar.a32(NQB * NKT).rearrange("p (a t) -> p a t", t=NKT)
    bcol = ar.a32(NKT); bcolb = Buf("bcol")
    zacc = [ar.a32(512) for _ in range(2)]; zaccb = [Buf("zacc0"), Buf("zacc1")]
    sp_ = [ar.a32(512) for _ in range(2)]; spb = [Buf("sp0"), Buf("sp1")]
    o32 = ar.a32(1024).rearrange("p (k t) -> p k t", t=512); o32b = Buf("o32")
    t32 = ar.a32(1024).rearrange("p (k t) -> p k t", t=512); t32b = Buf("t32")
    rz = ar.a32(1024).rearrange("p (k t) -> p k t", t=512); rzb = Buf("rz")
    lam = ar.a32(8); lamb = Buf("lam")
    sgc = ar.a32(2); sgcb = Buf("sgc")
    lraw = ar.a32(512)
    dma(babs[:], c_babs.ap().rearrange("a p t -> p a t"), [B_x], [B_const], "c0")
    dma(absd[:], c_absd.ap().rearrange("a p t -> p a t"), [B_x], [B_const], "c0")
    dma(sgc[:], subln.ap().rearrange("(k p) -> p k", p=128), [B_x], [sgcb], "c0")
    dma(lraw[:], dlam.ap().rearrange("a f -> (a f)").partition_broadcast(128), [B_x], [lamb], "c0")
    P.op("dve", lambda E: E.tensor_tensor(out=lraw[:, 0:128], in0=lraw[:, 0:128], in1=lraw[:, 128:256], op=ALU.mult), r=[lamb], w=[lamb])
    P.op("dve", lambda E: E.tensor_tensor(out=lraw[:, 256:384], in0=lraw[:, 256:384], in1=lraw[:, 384:512], op=ALU.mult), r=[lamb], w=[lamb])
    P.op("dve", lambda E: E.reduce_sum(out=lam[:, 0:1], in_=lraw[:, 0:128], axis=mybir.AxisListType.X), r=[lamb], w=[lamb])
    P.op("dve", lambda E: E.reduce_sum(out=lam[:, 1:2], in_=lraw[:, 256:384], axis=mybir.AxisListType.X), r=[lamb], w=[lamb])
    P.op("act", lambda E: E.activation(out=lam[:, 2:4], in_=lam[:, 0:2], func=AF.Exp), r=[lamb], w=[lamb])
    P.op("dve", lambda E: E.tensor_tensor(out=lam[:, 4:5], in0=lam[:, 3:4], in1=lam[:, 2:3], op=ALU.subtract), r=[lamb], w=[lamb])
    P.op("dve", lambda E: E.tensor_scalar(out=lam[:, 4:5], in0=lam[:, 4:5], scalar1=-cfg.LAM_INIT, scalar2=None, op0=ALU.add), r=[lamb], w=[lamb])
    P.op("dve", lambda E: E.tensor_scalar(out=sgc[:], in0=sgc[:], scalar1=1.0 - cfg.LAM_INIT, scalar2=None, op0=ALU.mult),
         r=[sgcb], w=[sgcb])
    SL = slopes(cfg)

    def diff_S(h, qb, qi, kt, pos):
        np_ = tile_np(kt)
        diag = (qb * 4 <= kt < qb * 4 + 4)
        for m, kk in enumerate((k1, k2)):
            bS = next_ps(4, 8)
            ie = (2 * pos + m) % 4
            P.op("pe", lambda E, kk=kk, bS=bS, m=m: E.matmul(
                PS[bS][:np_, :], lhsT=kk[:, kt * 128:kt * 128 + np_], rhs=qq[qi][:, m, :], start=True, stop=diag),
                r=[kkb, qqb[qi]], w=[PB[bS]])
            if not diag:
                P.op("pe", lambda E, bS=bS: E.matmul(
                    PS[bS][:np_, :], lhsT=sg[qi][0:2, kt * 128:kt * 128 + np_], rhs=rbt[0:2, :], start=False, stop=True),
                    r=[sgb[qi], rbb], w=[PB[bS]])
                P.op("act", lambda E, bS=bS, ie=ie: E.activation(
                    out=ee[ie][:np_, :], in_=PS[bS][:np_, :], func=AF.Exp, bias=bcol[:np_, kt:kt + 1]),
                    r=[PB[bS], bcolb], w=[eeb[ie]])
            else:
                ci = kt - qb * 4
                P.op("dve", lambda E, bS=bS, m=m: E.scalar_tensor_tensor(
                    out=sp_[m][:], in0=babs[:, ci, :], scalar=-SL[h], in1=PS[bS][:, :], op0=ALU.mult, op1=ALU.add),
                    r=[PB[bS], B_const], w=[spb[m]])
                P.op("act", lambda E, m=m, ie=ie: E.activation(out=ee[ie][:], in_=sp_[m][:], func=AF.Exp),
                     r=[spb[m]], w=[eeb[ie]])

    def diff_AV(h, qb, qi, kt, pO, pos, first, last):
        np_ = tile_np(kt)
        for m in range(2):
            ie = (2 * pos + m) % 4
            if first:
                P.op("dve", lambda E, m=m, ie=ie: E.tensor_copy(out=zacc[m][:], in_=ee[ie][:]), r=[eeb[ie]], w=[zaccb[m]])
            else:
                P.op("dve", lambda E, m=m, ie=ie: E.tensor_tensor(out=zacc[m][:np_, :], in0=zacc[m][:np_, :],
                                                                 in1=ee[ie][:np_, :], op=ALU.add),
                     r=[eeb[ie], zaccb[m]], w=[zaccb[m]])
            for vh in range(2):
                P.op("pe", lambda E, vh=vh, m=m, ie=ie: E.matmul(
                    PS[pO[m][vh]][:, :], lhsT=vv[:np_, kt, vh * 128:(vh + 1) * 128], rhs=ee[ie][:np_, :],
                    start=first, stop=last), r=[vvb, eeb[ie]], w=[PB[pO[m][vh]]])

    def key_tiles(h):
        R = 60.0 / SL[h]
        r = int(math.ceil(R / 128.0))
        nother = NTALL - NT
        if 2 * r >= nother:
            oth = list(range(NT, NTALL))
        else:
            oth = list(range(NT, NT + r)) + list(range(NTALL - r, NTALL))
        return list(range(NT)) + oth + [NTALL]

    def diff_iter(h, qb, qi):
        q0 = qb * 512
        dma(qq[qi][:], dQT.ap()[h * 256:(h + 1) * 256, q0:q0 + 512].rearrange("(m p) t -> p m t", p=128),
            [B_scr["dQT"]], [qqb[qi]], "qq%d" % qi)
        dma(sg[qi][0:2, :], c_sgn.ap()[qb], [B_x], [sgb[qi]], "sg%d" % qi)
        P.op("dve", lambda E: E.tensor_scalar(out=bcol[:], in0=absd[:, qb, :], scalar1=-SL[h], scalar2=None,
                                               op0=ALU.mult), r=[B_const], w=[bcolb])
        pO = [[0, 1], [2, 3]]
        tl = key_tiles(h)
        for i in range(len(tl) + 1):
            if i < len(tl):
                diff_S(h, qb, qi, tl[i], i)
            if i >= 1:
                diff_AV(h, qb, qi, tl[i - 1], pO, i - 1, i - 1 == 0, i - 1 == len(tl) - 1)
        for m in range(2):
            bz = next_ps(4, 8)
            P.op("pe", lambda E, m=m, bz=bz: E.matmul(PS[bz][:, :], lhsT=ones32[:], rhs=zacc[m][:], start=True, stop=True),
                 r=[zaccb[m], B_const], w=[PB[bz]])
            P.op("dve", lambda E, m=m, bz=bz: E.reciprocal(out=rz[:, m, :], in_=PS[bz][:, :]), r=[PB[bz]], w=[rzb])
        for vh in range(2):
            P.op("dve", lambda E, vh=vh: E.tensor_tensor(out=t32[:, vh, :], in0=PS[pO[1][vh]][:, :], in1=rz[:, 1, :], op=ALU.mult),
                 r=[PB[pO[1][vh]], rzb], w=[t32b])
            P.op("dve", lambda E, vh=vh: E.tensor_tensor(out=o32[:, vh, :], in0=PS[pO[0][vh]][:, :], in1=rz[:, 0, :], op=ALU.mult),
                 r=[PB[pO[0][vh]], rzb], w=[o32b])
            P.op("dve", lambda E, vh=vh: E.scalar_tensor_tensor(out=o32[:, vh, :], in0=t32[:, vh, :], scalar=lam[:, 4:5],
                                                                in1=o32[:, vh, :], op0=ALU.mult, op1=ALU.add),
                 r=[t32b, o32b, lamb], w=[o32b])
            P.op("act", lambda E, vh=vh: E.activation(out=t32[:, vh, :], in_=o32[:, vh, :], func=AF.Square), r=[o32b, t32b], w=[t32b])
        br = next_ps(4, 8)
        for vh in range(2):
            P.op("pe", lambda E, vh=vh: E.matmul(PS[br][:, :], lhsT=ones32[:], rhs=t32[:, vh, :], start=(vh == 0), stop=(vh == 1)),
                 r=[t32b, B_const], w=[PB[br]])
        P.op("act", lambda E: E.activation(out=rz[:, 0, :], in_=PS[br][:, :], func=AF.Ln, scale=1.0 / 256, bias=epsc[:, 0:1]),
             r=[PB[br], rzb, B_const], w=[rzb])
        P.op("act", lambda E: E.activation(out=rz[:, 0, :], in_=rz[:, 0, :], func=AF.Exp, scale=-0.5), r=[rzb], w=[rzb])
        for vh in range(2):
            P.op("dve", lambda E, vh=vh: E.scalar_tensor_tensor(out=zd[vh][:], in0=o32[:, vh, :], scalar=sgc[:, vh:vh + 1],
                                                                in1=rz[:, 0, :], op0=ALU.mult, op1=ALU.mult),
                 r=[o32b, rzb, sgcb], w=[zdb[vh]])
            dma(ZdT.ap()[h * 256 + vh * 128:h * 256 + (vh + 1) * 128, q0:q0 + 512], zd[vh][:], [zdb[vh]], [B_scr["ZdT"]], "st_ZdT", eng="act")

    def diff_head(h, it0):
        dma(k1[:, 0:S + N_META], dKT.ap()[h * 256:h * 256 + 128, 0:S + N_META], [B_scr["dKT"]], [kkb], "kk")
        dma(k2[:, 0:S + N_META], dKT.ap()[h * 256 + 128:h * 256 + 256, 0:S + N_META], [B_scr["dKT"]], [kkb], "kk")
        for t8 in range(0, NTALL, 16):
            t9 = min(NTALL, t8 + 16)
            dma(vv[:, t8:t9, :], dVtm.ap()[t8 * 128:t9 * 128, h * 256:(h + 1) * 256].rearrange("(t p) c -> p t c", p=128), [B_scr["dVtm"]], [vvb], "vv")
        dma(vv[:N_META, NTALL, :], dVtm.ap()[S:S + N_META, h * 256:(h + 1) * 256], [B_scr["dVtm"]], [vvb], "vv")
        dma(rbt[0:2, :], c_rb.ap()[h], [B_x], [rbb], "rb")
        for qb in range(NQB):
            diff_iter(h, qb, (it0 + qb) % 2)

    for h in range(H):
        diff_head(h, h * NQB)
    P.barrier()

    ar = Arena()
    TF = 512
    WC4 = 512
    zT = ar.a16(KC * TF).rearrange("p (k t) -> p k t", t=TF); zTb = Buf("zT")
    mxbase = ar.a16(max(KC, 32) * TF).rearrange("p (k t) -> p k t", t=TF); mxTb = Buf("mxT")
    mxT = mxbase[:, 0:KC, :]
    aTf = mxbase; aTfb = mxTb
    wt4 = [ar.a16(32 * WC4).rearrange("p (k c) -> p k c", c=WC4) for _ in range(2)]
    wt4b = [Buf("w40"), Buf("w41")]
    gg = [ar.a16(TF) for _ in range(4)]; ggb = [Buf("gg%d" % i) for i in range(4)]
    ub4 = ar.a16(D); ub4b = Buf("ub4")
    gsb = ar.a16(TF); gsbb = Buf("gsb")
    xt4 = ar.a32(D); xt4b = Buf("xt4")
    gfin = ar.a32(D)
    hst = [ar.a32(WC4) for _ in range(3)]; hstb = [Buf("hst%d" % i) for i in range(3)]
    hrow = [ar.a32(WC4) for _ in range(3)]; hrowb = [Buf("hrow%d" % i) for i in range(3)]
    B_h1 = [Buf("h1_%d" % i) for i in range(4)]
    rr = [0]
    dma(gfin[:], gvec.ap()[2].partition_broadcast(128), [B_x], [B_const], "c0")

    def s4_mix(tb, pas, Zs, nm, wdr, wn, gsrc, gn):
        for k8 in range(0, KC, 8):
            k9 = min(KC, k8 + 8)
            dma(zT[:, k8:k9, :], Zs.ap()[k8 * 128:k9 * 128, tb:tb + TF].rearrange("(k p) t -> p k t", p=128), [B_scr[nm]], [zTb], "zT")

        def ev_mix(b, pap, t0, n, c0, ncol):
            i = rr[0] % 4; rr[0] += 1
            kc = c0 // 128
            dma(gg[i][:, 0:n], gsrc.ap()[c0:c0 + 128, tb + t0:tb + t0 + n], [B_scr[gn]], [ggb[i]], "gg%d" % i)
            if pas == 0:
                P.op("dve", lambda E: E.tensor_tensor(out=mxT[:, kc, t0:t0 + n], in0=pap, in1=gg[i][:, 0:n], op=ALU.mult),
                     r=[PB[b], ggb[i]], w=[mxTb])
            else:
                P.op("dve", lambda E: E.tensor_tensor(out=gsb[:, 0:n], in0=pap, in1=gg[i][:, 0:n], op=ALU.mult),
                     r=[PB[b], ggb[i]], w=[gsbb])
                P.op("dve", lambda E: E.tensor_tensor(out=mxT[:, kc, t0:t0 + n], in0=mxT[:, kc, t0:t0 + n], in1=gsb[:, 0:n], op=ALU.add),
                     r=[gsbb, mxTb], w=[mxTb])
        proj(zT, zTb, TF, wdr, WB[wn], D, 0, D, "fm", ev_mix, wt4, wt4b, WC4)

    def s4_seg(tb, f0, f1):
        def ev_gate(b, pap, t0, n, c0, ncol):
            kc = (c0 // 128)
            P.op("act", lambda E: E.activation(out=aTf[:, kc, t0:t0 + n], in_=pap, func=AF.Silu), r=[PB[b]], w=[aTfb])

        def ev_up(b, pap, t0, n, c0, ncol):
            kc = (c0 // 128)
            P.op("dve", lambda E: E.tensor_tensor(out=aTf[:, kc, t0:t0 + n], in0=pap, in1=aTf[:, kc, t0:t0 + n], op=ALU.mult),
                 r=[PB[b], aTfb], w=[aTfb])

        class _W:
            def __init__(s, a):
                s.a = a

            def ap(s):
                return s.a
        proj(zT, zTb, TF, _W(wb_g.ap()[:, f0 * 128:f1 * 128]), WB["g"], D, 0, (f1 - f0) * 128, "fm", ev_gate, wt4, wt4b, WC4)
        proj(zT, zTb, TF, _W(wb_u.ap()[:, f0 * 128:f1 * 128]), WB["u"], D, 0, (f1 - f0) * 128, "fm", ev_up, wt4, wt4b, WC4)

        def ev_dn(b, pap, t0, n, c0, ncol):
            i = rr[0] % 3; rr[0] += 1
            hb = B_h1[t0 // 128]
            dma(hrow[i][:n, 0:ncol], h1.ap()[tb + t0:tb + t0 + n, c0:c0 + ncol], [hb], [hrowb[i]], "hrow%d" % i)
            P.op("dve", lambda E: E.tensor_tensor(out=hst[i][:n, 0:ncol], in0=pap, in1=hrow[i][:n, 0:ncol], op=ALU.add),
                 r=[PB[b], hrowb[i]], w=[hstb[i]])
            dma(h1.ap()[tb + t0:tb + t0 + n, c0:c0 + ncol], hst[i][:n, 0:ncol], [hstb[i]], [hb], "st_h1_%d" % (t0 // 128), eng="act")
        proj(aTf, aTfb, TF, _W(wb_d.ap()[f0 * 128:f1 * 128, :]), WB["d"], (f1 - f0) * 128, 0, D, "tm", ev_dn, wt4, wt4b, WC4)

    def s4_block(tb):
        s4_mix(tb, 0, ZrT, "ZrT", wb_br, "br", gaT, "gaT")
        s4_mix(tb, 1, ZdT, "ZdT", wb_bd, "bd", gbT, "gbT")

        def ev_h1(b, pap, t0, n, c0, ncol):
            i = rr[0] % 3; rr[0] += 1
            hb = B_h1[t0 // 128]
            dma(hrow[i][:n, 0:ncol], x_rot.ap()[tb + t0:tb + t0 + n, c0:c0 + ncol], [B_x], [hrowb[i]], "hrow%d" % i)
            P.op("dve", lambda E: E.tensor_tensor(out=hst[i][:n, 0:ncol], in0=pap, in1=hrow[i][:n, 0:ncol], op=ALU.add),
                 r=[PB[b], hrowb[i]], w=[hstb[i]])
            dma(h1.ap()[tb + t0:tb + t0 + n, c0:c0 + ncol], hst[i][:n, 0:ncol], [hstb[i]], [hb], "st_h1_%d" % (t0 // 128), eng="act")
        proj(mxT, mxTb, TF, wb_out, WB["out"], D, 0, D, "tm", ev_h1, wt4, wt4b, WC4)
        for t0 in range(0, TF, 128):
            norm_T(ar, h1.ap()[tb + t0:tb + t0 + 128, :], 128, 1, zT, zTb, t0, xt4, xt4b, ub4, ub4b, sm, src_buf=B_h1[t0 // 128])
        for f0 in range(0, FC, 32):
            s4_seg(tb, f0, min(FC, f0 + 32))
        for t0 in range(0, TF, 128):
            s4_fin(tb, t0)

    def s4_fin(tb, t0):
        dma(xt4[:, :], h1.ap()[tb + t0:tb + t0 + 128, :], [B_h1[t0 // 128]], [xt4b], "xt")
        P.op("dve", lambda E: E.memset(sm[:, 0:1], 0.0), w=[B_ss])
        P.op("act", lambda E: E.activation(out=ub4[:, :], in_=xt4[:, :], func=AF.Square, accum_out=sm[:, 0:1]),
             r=[xt4b], w=[ub4b, B_ss])
        P.op("act", lambda E: E.activation(out=sm[:, 1:2], in_=sm[:, 0:1], func=AF.Ln, scale=1.0 / D, bias=epsc[:, 0:1]),
             r=[B_ss, B_const], w=[B_ss])
        P.op("act", lambda E: E.activation(out=sm[:, 2:3], in_=sm[:, 1:2], func=AF.Exp, scale=-0.5), r=[B_ss], w=[B_ss])
        P.op("dve", lambda E: E.scalar_tensor_tensor(out=xt4[:, :], in0=xt4[:, :], scalar=sm[:, 2:3], in1=gfin[:, :],
                                                     op0=ALU.mult, op1=ALU.mult), r=[xt4b, B_ss, B_const], w=[xt4b])
        dma(y.ap()[tb + t0:tb + t0 + 128, :], xt4[:, :], [xt4b], [B_scr["y"]], "st_y", eng="act")

    for tb in range(0, T, TF):
        s4_block(tb)

    with nc.allow_non_contiguous_dma(reason="small strided constant/layout loads"):
        with nc.Block() as block:
            sems = P.emit(nc, block)
        sems.close()
    stack.close()
    return nc


def host_consts(cfg, qt):
    S, T, H, NKT, NQB, NTALL = cfg.S, cfg.T, cfg.H, cfg.NKT, cfg.NQB, cfg.NTALL
    own0 = qt * T
    perm = (own0 + np.arange(S)) % S
    kpos = np.full((NKT, 128), 1e9, np.float64)
    kpos[:NTALL] = (N_META + perm).reshape(NTALL, 128)
    kpos[NTALL, :N_META] = np.arange(N_META)
    absd = np.zeros((NQB, 128, NKT), np.float32)
    sgn = np.zeros((NQB, 2, NKT * 128), np.float32)
    for qb in range(NQB):
        qc = N_META + own0 + qb * 512 + 255.5
        absd[qb] = np.abs(kpos - qc).T
        sg = np.where(kpos < qc, 1.0, -1.0).reshape(-1)
        sgn[qb, 0] = sg; sgn[qb, 1] = sg
    absd = np.minimum(absd, 1e6).astype(np.float32)
    sl = np.array(slopes(cfg), np.float64)
    qrel = np.arange(512) - 255.5
    val = (-sl[:, None] * qrel[None, :]).astype(np.float32)
    hi = val.astype(ml_dtypes.bfloat16)
    lo = (val - hi.astype(np.float32)).astype(ml_dtypes.bfloat16)
    rb = np.stack([hi, lo], axis=1)
    p = np.arange(128)[:, None]; f = np.arange(512)[None, :]
    babs = np.stack([np.abs(ci * 128 + p - f) for ci in range(4)]).astype(np.float32)
    p0 = N_META + own0; p1 = p0 + T
    distf = np.where(kpos < p0, p0 - 1 - kpos, BIG)
    distb = np.where((kpos >= p1) & (kpos < 1e8), kpos - p1, BIG)
    dist = np.stack([distf.T, distb.T]).astype(np.float32)
    s_ = np.arange(128)[:, None]; t_ = np.arange(128)[None, :]
    dpos = np.where(t_ >= s_, t_ - s_, BIG)
    dneg = np.where(s_ > t_, s_ - t_, BIG)
    qdf = np.broadcast_to(t_ + 1.0, (128, 128))
    qdb = np.broadcast_to(128.0 - t_, (128, 128))
    sq = np.stack([dpos, dneg, qdf, qdb]).astype(np.float32)
    kd = np.stack([127.0 - np.arange(128), np.arange(128) * 1.0], axis=1).astype(np.float32)
    return perm, dict(c_absd=absd, c_sgn=sgn.astype(ml_dtypes.bfloat16), c_rb=rb, c_babs=babs, c_dist=dist,
                      c_sq=sq, c_kd=kd, c_ident=np.eye(128, dtype=np.float32).astype(ml_dtypes.bfloat16))


_NC_CACHE = {}


def run(cfg, inp):
    key = (cfg.D, cfg.S)
    if key not in _NC_CACHE:
        _NC_CACHE[key] = build(cfg)
    nc = _NC_CACHE[key]
    f32 = lambda a: np.ascontiguousarray(np.asarray(a, dtype=np.float32))
    x = f32(inp["x"])
    shared = dict(
        meta=f32(inp["meta_tokens"]),
        gvec=np.stack([f32(inp["norm_mix_g"])[0], f32(inp["norm_ffn_g"])[0], f32(inp["norm_final_g"])]),
        w_in=f32(inp["w_in"])[0], w_br=f32(inp["w_branch_ret"])[0], w_bd=f32(inp["w_branch_diff"])[0],
        w_out=f32(inp["w_out"])[0], w_g=f32(inp["w_ffn_gate"])[0], w_u=f32(inp["w_ffn_up"])[0],
        w_d=f32(inp["w_ffn_down"])[0], rdecay=f32(inp["ret_log_decay"])[0], dlam=f32(inp["diff_lambda"])[0],
        subln=f32(inp["diff_subln_g"])[0])
    in_maps = []
    for c in range(8):
        b, qt = c // 4, c % 4
        perm, consts = host_consts(cfg, qt)
        m = dict(shared)
        m["x_rot"] = np.ascontiguousarray(x[b][perm])
        m.update(consts)
        in_maps.append(m)
    res = run_bass_kernel_spmd(nc, in_maps, core_ids=list(range(8)))
    out = np.zeros((2, cfg.S, cfg.D), np.float32)
    for c in range(8):
        b, qt = c // 4, c % 4
        out[b, qt * cfg.T:(qt + 1) * cfg.T] = res.results[c]["y"]
    return out


def kernel(**inputs):
    return run(Cfg(4096, 8192), inputs)
```

```python
import math, contextlib
import numpy as np
import ml_dtypes
import concourse.bass as bass
import concourse.mybir as mybir
from concourse.bass_utils import run_bass_kernel_spmd

F32 = mybir.dt.float32
BF16 = mybir.dt.bfloat16
AF = mybir.ActivationFunctionType
ALU = mybir.AluOpType
BIG = 1.0e30
EPS = 1e-6
N_META = 16


class Cfg:
    def __init__(s, D=4096, S=8192):
        s.D, s.S = D, S
        s.T = S // 4
        s.H = D // 256
        s.DIN = 9 * D
        s.DFF = ((8 * D + 3 * 256 - 1) // (3 * 256)) * 256
        s.KC = D // 128
        s.FC = s.DFF // 128
        s.NT = s.T // 128
        s.NTALL = S // 128
        s.NKT = s.NTALL + 1
        s.NQB = s.T // 512
        s.TB = min(1024, s.T)
        s.LAM_INIT = 0.8 - 0.6 * math.exp(-0.3 * 0)


class Buf:
    __slots__ = ("name",)

    def __init__(s, name):
        s.name = name


class Prog:
    COMPUTE = ("pe", "act", "dve", "pool")

    def __init__(s):
        s.ops = []
        s.lastw = {}
        s.rd_eng = {}
        s.rd_dma = {}
        s.dmacnt = {}
        s.bar = None
        s.bar_done = set()

    def op(s, eng, fn, r=(), w=(), dma=None):
        i = len(s.ops)
        deps = set()
        for b in tuple(r) + tuple(w):
            lw = s.lastw.get(b)
            if lw is not None:
                deps.add(lw)
        for b in w:
            for x in s.rd_eng.get(b, {}).values():
                deps.add(x)
            for x in s.rd_dma.get(b, ()):
                deps.add(x)
        o = dict(eng=eng, fn=fn, deps=deps, dma=dma, mark=False, bar=None)
        if s.bar is not None and eng not in s.bar_done:
            o["bar"] = s.bar
            s.bar_done.add(eng)
        if dma is not None:
            s.dmacnt[dma] = s.dmacnt.get(dma, 0) + 1
            o["dval"] = 16 * s.dmacnt[dma]
        s.ops.append(o)
        for b in w:
            s.lastw[b] = i
            s.rd_eng[b] = {}
            s.rd_dma[b] = []
        for b in r:
            if dma is not None:
                s.rd_dma.setdefault(b, []).append(i)
            else:
                s.rd_eng.setdefault(b, {})[eng] = i
        return i

    def barrier(s):
        last = {}
        for i, o in enumerate(s.ops):
            if o["dma"] is None:
                last[o["eng"]] = i
        s.bar = (dict(last), dict(s.dmacnt))
        s.bar_done = set()

    def emit(s, nc, block):
        ops = s.ops
        for o in ops:
            for d in o["deps"]:
                if ops[d]["dma"] is None:
                    ops[d]["mark"] = True
            if o["bar"] is not None:
                for e, d in o["bar"][0].items():
                    ops[d]["mark"] = True
        cnt = {e: 0 for e in ("pe", "act", "dve", "pool", "sp")}
        for o in ops:
            if o["dma"] is None:
                if o["mark"]:
                    cnt[o["eng"]] += 1
                o["sval"] = cnt[o["eng"]]
        stack = contextlib.ExitStack()
        esem = {e: stack.enter_context(nc.semaphore("s_" + e)) for e in cnt}
        dsem = {k: stack.enter_context(nc.semaphore("d_" + str(k))) for k in s.dmacnt}
        byeng = {e: [] for e in cnt}
        for o in ops:
            byeng[o["eng"]].append(o)

        def run(eng_name, E):
            known = {}

            def need(sem, val):
                if known.get(sem.name if hasattr(sem, "name") else id(sem), 0) < val:
                    E.wait_ge(sem, val)
                    known[sem.name if hasattr(sem, "name") else id(sem)] = val

            for o in byeng[eng_name]:
                if o["bar"] is not None:
                    lastc, dcnt = o["bar"]
                    for e, d in lastc.items():
                        if e != eng_name or e != "pe":
                            need(esem[e], ops[d]["sval"])
                    for k, c in dcnt.items():
                        need(dsem[k], 16 * c)
                for d in sorted(o["deps"]):
                    do = ops[d]
                    if do["dma"] is not None:
                        need(dsem[do["dma"]], do["dval"])
                    else:
                        if do["eng"] == "pe" and eng_name == "pe":
                            continue
                        need(esem[do["eng"]], do["sval"])
                ins = o["fn"](E)
                if o["dma"] is not None:
                    ins.then_inc(dsem[o["dma"]], 16)
                elif o["mark"]:
                    ins.then_inc(esem[eng_name], 1)
            if eng_name == "sp":
                for k, c in s.dmacnt.items():
                    need(dsem[k], 16 * c)

        @block.tensor
        def _(E):
            run("pe", E)

        @block.scalar
        def _(E):
            run("act", E)

        @block.vector
        def _(E):
            run("dve", E)

        @block.gpsimd
        def _(E):
            run("pool", E)

        @block.sync
        def _(E):
            run("sp", E)

        return stack


def slopes(cfg):
    return [2.0 ** (-8.0 * (h + 1) / cfg.H) for h in range(cfg.H)]


def build(cfg):
    D, S, T, H, KC, DFF, FC = cfg.D, cfg.S, cfg.T, cfg.H, cfg.KC, cfg.DFF, cfg.FC
    NKT, NTALL, NT, NQB, TB = cfg.NKT, cfg.NTALL, cfg.NT, cfg.NQB, cfg.TB
    nc = bass.Bass("TRN2", target_bir_lowering=False)
    P = Prog()

    def din(name, shape, dt=F32):
        return nc.dram_tensor(name, list(shape), dt, kind="ExternalInput")

    x_rot = din("x_rot", [S, D])
    meta = din("meta", [N_META, D])
    gvec = din("gvec", [3, D])
    w_in = din("w_in", [D, 9 * D])
    w_br = din("w_br", [D, D])
    w_bd = din("w_bd", [D, D])
    w_out = din("w_out", [D, D])
    w_g = din("w_g", [D, DFF])
    w_u = din("w_u", [D, DFF])
    w_d = din("w_d", [DFF, D])
    rdecay = din("rdecay", [2, H])
    dlam = din("dlam", [4, 128])
    subln = din("subln", [256])
    c_absd = din("c_absd", [NQB, 128, NKT])
    c_sgn = din("c_sgn", [NQB, 2, NKT * 128], BF16)
    c_rb = din("c_rb", [H, 2, 512], BF16)
    c_babs = din("c_babs", [4, 128, 512])
    c_dist = din("c_dist", [2, 128, NKT])
    c_sq = din("c_sq", [4, 128, 128])
    c_kd = din("c_kd", [128, 2])
    c_ident = din("c_ident", [128, 128], BF16)
    y = nc.dram_tensor("y", [T, D], F32, kind="ExternalOutput")

    def dscr(name, shape, dt=BF16):
        return nc.dram_tensor(name, list(shape), dt)

    wb_in = {g: dscr("wb_in_" + g, [D, D]) for g in ("rq", "rk", "rv", "rg", "dq", "dk", "dv", "ga", "gb")}; wb_br = dscr("wb_br", [D, D]); wb_bd = dscr("wb_bd", [D, D])
    wb_out = dscr("wb_out", [D, D]); wb_g = dscr("wb_g", [D, DFF]); wb_u = dscr("wb_u", [D, DFF])
    wb_d = dscr("wb_d", [DFF, D])
    NR = NKT * 128
    rKtm = dscr("rKtm", [NR, D]); rVtm = dscr("rVtm", [NR, D]); dVtm = dscr("dVtm", [NR, D])
    dKT = dscr("dKT", [D, NR])
    rKT = dscr("rKT", [D, T]); rQT = dscr("rQT", [D, T]); rGT = dscr("rGT", [D, T]); dQT = dscr("dQT", [D, T])
    gaT = dscr("gaT", [D, T]); gbT = dscr("gbT", [D, T]); ZrT = dscr("ZrT", [D, T]); ZdT = dscr("ZdT", [D, T])
    h1 = dscr("h1", [T, D], F32)

    stack = contextlib.ExitStack()
    NB16 = 78 * 1024
    NF32 = 11 * 1024
    A16 = stack.enter_context(nc.sbuf_tensor("A16", [128, NB16], BF16))
    A32 = stack.enter_context(nc.sbuf_tensor("A32", [128, NF32], F32))
    ident = stack.enter_context(nc.sbuf_tensor("ident", [128, 128], BF16))
    ones32 = stack.enter_context(nc.sbuf_tensor("ones32", [128, 128], F32))
    ones16 = stack.enter_context(nc.sbuf_tensor("ones16", [128, 128], BF16))
    gT = stack.enter_context(nc.sbuf_tensor("gT", [128, 3, KC], F32))
    csq = stack.enter_context(nc.sbuf_tensor("csq", [128, 4, 128], F32))
    ckd = stack.enter_context(nc.sbuf_tensor("ckd", [128, 2], F32))
    sm = stack.enter_context(nc.sbuf_tensor("sm", [128, 64], F32))
    epsc = stack.enter_context(nc.sbuf_tensor("epsc", [128, 2], F32))
    trigt = stack.enter_context(nc.sbuf_tensor("trigt", [128, 2], F32))
    PS = [stack.enter_context(nc.psum_tensor("ps%d" % i, [128, 512], F32)) for i in range(8)]
    PB = [Buf("ps%d" % i) for i in range(8)]
    B_const = Buf("const")

    class Arena:
        def __init__(s):
            s.o16 = 0; s.o32 = 0

        def a16(s, n, shape=None):
            v = A16[:, s.o16:s.o16 + n]; s.o16 += n
            assert s.o16 <= NB16, ("bf16 arena overflow", s.o16)
            return v

        def a32(s, n):
            v = A32[:, s.o32:s.o32 + n]; s.o32 += n
            assert s.o32 <= NF32, ("f32 arena overflow", s.o32)
            return v

    evq = [0]

    def ev_eng():
        evq[0] += 1
        return "act" if evq[0] % 2 else "dve"

    def copy_op(eng, out, in_, r, w, scale=None):
        if eng == "act":
            if scale is None:
                P.op("act", lambda E: E.activation(out=out, in_=in_, func=AF.Copy), r=r, w=w)
            else:
                P.op("act", lambda E: E.activation(out=out, in_=in_, func=AF.Copy, scale=scale), r=r, w=w)
        else:
            if scale is None:
                P.op("dve", lambda E: E.tensor_copy(out=out, in_=in_), r=r, w=w)
            else:
                P.op("dve", lambda E: E.tensor_scalar(out=out, in0=in_, scalar1=scale, scalar2=None, op0=ALU.mult), r=r, w=w)

    ckey = [0]

    def dma(out, in_, r, w, key, eng="sp"):
        if key == "c0":
            ckey[0] += 1
            key = "c%d" % ckey[0]
        P.op(eng, lambda E: E.dma_start(out=out, in_=in_), r=r, w=w, dma=key)

    B_x = Buf("x_in"); B_w = Buf("w_f32")
    dma(ident[:], c_ident.ap(), [B_x], [B_const], "c0")
    dma(csq[:], c_sq.ap().rearrange("a p f -> p a f"), [B_x], [B_const], "c0")
    dma(ckd[:], c_kd.ap(), [B_x], [B_const], "c0")
    with nc.allow_non_contiguous_dma(reason="tiny gamma relayout"):
        dma(gT[:], gvec.ap().rearrange("a (kc p) -> p a kc", p=128), [B_x], [B_const], "c0")
    P.op("dve", lambda E: E.memset(ones32[:], 1.0), w=[B_const])
    P.op("dve", lambda E: E.memset(ones16[:], 1.0), w=[B_const])
    P.op("dve", lambda E: E.memset(epsc[:], EPS), w=[B_const])

    WB = {}

    pending_casts = []
    pcnt = [0]

    def cast_w(name, src, dst, c0, c1, rows, d0=None, defer=False):
        if name not in WB:
            WB[name] = Buf("wb_" + name)
        if defer:
            pending_casts.append((name, src, dst, c0, c1, rows, d0))
            return
        b = WB[name]
        d0 = c0 if d0 is None else d0
        RB = 512 if (c1 - c0) <= 4096 else 128
        trig = Buf("trig_" + name)
        P.op("dve", lambda E: E.memset(trigt[:, 0:1], 0.0), w=[trig])
        for r0 in range(0, rows, RB):
            r1 = min(rows, r0 + RB)
            P.op("pool", lambda E, r0=r0, r1=r1: E.dma_start(out=dst.ap()[r0:r1, d0:d0 + c1 - c0], in_=src.ap()[r0:r1, c0:c1]),
                 r=[B_w, trig], w=[b], dma="cw_" + name)

    def next_cast():
        if pending_casts:
            a = pending_casts.pop(0)
            cast_w(*a, defer=False)

    GR = {"rq": 0, "rk": 1, "rv": 2, "rg": 3, "dq": 4, "dk": 5, "dv": 6, "ga": 7, "gb": 8}
    for g in ("rk", "rv", "dv", "dk", "rq", "dq", "rg", "ga", "gb"):
        cast_w(g, w_in, wb_in[g], GR[g] * D, (GR[g] + 1) * D, D, d0=0, defer=(g not in ("rk", "rv", "dv", "dk")))
    cast_w("br", w_br, wb_br, 0, D, D, defer=True); cast_w("bd", w_bd, wb_bd, 0, D, D, defer=True)
    cast_w("out", w_out, wb_out, 0, D, D, defer=True)
    cast_w("g", w_g, wb_g, 0, DFF, D, defer=True); cast_w("u", w_u, wb_u, 0, DFF, D, defer=True)
    cast_w("d", w_d, wb_d, 0, D, DFF, defer=True)

    psrr = {}

    def next_ps(lo=0, hi=8):
        k = (lo, hi)
        i = psrr.get(k, lo); psrr[k] = lo + ((i - lo + 1) % (hi - lo))
        return i

    def norm_T(ar, src_rows_ap, np_, gi, uT, uTb, tcol, xt, xtb, ub, ubb, ssc, src_buf=None):
        src_buf = src_buf or B_x
        dma(xt[:np_, :], src_rows_ap, [src_buf], [xtb], "xt")
        P.op("dve", lambda E: E.memset(ssc[:, 0:1], 0.0), w=[B_ss])
        P.op("act", lambda E: E.activation(out=ub[:np_, :], in_=xt[:np_, :], func=AF.Square, accum_out=ssc[:np_, 0:1]),
             r=[xtb], w=[ubb, B_ss])
        P.op("act", lambda E: E.activation(out=ssc[:np_, 1:2], in_=ssc[:np_, 0:1], func=AF.Ln, scale=1.0 / D, bias=epsc[:np_, 0:1]),
             r=[B_ss, B_const], w=[B_ss])
        P.op("act", lambda E: E.activation(out=ssc[:np_, 2:3], in_=ssc[:np_, 1:2], func=AF.Exp, scale=-0.5), r=[B_ss], w=[B_ss])
        P.op("act", lambda E: E.activation(out=ub[:np_, :], in_=xt[:np_, :], func=AF.Copy, scale=ssc[:np_, 2:3]),
             r=[xtb, B_ss], w=[ubb])
        for k0 in range(0, KC, 8):
            nk = min(8, KC - k0)
            pi = next_ps()
            pv = PS[pi][:].bitcast(BF16)
            for j in range(nk):
                P.op("pe", lambda E, j=j, k0=k0, pv=pv: E.transpose(out=pv[:, j * 128:j * 128 + np_],
                                                                   in_=ub[:np_, (k0 + j) * 128:(k0 + j + 1) * 128],
                                                                   identity=ident[:np_, :np_]),
                     r=[ubb, B_const], w=[PB[pi]])
            src = pv[:, 0:nk * 128].rearrange("p (k t) -> p k t", t=128)[:, :, 0:np_]
            gsl = gT[:, gi, k0:k0 + nk]
            P.op("dve", lambda E, src=src, gsl=gsl, k0=k0, nk=nk: E.tensor_tensor(
                out=uT[:, k0:k0 + nk, tcol:tcol + np_], in0=src,
                in1=gsl.unsqueeze(2).to_broadcast([128, nk, np_]), op=ALU.mult),
                r=[PB[pi], B_const], w=[uTb])

    B_ss = Buf("ss")

    def proj(actT, actb, ntok, wdram, wbuf, K, c0, ncols, mode, evac, wt, wtb, WC):
        nkc = K // 128
        kgs = [(k, min(nkc, k + 32)) for k in range(0, nkc, 32)]
        wr = [0]
        for cb0 in range(0, ncols, WC):
            wc = min(WC, ncols - cb0)
            tiles = []

            def load(kg):
                i = wr[0] % len(wt); wr[0] += 1
                ka, kb = kgs[kg]
                for k8 in range(ka, kb, 8):
                    k9 = min(kb, k8 + 8)
                    dma(wt[i][:, k8 - ka:k9 - ka, 0:wc],
                        wdram.ap()[k8 * 128:k9 * 128, c0 + cb0:c0 + cb0 + wc].rearrange("(k p) c -> p k c", p=128),
                        [wbuf], [wtb[i]], "ld_" + wtb[i].name)
                return i
            if mode == "tm":
                ntt = (ntok + 127) // 128
                for th0 in range(0, ntt, 4):
                    tts = list(range(th0, min(ntt, th0 + 4)))
                    banks = {tt: next_ps() for tt in tts}
                    for kg, (ka, kb) in enumerate(kgs):
                        if th0 == 0 or len(kgs) > 1:
                            wi = load(kg)
                            if len(kgs) == 1:
                                tiles = [wi]
                        else:
                            wi = tiles[0]
                        for tt in tts:
                            n = min(128, ntok - tt * 128)
                            for kc in range(ka, kb):
                                P.op("pe", lambda E, tt=tt, n=n, kc=kc, ka=ka, wi=wi, b=banks[tt]: E.matmul(
                                    PS[b][:n, 0:wc], lhsT=actT[:, kc, tt * 128:tt * 128 + n], rhs=wt[wi][:, kc - ka, 0:wc],
                                    start=(kc == 0), stop=(kc == nkc - 1)), r=[actb, wtb[wi]], w=[PB[banks[tt]]])
                    for tt in tts:
                        n = min(128, ntok - tt * 128)
                        evac(banks[tt], PS[banks[tt]][:n, 0:wc], tt * 128, n, cb0, wc)
            else:
                wi = load(0)
                for cc in range(0, wc, 128):
                    for tg in range(0, ntok, 512):
                        n = min(512, ntok - tg)
                        b = next_ps()
                        for kc in range(nkc):
                            P.op("pe", lambda E, kc=kc, cc=cc, tg=tg, n=n, b=b, wi=wi: E.matmul(
                                PS[b][:, 0:n], lhsT=wt[wi][:, kc, cc:cc + 128], rhs=actT[:, kc, tg:tg + n],
                                start=(kc == 0), stop=(kc == nkc - 1)), r=[actb, wtb[wi]], w=[PB[b]])
                        evac(b, PS[b][:, 0:n], tg, n, cb0 + cc, 128)

    ar = Arena()
    uT = ar.a16(KC * TB).rearrange("p (k t) -> p k t", t=TB); uTb = Buf("uT")
    WC1 = 512
    wt = [ar.a16(32 * WC1).rearrange("p (k c) -> p k c", c=WC1) for _ in range(2)]
    wtb = [Buf("wt0"), Buf("wt1")]
    ub = ar.a16(D); ubb = Buf("ub")
    ost = [ar.a16(512) for _ in range(4)]; ostb = [Buf("ost%d" % i) for i in range(4)]
    xt = ar.a32(D); xtb = Buf("xt")
    osr = [0]
    B_scr = {n: Buf("scr_" + n) for n in ("rKtm", "rVtm", "dVtm", "dKT", "rKT", "rQT", "rGT", "dQT", "gaT", "gbT", "ZrT", "ZdT", "h1", "y")}

    def mk_evac(dst, dstb, tm, row0, scale=None, func=None):
        def evac(b, pap, t0, n, c0, ncol):
            i = osr[0] % 4; osr[0] += 1
            if tm:
                o = ost[i][:n, 0:ncol]
            else:
                o = ost[i][:, 0:n]
            if func is not None:
                P.op("act", lambda E: E.activation(out=o, in_=pap, func=func), r=[PB[b]], w=[ostb[i]])
            else:
                copy_op(ev_eng(), o, pap, [PB[b]], [ostb[i]], scale)
            if tm:
                dma(dst.ap()[row0 + t0:row0 + t0 + n, c0:c0 + ncol], o, [ostb[i]], [dstb], "st_" + dstb.name, eng="act")
            else:
                dma(dst.ap()[c0:c0 + ncol, row0 + t0:row0 + t0 + n], o, [ostb[i]], [dstb], "st_" + dstb.name, eng="act")
        return evac

    blocks = [(r0, TB) for r0 in range(0, S, TB)] + [(S, N_META)]

    def stage1_block(r0, ntok):
        own = r0 < T
        for t0 in range(0, ntok, 128):
            np_ = min(128, ntok - t0)
            src = (x_rot.ap()[r0 + t0:r0 + t0 + np_, :] if r0 < S else meta.ap()[0:np_, :])
            norm_T(ar, src, np_, 0, uT, uTb, t0, xt, xtb, ub, ubb, sm)
        def P_(g, dst, tm, scale=None, func=None, row0=r0):
            pcnt[0] += 1
            if pcnt[0] % 2 == 0 and len(pending_casts) > 6:
                next_cast()
            proj(uT, uTb, ntok, wb_in[g], WB[g], D, 0, D, "tm" if tm else "fm",
                 mk_evac(dst, B_scr[dst.name], tm, row0, scale, func), wt, wtb, WC1)
        P_("rk", rKtm, True, scale=256.0 ** -0.5)
        P_("rv", rVtm, True)
        P_("dv", dVtm, True)
        P_("dk", dKT, False)
        if own:
            P_("rk", rKT, False, scale=256.0 ** -0.5)
            P_("rq", rQT, False)
            P_("dq", dQT, False, scale=128.0 ** -0.5)
            P_("rg", rGT, False, func=AF.Silu)
            P_("ga", gaT, False, func=AF.Sigmoid)
            P_("gb", gbT, False, func=AF.Sigmoid)
    blocks = [b_ for b_ in blocks if b_[0] >= T] + [b_ for b_ in blocks if b_[0] < T]
    for (r0_, ntok_) in blocks:
        stage1_block(r0_, ntok_)
    while len(pending_casts) > 6:
        next_cast()
    P.barrier()

    ar = Arena()
    rK = ar.a16(NKT * 256).rearrange("p (t c) -> p t c", c=256); rKb = Buf("rK")
    rV = ar.a16(NKT * 256).rearrange("p (t c) -> p t c", c=256); rVb = Buf("rV")
    kT = ar.a16(2 * T).rearrange("p (k t) -> p k t", t=T); kTb = Buf("kT")
    qT = ar.a16(2 * T).rearrange("p (k t) -> p k t", t=T); qTb = Buf("qT")
    gTt = ar.a16(2 * T).rearrange("p (k t) -> p k t", t=T); gTb = Buf("gTt")
    qf = ar.a16(2 * T).rearrange("p (k t) -> p k t", t=T); qfb = Buf("qf")
    qb_ = ar.a16(2 * T).rearrange("p (k t) -> p k t", t=T); qbb = Buf("qb")
    Sbs = ar.a16(NT * 512).rearrange("p (c k v) -> p c k v", k=2, v=256); Sbsb = Buf("Sbs")
    Sf16 = ar.a16(512).rearrange("p (k v) -> p k v", v=256); Sf16b = Buf("Sf16")
    kw = [ar.a16(256) for _ in range(2)]; kwb = [Buf("kw0"), Buf("kw1")]
    aT = [ar.a16(128) for _ in range(2)]; aTb = [Buf("aT0"), Buf("aT1")]
    zo = [ar.a16(512) for _ in range(2)]; zob = [Buf("zo0"), Buf("zo1")]
    Mh = ar.a32(128); Mhb = Buf("Mh")
    Mt = ar.a32(128)
    QD = ar.a32(256).rearrange("p (a t) -> p a t", t=128); QDb = Buf("QD")
    wfb_ = ar.a32(2 * NKT).rearrange("p (a t) -> p a t", t=NKT); wfbb = Buf("wfb")
    cdist = ar.a32(2 * NKT).rearrange("p (a t) -> p a t", t=NKT)
    kdc = ar.a32(4); kdcb = Buf("kdc")
    Sm = ar.a32(1024).rearrange("p (d k v) -> p d k v", k=2, v=256); Smb = [Buf("Smf"), Buf("Smb")]
    osq = ar.a32(1024).rearrange("p (k t) -> p k t", t=512); osqb = Buf("osq")
    orr = ar.a32(512); orrb = Buf("orr")
    ld_raw = ar.a32(2 * H).rearrange("p (a h) -> p a h", h=H); ldb = Buf("ld")
    dma(cdist[:], c_dist.ap().rearrange("a p t -> p a t"), [B_x], [B_const], "c0")
    dma(ld_raw[:].rearrange("p a h -> p (a h)"), rdecay.ap().rearrange("a h -> (a h)").partition_broadcast(128), [B_x], [ldb], "c0")
    P.op("act", lambda E: E.activation(out=ld_raw[:], in_=ld_raw[:], func=AF.Exp), r=[ldb], w=[ldb])
    P.op("dve", lambda E: E.tensor_scalar(out=ld_raw[:], in0=ld_raw[:], scalar1=-1.0, scalar2=None, op0=ALU.mult),
         r=[ldb], w=[ldb])

    def tile_np(kt):
        return N_META if kt == NKT - 1 else 128

    def flat(a):
        return a.rearrange("p k v -> p (k v)")

    def ret_upd_state(d, c):
        i = (c + d) % 2
        P.op("dve", lambda E: E.tensor_scalar(out=kw[i][:, :], in0=rK[:, c, :], scalar1=kdc[:, d:d + 1], scalar2=None,
                                               op0=ALU.mult), r=[rKb, kdcb], w=[kwb[i]])
        b = next_ps(4, 8)
        for dk in range(2):
            P.op("pe", lambda E, dk=dk: E.matmul(PS[b][:, dk * 256:(dk + 1) * 256], lhsT=kw[i][:, dk * 128:(dk + 1) * 128],
                                                 rhs=rV[:, c, :], start=(dk == 0), stop=(dk == 1)),
                 r=[kwb[i], rVb], w=[PB[b]])
        P.op("dve", lambda E: E.scalar_tensor_tensor(
            out=flat(Sm[:, d, :, :]), in0=flat(Sm[:, d, :, :]),
            scalar=kdc[:, 2 + d:3 + d], in1=PS[b][:, :], op0=ALU.mult, op1=ALU.add),
            r=[Smb[d], PB[b], kdcb], w=[Smb[d]])

    def ret_chunk(h, c, c4, po):
        cs = slice(c * 128, (c + 1) * 128)
        P.op("act", lambda E: E.activation(out=flat(Sf16[:]), in_=flat(Sm[:, 0, :, :]), func=AF.Copy),
             r=[Smb[0]], w=[Sf16b])
        bs = next_ps(4, 8)
        for dk in range(2):
            P.op("pe", lambda E, dk=dk: E.matmul(PS[bs][:, 0:128], lhsT=kT[:, dk, cs], rhs=qT[:, dk, cs],
                                                 start=(dk == 0), stop=(dk == 1)), r=[kTb, qTb], w=[PB[bs]])
        ia = c % 2
        P.op("dve", lambda E: E.tensor_tensor(out=aT[ia][:], in0=PS[bs][:, 0:128], in1=Mh[:], op=ALU.mult),
             r=[PB[bs], Mhb], w=[aTb[ia]])
        for vh in range(2):
            vs = slice(vh * 128, (vh + 1) * 128)
            oc = slice((c - c4) * 128, (c - c4 + 1) * 128)
            seq = [(rV[:, c, vs], aT[ia][:], [rVb, aTb[ia]])]
            for dk in range(2):
                seq.append((Sf16[:, dk, vs], qf[:, dk, cs], [Sf16b, qfb]))
                seq.append((Sbs[:, c, dk, vs], qb_[:, dk, cs], [Sbsb, qbb]))
            ns = len(seq)
            for j, (l, r_, rb_) in enumerate(seq):
                P.op("pe", lambda E, l=l, r_=r_, j=j, vh=vh, oc=oc: E.matmul(
                    PS[po[vh]][:, oc], lhsT=l, rhs=r_, start=(j == 0), stop=(j == ns - 1)),
                    r=rb_, w=[PB[po[vh]]])
        if c < NT - 1:
            ret_upd_state(0, c)

    def ret_group(h, c4):
        po = [2, 3]
        c5 = min(NT, c4 + 4)
        for c in range(c4, c5):
            ret_chunk(h, c, c4, po)
        n = (c5 - c4) * 128
        ts_ = slice(c4 * 128, c4 * 128 + n)
        for vh in range(2):
            P.op("act", lambda E, vh=vh: E.activation(out=osq[:, vh, 0:n], in_=PS[po[vh]][:, 0:n], func=AF.Square),
                 r=[PB[po[vh]]], w=[osqb])
        br = next_ps(4, 8)
        for vh in range(2):
            P.op("pe", lambda E, vh=vh: E.matmul(PS[br][:, 0:n], lhsT=ones32[:], rhs=osq[:, vh, 0:n],
                                                 start=(vh == 0), stop=(vh == 1)), r=[osqb, B_const], w=[PB[br]])
        P.op("act", lambda E: E.activation(out=orr[:, 0:n], in_=PS[br][:, 0:n], func=AF.Ln, scale=1.0 / 256, bias=epsc[:, 0:1]),
             r=[PB[br], B_const], w=[orrb])
        P.op("act", lambda E: E.activation(out=orr[:, 0:n], in_=orr[:, 0:n], func=AF.Exp, scale=-0.5), r=[orrb], w=[orrb])
        for vh in range(2):
            P.op("dve", lambda E, vh=vh: E.tensor_tensor(out=osq[:, vh, 0:n], in0=PS[po[vh]][:, 0:n], in1=orr[:, 0:n],
                                                         op=ALU.mult), r=[PB[po[vh]], orrb], w=[osqb])
            P.op("dve", lambda E, vh=vh: E.tensor_tensor(out=zo[vh][:, 0:n], in0=osq[:, vh, 0:n], in1=gTt[:, vh, ts_],
                                                         op=ALU.mult), r=[osqb, gTb], w=[zob[vh]])
            dma(ZrT.ap()[h * 256 + vh * 128:h * 256 + (vh + 1) * 128, ts_], zo[vh][:, 0:n], [zob[vh]], [B_scr["ZrT"]], "st_ZrT", eng="act")

    def ret_head(h):
        hc = slice(h * 256, (h + 1) * 256)
        for t8 in range(0, NTALL, 16):
            t9 = min(NTALL, t8 + 16)
            dma(rK[:, t8:t9, :], rKtm.ap()[t8 * 128:t9 * 128, hc].rearrange("(t p) c -> p t c", p=128), [B_scr["rKtm"]], [rKb], "rK")
        dma(rK[:N_META, NTALL, :], rKtm.ap()[S:S + N_META, hc], [B_scr["rKtm"]], [rKb], "rK")
        for t8 in range(0, NTALL, 16):
            t9 = min(NTALL, t8 + 16)
            dma(rV[:, t8:t9, :], rVtm.ap()[t8 * 128:t9 * 128, hc].rearrange("(t p) c -> p t c", p=128), [B_scr["rVtm"]], [rVb], "rV")
        dma(rV[:N_META, NTALL, :], rVtm.ap()[S:S + N_META, hc], [B_scr["rVtm"]], [rVb], "rV")
        for (dst, dstb, srcT, nm) in ((kT, kTb, rKT, "rKT"), (qT, qTb, rQT, "rQT"), (gTt, gTb, rGT, "rGT")):
            dma(dst[:], srcT.ap()[hc, :].rearrange("(k p) t -> p k t", p=128), [B_scr[nm]], [dstb], "r" + nm)
        for d in range(2):
            lg = ld_raw[:, d, h:h + 1]
            P.op("act", lambda E, d=d, lg=lg: E.activation(out=wfb_[:, d, :], in_=cdist[:, d, :], func=AF.Exp, scale=lg),
                 r=[B_const, ldb], w=[wfbb])
            P.op("act", lambda E, d=d, lg=lg: E.activation(out=QD[:, d, :], in_=csq[:, 2 + d, :], func=AF.Exp, scale=lg),
                 r=[B_const, ldb], w=[QDb])
            P.op("act", lambda E, d=d, lg=lg: E.activation(out=kdc[:, d:d + 1], in_=ckd[:, d:d + 1], func=AF.Exp, scale=lg),
                 r=[B_const, ldb], w=[kdcb])
            P.op("act", lambda E, d=d, lg=lg: E.activation(out=kdc[:, 2 + d:3 + d], in_=lg, func=AF.Exp, scale=128.0),
                 r=[B_const, ldb], w=[kdcb])
        P.op("act", lambda E: E.activation(out=Mh[:], in_=csq[:, 0, :], func=AF.Exp, scale=ld_raw[:, 0, h:h + 1]),
             r=[B_const, ldb], w=[Mhb])
        P.op("act", lambda E: E.activation(out=Mt[:], in_=csq[:, 1, :], func=AF.Exp, scale=ld_raw[:, 1, h:h + 1]),
             r=[B_const, ldb, Mhb], w=[Mhb])
        P.op("dve", lambda E: E.tensor_tensor(out=Mh[:], in0=Mh[:], in1=Mt[:], op=ALU.add), r=[Mhb], w=[Mhb])
        for d, (dst, dstb) in enumerate(((qf, qfb), (qb_, qbb))):
            P.op("dve", lambda E, d=d, dst=dst: E.tensor_tensor(
                out=dst[:].rearrange("p k (c i) -> p (k c) i", i=128),
                in0=qT[:].rearrange("p k (c i) -> p (k c) i", i=128),
                in1=QD[:, d, :].unsqueeze(1).to_broadcast([128, 2 * NT, 128]), op=ALU.mult),
                r=[qTb, QDb], w=[dstb])
        pin = [0, 1]
        for d in range(2):
            for kt in range(NT, NKT):
                np_ = tile_np(kt)
                i = (kt + d) % 2
                P.op("dve", lambda E, i=i, kt=kt, d=d, np_=np_: E.tensor_scalar(
                    out=kw[i][:np_, :], in0=rK[:np_, kt, :], scalar1=wfb_[:np_, d, kt:kt + 1], scalar2=None, op0=ALU.mult),
                    r=[rKb, wfbb], w=[kwb[i]])
                for dk in range(2):
                    P.op("pe", lambda E, i=i, kt=kt, d=d, dk=dk, np_=np_: E.matmul(
                        PS[pin[d]][:, dk * 256:(dk + 1) * 256], lhsT=kw[i][:np_, dk * 128:(dk + 1) * 128],
                        rhs=rV[:np_, kt, :], start=(kt == NT and dk == 0), stop=(kt == NKT - 1 and dk == 1),
                        skip_group_check=True),
                        r=[kwb[i], rVb], w=[PB[pin[d]]])
            P.op("dve", lambda E, d=d: E.tensor_copy(out=flat(Sm[:, d, :, :]), in_=PS[pin[d]][:, :]),
                 r=[PB[pin[d]]], w=[Smb[d]])
        for c in range(NT - 1, -1, -1):
            P.op("act", lambda E, c=c: E.activation(out=flat(Sbs[:, c, :, :]), in_=flat(Sm[:, 1, :, :]), func=AF.Copy),
                 r=[Smb[1]], w=[Sbsb])
            if c > 0:
                ret_upd_state(1, c)
        for c4 in range(0, NT, 4):
            ret_group(h, c4)

    for h in range(H):
        ret_head(h)
    P.barrier()

    ar = Arena()
    k1 = ar.a16(NKT * 128); k2 = ar.a16(NKT * 128); kkb = Buf("kk")
    vv = ar.a16(NKT * 256).rearrange("p (t c) -> p t c", c=256); vvb = Buf("vv")
    sg = [ar.a16(NKT * 128) for _ in range(2)]; sgb = [Buf("sg0"), Buf("sg1")]
    qq = [ar.a16(1024).rearrange("p (m t) -> p m t", t=512) for _ in range(2)]; qqb = [Buf("qq0"), Buf("qq1")]
    rbt = ar.a16(512); rbb = Buf("rb")
    ee = [ar.a16(512) for _ in range(4)]; eeb = [Buf("ee%d" % i) for i in range(4)]
    zd = [ar.a16(512) for _ in range(2)]; zdb = [Buf("zd0"), Buf("zd1")]
    babs = ar.a32(4 * 512).rearrange("p (a t) -> p a t", t=512)
    absd = ar.a32(NQB * NKT).rearrange("p (a t) -> p a t", t=NKT)
    bcol = ar.a32(NKT); bcolb = Buf("bcol")
    zacc = [ar.a32(512) for _ in range(2)]; zaccb = [Buf("zacc0"), Buf("zacc1")]
    sp_ = [ar.a32(512) for _ in range(2)]; spb = [Buf("sp0"), Buf("sp1")]
    o32 = ar.a32(1024).rearrange("p (k t) -> p k t", t=512); o32b = Buf("o32")
    t32 = ar.a32(1024).rearrange("p (k t) -> p k t", t=512); t32b = Buf("t32")
    rz = ar.a32(1024).rearrange("p (k t) -> p k t", t=512); rzb = Buf("rz")
    lam = ar.a32(8); lamb = Buf("lam")
    sgc = ar.a32(2); sgcb = Buf("sgc")
    lraw = ar.a32(512)
    dma(babs[:], c_babs.ap().rearrange("a p t -> p a t"), [B_x], [B_const], "c0")
    dma(absd[:], c_absd.ap().rearrange("a p t -> p a t"), [B_x], [B_const], "c0")
    dma(sgc[:], subln.ap().rearrange("(k p) -> p k", p=128), [B_x], [sgcb], "c0")
    dma(lraw[:], dlam.ap().rearrange("a f -> (a f)").partition_broadcast(128), [B_x], [lamb], "c0")
    P.op("dve", lambda E: E.tensor_tensor(out=lraw[:, 0:128], in0=lraw[:, 0:128], in1=lraw[:, 128:256], op=ALU.mult), r=[lamb], w=[lamb])
    P.op("dve", lambda E: E.tensor_tensor(out=lraw[:, 256:384], in0=lraw[:, 256:384], in1=lraw[:, 384:512], op=ALU.mult), r=[lamb], w=[lamb])
    P.op("dve", lambda E: E.reduce_sum(out=lam[:, 0:1], in_=lraw[:, 0:128], axis=mybir.AxisListType.X), r=[lamb], w=[lamb])
    P.op("dve", lambda E: E.reduce_sum(out=lam[:, 1:2], in_=lraw[:, 256:384], axis=mybir.AxisListType.X), r=[lamb], w=[lamb])
    P.op("act", lambda E: E.activation(out=lam[:, 2:4], in_=lam[:, 0:2], func=AF.Exp), r=[lamb], w=[lamb])
    P.op("dve", lambda E: E.tensor_tensor(out=lam[:, 4:5], in0=lam[:, 3:4], in1=lam[:, 2:3], op=ALU.subtract), r=[lamb], w=[lamb])
    P.op("dve", lambda E: E.tensor_scalar(out=lam[:, 4:5], in0=lam[:, 4:5], scalar1=-cfg.LAM_INIT, scalar2=None, op0=ALU.add), r=[lamb], w=[lamb])
    P.op("dve", lambda E: E.tensor_scalar(out=sgc[:], in0=sgc[:], scalar1=1.0 - cfg.LAM_INIT, scalar2=None, op0=ALU.mult),
         r=[sgcb], w=[sgcb])
    SL = slopes(cfg)

    def diff_S(h, qb, qi, kt, pos):
        np_ = tile_np(kt)
        diag = (qb * 4 <= kt < qb * 4 + 4)
        for m, kk in enumerate((k1, k2)):
            bS = next_ps(4, 8)
            ie = (2 * pos + m) % 4
            P.op("pe", lambda E, kk=kk, bS=bS, m=m: E.matmul(
                PS[bS][:np_, :], lhsT=kk[:, kt * 128:kt * 128 + np_], rhs=qq[qi][:, m, :], start=True, stop=diag),
                r=[kkb, qqb[qi]], w=[PB[bS]])
            if not diag:
                P.op("pe", lambda E, bS=bS: E.matmul(
                    PS[bS][:np_, :], lhsT=sg[qi][0:2, kt * 128:kt * 128 + np_], rhs=rbt[0:2, :], start=False, stop=True),
                    r=[sgb[qi], rbb], w=[PB[bS]])
                P.op("act", lambda E, bS=bS, ie=ie: E.activation(
                    out=ee[ie][:np_, :], in_=PS[bS][:np_, :], func=AF.Exp, bias=bcol[:np_, kt:kt + 1]),
                    r=[PB[bS], bcolb], w=[eeb[ie]])
            else:
                ci = kt - qb * 4
                P.op("dve", lambda E, bS=bS, m=m: E.scalar_tensor_tensor(
                    out=sp_[m][:], in0=babs[:, ci, :], scalar=-SL[h], in1=PS[bS][:, :], op0=ALU.mult, op1=ALU.add),
                    r=[PB[bS], B_const], w=[spb[m]])
                P.op("act", lambda E, m=m, ie=ie: E.activation(out=ee[ie][:], in_=sp_[m][:], func=AF.Exp),
                     r=[spb[m]], w=[eeb[ie]])

    def diff_AV(h, qb, qi, kt, pO, pos, first, last):
        np_ = tile_np(kt)
        for m in range(2):
            ie = (2 * pos + m) % 4
            if first:
                P.op("dve", lambda E, m=m, ie=ie: E.tensor_copy(out=zacc[m][:], in_=ee[ie][:]), r=[eeb[ie]], w=[zaccb[m]])
            else:
                P.op("dve", lambda E, m=m, ie=ie: E.tensor_tensor(out=zacc[m][:np_, :], in0=zacc[m][:np_, :],
                                                                 in1=ee[ie][:np_, :], op=ALU.add),
                     r=[eeb[ie], zaccb[m]], w=[zaccb[m]])
            for vh in range(2):
                P.op("pe", lambda E, vh=vh, m=m, ie=ie: E.matmul(
                    PS[pO[m][vh]][:, :], lhsT=vv[:np_, kt, vh * 128:(vh + 1) * 128], rhs=ee[ie][:np_, :],
                    start=first, stop=last), r=[vvb, eeb[ie]], w=[PB[pO[m][vh]]])

    def key_tiles(h):
        R = 60.0 / SL[h]
        r = int(math.ceil(R / 128.0))
        nother = NTALL - NT
        if 2 * r >= nother:
            oth = list(range(NT, NTALL))
        else:
            oth = list(range(NT, NT + r)) + list(range(NTALL - r, NTALL))
        return list(range(NT)) + oth + [NTALL]

    def diff_iter(h, qb, qi):
        q0 = qb * 512
        dma(qq[qi][:], dQT.ap()[h * 256:(h + 1) * 256, q0:q0 + 512].rearrange("(m p) t -> p m t", p=128),
            [B_scr["dQT"]], [qqb[qi]], "qq%d" % qi)
        dma(sg[qi][0:2, :], c_sgn.ap()[qb], [B_x], [sgb[qi]], "sg%d" % qi)
        P.op("dve", lambda E: E.tensor_scalar(out=bcol[:], in0=absd[:, qb, :], scalar1=-SL[h], scalar2=None,
                                               op0=ALU.mult), r=[B_const], w=[bcolb])
        pO = [[0, 1], [2, 3]]
        tl = key_tiles(h)
        for i in range(len(tl) + 1):
            if i < len(tl):
                diff_S(h, qb, qi, tl[i], i)
            if i >= 1:
                diff_AV(h, qb, qi, tl[i - 1], pO, i - 1, i - 1 == 0, i - 1 == len(tl) - 1)
        for m in range(2):
            bz = next_ps(4, 8)
            P.op("pe", lambda E, m=m, bz=bz: E.matmul(PS[bz][:, :], lhsT=ones32[:], rhs=zacc[m][:], start=True, stop=True),
                 r=[zaccb[m], B_const], w=[PB[bz]])
            P.op("dve", lambda E, m=m, bz=bz: E.reciprocal(out=rz[:, m, :], in_=PS[bz][:, :]), r=[PB[bz]], w=[rzb])
        for vh in range(2):
            P.op("dve", lambda E, vh=vh: E.tensor_tensor(out=t32[:, vh, :], in0=PS[pO[1][vh]][:, :], in1=rz[:, 1, :], op=ALU.mult),
                 r=[PB[pO[1][vh]], rzb], w=[t32b])
            P.op("dve", lambda E, vh=vh: E.tensor_tensor(out=o32[:, vh, :], in0=PS[pO[0][vh]][:, :], in1=rz[:, 0, :], op=ALU.mult),
                 r=[PB[pO[0][vh]], rzb], w=[o32b])
            P.op("dve", lambda E, vh=vh: E.scalar_tensor_tensor(out=o32[:, vh, :], in0=t32[:, vh, :], scalar=lam[:, 4:5],
                                                                in1=o32[:, vh, :], op0=ALU.mult, op1=ALU.add),
                 r=[t32b, o32b, lamb], w=[o32b])
            P.op("act", lambda E, vh=vh: E.activation(out=t32[:, vh, :], in_=o32[:, vh, :], func=AF.Square), r=[o32b, t32b], w=[t32b])
        br = next_ps(4, 8)
        for vh in range(2):
            P.op("pe", lambda E, vh=vh: E.matmul(PS[br][:, :], lhsT=ones32[:], rhs=t32[:, vh, :], start=(vh == 0), stop=(vh == 1)),
                 r=[t32b, B_const], w=[PB[br]])
        P.op("act", lambda E: E.activation(out=rz[:, 0, :], in_=PS[br][:, :], func=AF.Ln, scale=1.0 / 256, bias=epsc[:, 0:1]),
             r=[PB[br], rzb, B_const], w=[rzb])
        P.op("act", lambda E: E.activation(out=rz[:, 0, :], in_=rz[:, 0, :], func=AF.Exp, scale=-0.5), r=[rzb], w=[rzb])
        for vh in range(2):
            P.op("dve", lambda E, vh=vh: E.scalar_tensor_tensor(out=zd[vh][:], in0=o32[:, vh, :], scalar=sgc[:, vh:vh + 1],
                                                                in1=rz[:, 0, :], op0=ALU.mult, op1=ALU.mult),
                 r=[o32b, rzb, sgcb], w=[zdb[vh]])
            dma(ZdT.ap()[h * 256 + vh * 128:h * 256 + (vh + 1) * 128, q0:q0 + 512], zd[vh][:], [zdb[vh]], [B_scr["ZdT"]], "st_ZdT", eng="act")

    def diff_head(h, it0):
        dma(k1[:, 0:S + N_META], dKT.ap()[h * 256:h * 256 + 128, 0:S + N_META], [B_scr["dKT"]], [kkb], "kk")
        dma(k2[:, 0:S + N_META], dKT.ap()[h * 256 + 128:h * 256 + 256, 0:S + N_META], [B_scr["dKT"]], [kkb], "kk")
        for t8 in range(0, NTALL, 16):
            t9 = min(NTALL, t8 + 16)
            dma(vv[:, t8:t9, :], dVtm.ap()[t8 * 128:t9 * 128, h * 256:(h + 1) * 256].rearrange("(t p) c -> p t c", p=128), [B_scr["dVtm"]], [vvb], "vv")
        dma(vv[:N_META, NTALL, :], dVtm.ap()[S:S + N_META, h * 256:(h + 1) * 256], [B_scr["dVtm"]], [vvb], "vv")
        dma(rbt[0:2, :], c_rb.ap()[h], [B_x], [rbb], "rb")
        for qb in range(NQB):
            diff_iter(h, qb, (it0 + qb) % 2)

    for h in range(H):
        next_cast()
        diff_head(h, h * NQB)
    while pending_casts:
        next_cast()
    P.barrier()

    ar = Arena()
    TF = 512
    WC4 = 512
    zT = ar.a16(KC * TF).rearrange("p (k t) -> p k t", t=TF); zTb = Buf("zT")
    mxbase = ar.a16(max(KC, 32) * TF).rearrange("p (k t) -> p k t", t=TF); mxTb = Buf("mxT")
    mxT = mxbase[:, 0:KC, :]
    aTf = mxbase; aTfb = mxTb
    wt4 = [ar.a16(32 * WC4).rearrange("p (k c) -> p k c", c=WC4) for _ in range(2)]
    wt4b = [Buf("w40"), Buf("w41")]
    gg = [ar.a16(TF) for _ in range(4)]; ggb = [Buf("gg%d" % i) for i in range(4)]
    ub4 = ar.a16(D); ub4b = Buf("ub4")
    gsb = ar.a16(TF); gsbb = Buf("gsb")
    xt4 = ar.a32(D); xt4b = Buf("xt4")
    gfin = ar.a32(D)
    hst = [ar.a32(WC4) for _ in range(3)]; hstb = [Buf("hst%d" % i) for i in range(3)]
    hrow = [ar.a32(WC4) for _ in range(3)]; hrowb = [Buf("hrow%d" % i) for i in range(3)]
    B_h1 = [Buf("h1_%d" % i) for i in range(4)]
    rr = [0]
    dma(gfin[:], gvec.ap()[2].partition_broadcast(128), [B_x], [B_const], "c0")

    def s4_mix(tb, pas, Zs, nm, wdr, wn, gsrc, gn):
        for k8 in range(0, KC, 8):
            k9 = min(KC, k8 + 8)
            dma(zT[:, k8:k9, :], Zs.ap()[k8 * 128:k9 * 128, tb:tb + TF].rearrange("(k p) t -> p k t", p=128), [B_scr[nm]], [zTb], "zT")

        def ev_mix(b, pap, t0, n, c0, ncol):
            i = rr[0] % 4; rr[0] += 1
            kc = c0 // 128
            dma(gg[i][:, 0:n], gsrc.ap()[c0:c0 + 128, tb + t0:tb + t0 + n], [B_scr[gn]], [ggb[i]], "gg%d" % i)
            if pas == 0:
                P.op("dve", lambda E: E.tensor_tensor(out=mxT[:, kc, t0:t0 + n], in0=pap, in1=gg[i][:, 0:n], op=ALU.mult),
                     r=[PB[b], ggb[i]], w=[mxTb])
            else:
                P.op("dve", lambda E: E.tensor_tensor(out=gsb[:, 0:n], in0=pap, in1=gg[i][:, 0:n], op=ALU.mult),
                     r=[PB[b], ggb[i]], w=[gsbb])
                P.op("dve", lambda E: E.tensor_tensor(out=mxT[:, kc, t0:t0 + n], in0=mxT[:, kc, t0:t0 + n], in1=gsb[:, 0:n], op=ALU.add),
                     r=[gsbb, mxTb], w=[mxTb])
        proj(zT, zTb, TF, wdr, WB[wn], D, 0, D, "fm", ev_mix, wt4, wt4b, WC4)

    def s4_seg(tb, f0, f1):
        def ev_gate(b, pap, t0, n, c0, ncol):
            kc = (c0 // 128)
            P.op("act", lambda E: E.activation(out=aTf[:, kc, t0:t0 + n], in_=pap, func=AF.Silu), r=[PB[b]], w=[aTfb])

        def ev_up(b, pap, t0, n, c0, ncol):
            kc = (c0 // 128)
            P.op("dve", lambda E: E.tensor_tensor(out=aTf[:, kc, t0:t0 + n], in0=pap, in1=aTf[:, kc, t0:t0 + n], op=ALU.mult),
                 r=[PB[b], aTfb], w=[aTfb])

        class _W:
            def __init__(s, a):
                s.a = a

            def ap(s):
                return s.a
        proj(zT, zTb, TF, _W(wb_g.ap()[:, f0 * 128:f1 * 128]), WB["g"], D, 0, (f1 - f0) * 128, "fm", ev_gate, wt4, wt4b, WC4)
        proj(zT, zTb, TF, _W(wb_u.ap()[:, f0 * 128:f1 * 128]), WB["u"], D, 0, (f1 - f0) * 128, "fm", ev_up, wt4, wt4b, WC4)

        def ev_dn(b, pap, t0, n, c0, ncol):
            i = rr[0] % 3; rr[0] += 1
            hb = B_h1[t0 // 128]
            dma(hrow[i][:n, 0:ncol], h1.ap()[tb + t0:tb + t0 + n, c0:c0 + ncol], [hb], [hrowb[i]], "hrow%d" % i)
            P.op("dve", lambda E: E.tensor_tensor(out=hst[i][:n, 0:ncol], in0=pap, in1=hrow[i][:n, 0:ncol], op=ALU.add),
                 r=[PB[b], hrowb[i]], w=[hstb[i]])
            dma(h1.ap()[tb + t0:tb + t0 + n, c0:c0 + ncol], hst[i][:n, 0:ncol], [hstb[i]], [hb], "st_h1_%d" % (t0 // 128), eng="act")
        proj(aTf, aTfb, TF, _W(wb_d.ap()[f0 * 128:f1 * 128, :]), WB["d"], (f1 - f0) * 128, 0, D, "tm", ev_dn, wt4, wt4b, WC4)

    def s4_block(tb):
        s4_mix(tb, 0, ZrT, "ZrT", wb_br, "br", gaT, "gaT")
        s4_mix(tb, 1, ZdT, "ZdT", wb_bd, "bd", gbT, "gbT")

        def ev_h1(b, pap, t0, n, c0, ncol):
            i = rr[0] % 3; rr[0] += 1
            hb = B_h1[t0 // 128]
            dma(hrow[i][:n, 0:ncol], x_rot.ap()[tb + t0:tb + t0 + n, c0:c0 + ncol], [B_x], [hrowb[i]], "hrow%d" % i)
            P.op("dve", lambda E: E.tensor_tensor(out=hst[i][:n, 0:ncol], in0=pap, in1=hrow[i][:n, 0:ncol], op=ALU.add),
                 r=[PB[b], hrowb[i]], w=[hstb[i]])
            dma(h1.ap()[tb + t0:tb + t0 + n, c0:c0 + ncol], hst[i][:n, 0:ncol], [hstb[i]], [hb], "st_h1_%d" % (t0 // 128), eng="act")
        proj(mxT, mxTb, TF, wb_out, WB["out"], D, 0, D, "tm", ev_h1, wt4, wt4b, WC4)
        for t0 in range(0, TF, 128):
            norm_T(ar, h1.ap()[tb + t0:tb + t0 + 128, :], 128, 1, zT, zTb, t0, xt4, xt4b, ub4, ub4b, sm, src_buf=B_h1[t0 // 128])
        for f0 in range(0, FC, 32):
            s4_seg(tb, f0, min(FC, f0 + 32))
        for t0 in range(0, TF, 128):
            s4_fin(tb, t0)

    def s4_fin(tb, t0):
        dma(xt4[:, :], h1.ap()[tb + t0:tb + t0 + 128, :], [B_h1[t0 // 128]], [xt4b], "xt")
        P.op("dve", lambda E: E.memset(sm[:, 0:1], 0.0), w=[B_ss])
        P.op("act", lambda E: E.activation(out=ub4[:, :], in_=xt4[:, :], func=AF.Square, accum_out=sm[:, 0:1]),
             r=[xt4b], w=[ub4b, B_ss])
        P.op("act", lambda E: E.activation(out=sm[:, 1:2], in_=sm[:, 0:1], func=AF.Ln, scale=1.0 / D, bias=epsc[:, 0:1]),
             r=[B_ss, B_const], w=[B_ss])
        P.op("act", lambda E: E.activation(out=sm[:, 2:3], in_=sm[:, 1:2], func=AF.Exp, scale=-0.5), r=[B_ss], w=[B_ss])
        P.op("dve", lambda E: E.scalar_tensor_tensor(out=xt4[:, :], in0=xt4[:, :], scalar=sm[:, 2:3], in1=gfin[:, :],
                                                     op0=ALU.mult, op1=ALU.mult), r=[xt4b, B_ss, B_const], w=[xt4b])
        dma(y.ap()[tb + t0:tb + t0 + 128, :], xt4[:, :], [xt4b], [B_scr["y"]], "st_y", eng="act")

    for tb in range(0, T, TF):
        s4_block(tb)

    with nc.allow_non_contiguous_dma(reason="small strided constant/layout loads"):
        with nc.Block() as block:
            sems = P.emit(nc, block)
        sems.close()
    stack.close()
    return nc


def host_consts(cfg, qt):
    S, T, H, NKT, NQB, NTALL = cfg.S, cfg.T, cfg.H, cfg.NKT, cfg.NQB, cfg.NTALL
    own0 = qt * T
    perm = (own0 + np.arange(S)) % S
    kpos = np.full((NKT, 128), 1e9, np.float64)
    kpos[:NTALL] = (N_META + perm).reshape(NTALL, 128)
    kpos[NTALL, :N_META] = np.arange(N_META)
    absd = np.zeros((NQB, 128, NKT), np.float32)
    sgn = np.zeros((NQB, 2, NKT * 128), np.float32)
    for qb in range(NQB):
        qc = N_META + own0 + qb * 512 + 255.5
        absd[qb] = np.abs(kpos - qc).T
        sg = np.where(kpos < qc, 1.0, -1.0).reshape(-1)
        sgn[qb, 0] = sg; sgn[qb, 1] = sg
    absd = np.minimum(absd, 1e6).astype(np.float32)
    sl = np.array(slopes(cfg), np.float64)
    qrel = np.arange(512) - 255.5
    val = (-sl[:, None] * qrel[None, :]).astype(np.float32)
    hi = val.astype(ml_dtypes.bfloat16)
    lo = (val - hi.astype(np.float32)).astype(ml_dtypes.bfloat16)
    rb = np.stack([hi, lo], axis=1)
    p = np.arange(128)[:, None]; f = np.arange(512)[None, :]
    babs = np.stack([np.abs(ci * 128 + p - f) for ci in range(4)]).astype(np.float32)
    p0 = N_META + own0; p1 = p0 + T
    distf = np.where(kpos < p0, p0 - 1 - kpos, BIG)
    distb = np.where((kpos >= p1) & (kpos < 1e8), kpos - p1, BIG)
    dist = np.stack([distf.T, distb.T]).astype(np.float32)
    s_ = np.arange(128)[:, None]; t_ = np.arange(128)[None, :]
    dpos = np.where(t_ >= s_, t_ - s_, BIG)
    dneg = np.where(s_ > t_, s_ - t_, BIG)
    qdf = np.broadcast_to(t_ + 1.0, (128, 128))
    qdb = np.broadcast_to(128.0 - t_, (128, 128))
    sq = np.stack([dpos, dneg, qdf, qdb]).astype(np.float32)
    kd = np.stack([127.0 - np.arange(128), np.arange(128) * 1.0], axis=1).astype(np.float32)
    return perm, dict(c_absd=absd, c_sgn=sgn.astype(ml_dtypes.bfloat16), c_rb=rb, c_babs=babs, c_dist=dist,
                      c_sq=sq, c_kd=kd, c_ident=np.eye(128, dtype=np.float32).astype(ml_dtypes.bfloat16))


_NC_CACHE = {}


def run(cfg, inp):
    key = (cfg.D, cfg.S)
    if key not in _NC_CACHE:
        _NC_CACHE[key] = build(cfg)
    nc = _NC_CACHE[key]
    f32 = lambda a: np.ascontiguousarray(np.asarray(a, dtype=np.float32))
    x = f32(inp["x"])
    shared = dict(
        meta=f32(inp["meta_tokens"]),
        gvec=np.stack([f32(inp["norm_mix_g"])[0], f32(inp["norm_ffn_g"])[0], f32(inp["norm_final_g"])]),
        w_in=f32(inp["w_in"])[0], w_br=f32(inp["w_branch_ret"])[0], w_bd=f32(inp["w_branch_diff"])[0],
        w_out=f32(inp["w_out"])[0], w_g=f32(inp["w_ffn_gate"])[0], w_u=f32(inp["w_ffn_up"])[0],
        w_d=f32(inp["w_ffn_down"])[0], rdecay=f32(inp["ret_log_decay"])[0], dlam=f32(inp["diff_lambda"])[0],
        subln=f32(inp["diff_subln_g"])[0])
    in_maps = []
    for c in range(8):
        b, qt = c // 4, c % 4
        perm, consts = host_consts(cfg, qt)
        m = dict(shared)
        m["x_rot"] = np.ascontiguousarray(x[b][perm])
        m.update(consts)
        in_maps.append(m)
    res = run_bass_kernel_spmd(nc, in_maps, core_ids=list(range(8)))
    out = np.zeros((2, cfg.S, cfg.D), np.float32)
    for c in range(8):
        b, qt = c // 4, c % 4
        out[b, qt * cfg.T:(qt + 1) * cfg.T] = res.results[c]["y"]
    return out


def kernel(**inputs):
    return run(Cfg(4096, 8192), inputs)
```

```python
import math, contextlib
import numpy as np
import ml_dtypes
import concourse.bass as bass
import concourse.mybir as mybir
from concourse.bass_utils import run_bass_kernel_spmd

F32 = mybir.dt.float32
BF16 = mybir.dt.bfloat16
AF = mybir.ActivationFunctionType
ALU = mybir.AluOpType
BIG = 1.0e30
EPS = 1e-6
N_META = 16


class Cfg:
    def __init__(s, D=4096, S=8192):
        s.D, s.S = D, S
        s.T = S // 4
        s.H = D // 256
        s.DIN = 9 * D
        s.DFF = ((8 * D + 3 * 256 - 1) // (3 * 256)) * 256
        s.KC = D // 128
        s.FC = s.DFF // 128
        s.NT = s.T // 128
        s.NTALL = S // 128
        s.NKT = s.NTALL + 1
        s.NQB = s.T // 512
        s.TB = min(1024, s.T)
        s.LAM_INIT = 0.8 - 0.6 * math.exp(-0.3 * 0)


class Buf:
    __slots__ = ("name",)

    def __init__(s, name):
        s.name = name


class Prog:
    COMPUTE = ("pe", "act", "dve", "pool")

    def __init__(s):
        s.ops = []
        s.lastw = {}
        s.rd_eng = {}
        s.rd_dma = {}
        s.dmacnt = {}
        s.bar = None
        s.bar_done = set()

    def op(s, eng, fn, r=(), w=(), dma=None):
        i = len(s.ops)
        deps = set()
        for b in tuple(r) + tuple(w):
            lw = s.lastw.get(b)
            if lw is not None:
                deps.add(lw)
        for b in w:
            for x in s.rd_eng.get(b, {}).values():
                deps.add(x)
            for x in s.rd_dma.get(b, ()):
                deps.add(x)
        o = dict(eng=eng, fn=fn, deps=deps, dma=dma, mark=False, bar=None)
        if s.bar is not None and eng not in s.bar_done:
            o["bar"] = s.bar
            s.bar_done.add(eng)
        if dma is not None:
            s.dmacnt[dma] = s.dmacnt.get(dma, 0) + 1
            o["dval"] = 16 * s.dmacnt[dma]
        s.ops.append(o)
        for b in w:
            s.lastw[b] = i
            s.rd_eng[b] = {}
            s.rd_dma[b] = []
        for b in r:
            if dma is not None:
                s.rd_dma.setdefault(b, []).append(i)
            else:
                s.rd_eng.setdefault(b, {})[eng] = i
        return i

    def barrier(s):
        last = {}
        for i, o in enumerate(s.ops):
            if o["dma"] is None:
                last[o["eng"]] = i
        s.bar = (dict(last), dict(s.dmacnt))
        s.bar_done = set()

    def emit(s, nc, block):
        ops = s.ops
        for o in ops:
            for d in o["deps"]:
                if ops[d]["dma"] is None:
                    ops[d]["mark"] = True
            if o["bar"] is not None:
                for e, d in o["bar"][0].items():
                    ops[d]["mark"] = True
        cnt = {e: 0 for e in ("pe", "act", "dve", "pool", "sp")}
        for o in ops:
            if o["dma"] is None:
                if o["mark"]:
                    cnt[o["eng"]] += 1
                o["sval"] = cnt[o["eng"]]
        stack = contextlib.ExitStack()
        esem = {e: stack.enter_context(nc.semaphore("s_" + e)) for e in cnt}
        dsem = {k: stack.enter_context(nc.semaphore("d_" + str(k))) for k in s.dmacnt}
        byeng = {e: [] for e in cnt}
        for o in ops:
            byeng[o["eng"]].append(o)

        def run(eng_name, E):
            known = {}

            def need(sem, val):
                if known.get(sem.name if hasattr(sem, "name") else id(sem), 0) < val:
                    E.wait_ge(sem, val)
                    known[sem.name if hasattr(sem, "name") else id(sem)] = val

            for o in byeng[eng_name]:
                if o["bar"] is not None:
                    lastc, dcnt = o["bar"]
                    for e, d in lastc.items():
                        if e != eng_name or e != "pe":
                            need(esem[e], ops[d]["sval"])
                    for k, c in dcnt.items():
                        need(dsem[k], 16 * c)
                for d in sorted(o["deps"]):
                    do = ops[d]
                    if do["dma"] is not None:
                        need(dsem[do["dma"]], do["dval"])
                    else:
                        if do["eng"] == "pe" and eng_name == "pe":
                            continue
                        need(esem[do["eng"]], do["sval"])
                ins = o["fn"](E)
                if o["dma"] is not None:
                    ins.then_inc(dsem[o["dma"]], 16)
                elif o["mark"]:
                    ins.then_inc(esem[eng_name], 1)
            if eng_name == "sp":
                for k, c in s.dmacnt.items():
                    need(dsem[k], 16 * c)

        @block.tensor
        def _(E):
            run("pe", E)

        @block.scalar
        def _(E):
            run("act", E)

        @block.vector
        def _(E):
            run("dve", E)

        @block.gpsimd
        def _(E):
            run("pool", E)

        @block.sync
        def _(E):
            run("sp", E)

        return stack


def slopes(cfg):
    return [2.0 ** (-8.0 * (h + 1) / cfg.H) for h in range(cfg.H)]


def build(cfg):
    D, S, T, H, KC, DFF, FC = cfg.D, cfg.S, cfg.T, cfg.H, cfg.KC, cfg.DFF, cfg.FC
    NKT, NTALL, NT, NQB, TB = cfg.NKT, cfg.NTALL, cfg.NT, cfg.NQB, cfg.TB
    nc = bass.Bass("TRN2", target_bir_lowering=False)
    P = Prog()

    def din(name, shape, dt=F32):
        return nc.dram_tensor(name, list(shape), dt, kind="ExternalInput")

    x_rot = din("x_rot", [S, D])
    meta = din("meta", [N_META, D])
    gvec = din("gvec", [3, D])
    w_in = din("w_in", [D, 9 * D])
    w_br = din("w_br", [D, D])
    w_bd = din("w_bd", [D, D])
    w_out = din("w_out", [D, D])
    w_g = din("w_g", [D, DFF])
    w_u = din("w_u", [D, DFF])
    w_d = din("w_d", [DFF, D])
    rdecay = din("rdecay", [2, H])
    dlam = din("dlam", [4, 128])
    subln = din("subln", [256])
    c_absd = din("c_absd", [NQB, 128, NKT])
    c_sgn = din("c_sgn", [NQB, 2, NKT * 128], BF16)
    c_rb = din("c_rb", [H, 2, 512], BF16)
    c_babs = din("c_babs", [4, 128, 512])
    c_dist = din("c_dist", [2, 128, NKT])
    c_sq = din("c_sq", [4, 128, 128])
    c_kd = din("c_kd", [128, 2])
    c_ident = din("c_ident", [128, 128], BF16)
    y = nc.dram_tensor("y", [T, D], F32, kind="ExternalOutput")

    def dscr(name, shape, dt=BF16):
        return nc.dram_tensor(name, list(shape), dt)

    wb_in = {g: dscr("wb_in_" + g, [D, D]) for g in ("rq", "rk", "rv", "rg", "dq", "dk", "dv", "ga", "gb")}; wb_br = dscr("wb_br", [D, D]); wb_bd = dscr("wb_bd", [D, D])
    wb_out = dscr("wb_out", [D, D]); wb_g = dscr("wb_g", [D, DFF]); wb_u = dscr("wb_u", [D, DFF])
    wb_d = dscr("wb_d", [DFF, D])
    NR = NKT * 128
    rKtm = dscr("rKtm", [NR, D]); rVtm = dscr("rVtm", [NR, D]); dVtm = dscr("dVtm", [NR, D])
    dKT = dscr("dKT", [D, NR])
    rKT = dscr("rKT", [D, T]); rQT = dscr("rQT", [D, T]); rGT = dscr("rGT", [D, T]); dQT = dscr("dQT", [D, T])
    gaT = dscr("gaT", [D, T]); gbT = dscr("gbT", [D, T]); ZrT = dscr("ZrT", [D, T]); ZdT = dscr("ZdT", [D, T])
    h1 = dscr("h1", [T, D], F32)

    stack = contextlib.ExitStack()
    NB16 = 78 * 1024
    NF32 = 11 * 1024
    A16 = stack.enter_context(nc.sbuf_tensor("A16", [128, NB16], BF16))
    A32 = stack.enter_context(nc.sbuf_tensor("A32", [128, NF32], F32))
    ident = stack.enter_context(nc.sbuf_tensor("ident", [128, 128], BF16))
    ones32 = stack.enter_context(nc.sbuf_tensor("ones32", [128, 128], F32))
    ones16 = stack.enter_context(nc.sbuf_tensor("ones16", [128, 128], BF16))
    gT = stack.enter_context(nc.sbuf_tensor("gT", [128, 3, KC], F32))
    csq = stack.enter_context(nc.sbuf_tensor("csq", [128, 4, 128], F32))
    ckd = stack.enter_context(nc.sbuf_tensor("ckd", [128, 2], F32))
    sm = stack.enter_context(nc.sbuf_tensor("sm", [128, 64], F32))
    epsc = stack.enter_context(nc.sbuf_tensor("epsc", [128, 2], F32))
    trigt = stack.enter_context(nc.sbuf_tensor("trigt", [128, 2], F32))
    PS = [stack.enter_context(nc.psum_tensor("ps%d" % i, [128, 512], F32)) for i in range(8)]
    PB = [Buf("ps%d" % i) for i in range(8)]
    B_const = Buf("const")

    class Arena:
        def __init__(s):
            s.o16 = 0; s.o32 = 0

        def a16(s, n, shape=None):
            v = A16[:, s.o16:s.o16 + n]; s.o16 += n
            assert s.o16 <= NB16, ("bf16 arena overflow", s.o16)
            return v

        def a32(s, n):
            v = A32[:, s.o32:s.o32 + n]; s.o32 += n
            assert s.o32 <= NF32, ("f32 arena overflow", s.o32)
            return v

    evq = [0]

    def ev_eng():
        evq[0] += 1
        return "act" if evq[0] % 2 else "dve"

    def copy_op(eng, out, in_, r, w, scale=None):
        if eng == "act":
            if scale is None:
                P.op("act", lambda E: E.activation(out=out, in_=in_, func=AF.Copy), r=r, w=w)
            else:
                P.op("act", lambda E: E.activation(out=out, in_=in_, func=AF.Copy, scale=scale), r=r, w=w)
        else:
            if scale is None:
                P.op("dve", lambda E: E.tensor_copy(out=out, in_=in_), r=r, w=w)
            else:
                P.op("dve", lambda E: E.tensor_scalar(out=out, in0=in_, scalar1=scale, scalar2=None, op0=ALU.mult), r=r, w=w)

    ckey = [0]

    def dma(out, in_, r, w, key, eng="sp"):
        if key == "c0":
            ckey[0] += 1
            key = "c%d" % ckey[0]
        P.op(eng, lambda E: E.dma_start(out=out, in_=in_), r=r, w=w, dma=key)

    B_x = Buf("x_in"); B_w = Buf("w_f32")
    dma(ident[:], c_ident.ap(), [B_x], [B_const], "c0")
    dma(csq[:], c_sq.ap().rearrange("a p f -> p a f"), [B_x], [B_const], "c0")
    dma(ckd[:], c_kd.ap(), [B_x], [B_const], "c0")
    with nc.allow_non_contiguous_dma(reason="tiny gamma relayout"):
        dma(gT[:], gvec.ap().rearrange("a (kc p) -> p a kc", p=128), [B_x], [B_const], "c0")
    P.op("dve", lambda E: E.memset(ones32[:], 1.0), w=[B_const])
    P.op("dve", lambda E: E.memset(ones16[:], 1.0), w=[B_const])
    P.op("dve", lambda E: E.memset(epsc[:], EPS), w=[B_const])

    WB = {}

    pending_casts = []
    pcnt = [0]

    def cast_w(name, src, dst, c0, c1, rows, d0=None, defer=False):
        if name not in WB:
            WB[name] = Buf("wb_" + name)
        if defer:
            pending_casts.append((name, src, dst, c0, c1, rows, d0))
            return
        b = WB[name]
        d0 = c0 if d0 is None else d0
        RB = 512 if (c1 - c0) <= 4096 else 128
        trig = Buf("trig_" + name)
        P.op("dve", lambda E: E.memset(trigt[:, 0:1], 0.0), w=[trig])
        for r0 in range(0, rows, RB):
            r1 = min(rows, r0 + RB)
            P.op("pool", lambda E, r0=r0, r1=r1: E.dma_start(out=dst.ap()[r0:r1, d0:d0 + c1 - c0], in_=src.ap()[r0:r1, c0:c1]),
                 r=[B_w, trig], w=[b], dma="cw_" + name)

    def next_cast():
        if pending_casts:
            a = pending_casts.pop(0)
            cast_w(*a, defer=False)

    GR = {"rq": 0, "rk": 1, "rv": 2, "rg": 3, "dq": 4, "dk": 5, "dv": 6, "ga": 7, "gb": 8}
    for g in ("rk", "rv", "dv", "dk", "rq", "dq", "rg", "ga", "gb"):
        cast_w(g, w_in, wb_in[g], GR[g] * D, (GR[g] + 1) * D, D, d0=0, defer=(g not in ("rk", "rv", "dv", "dk")))
    cast_w("br", w_br, wb_br, 0, D, D, defer=True); cast_w("bd", w_bd, wb_bd, 0, D, D, defer=True)
    cast_w("out", w_out, wb_out, 0, D, D, defer=True)
    cast_w("g", w_g, wb_g, 0, DFF, D, defer=True); cast_w("u", w_u, wb_u, 0, DFF, D, defer=True)
    cast_w("d", w_d, wb_d, 0, D, DFF, defer=True)

    psrr = {}

    def next_ps(lo=0, hi=8):
        k = (lo, hi)
        i = psrr.get(k, lo); psrr[k] = lo + ((i - lo + 1) % (hi - lo))
        return i

    def norm_T(ar, src_rows_ap, np_, gi, uT, uTb, tcol, xt, xtb, ub, ubb, ssc, src_buf=None):
        src_buf = src_buf or B_x
        dma(xt[:np_, :], src_rows_ap, [src_buf], [xtb], "xt")
        P.op("dve", lambda E: E.memset(ssc[:, 0:1], 0.0), w=[B_ss])
        P.op("act", lambda E: E.activation(out=ub[:np_, :], in_=xt[:np_, :], func=AF.Square, accum_out=ssc[:np_, 0:1]),
             r=[xtb], w=[ubb, B_ss])
        P.op("act", lambda E: E.activation(out=ssc[:np_, 1:2], in_=ssc[:np_, 0:1], func=AF.Ln, scale=1.0 / D, bias=epsc[:np_, 0:1]),
             r=[B_ss, B_const], w=[B_ss])
        P.op("act", lambda E: E.activation(out=ssc[:np_, 2:3], in_=ssc[:np_, 1:2], func=AF.Exp, scale=-0.5), r=[B_ss], w=[B_ss])
        P.op("act", lambda E: E.activation(out=ub[:np_, :], in_=xt[:np_, :], func=AF.Copy, scale=ssc[:np_, 2:3]),
             r=[xtb, B_ss], w=[ubb])
        for k0 in range(0, KC, 8):
            nk = min(8, KC - k0)
            pi = next_ps()
            pv = PS[pi][:].bitcast(BF16)
            for j in range(nk):
                P.op("pe", lambda E, j=j, k0=k0, pv=pv: E.transpose(out=pv[:, j * 128:j * 128 + np_],
                                                                   in_=ub[:np_, (k0 + j) * 128:(k0 + j + 1) * 128],
                                                                   identity=ident[:np_, :np_]),
                     r=[ubb, B_const], w=[PB[pi]])
            src = pv[:, 0:nk * 128].rearrange("p (k t) -> p k t", t=128)[:, :, 0:np_]
            gsl = gT[:, gi, k0:k0 + nk]
            P.op("dve", lambda E, src=src, gsl=gsl, k0=k0, nk=nk: E.tensor_tensor(
                out=uT[:, k0:k0 + nk, tcol:tcol + np_], in0=src,
                in1=gsl.unsqueeze(2).to_broadcast([128, nk, np_]), op=ALU.mult),
                r=[PB[pi], B_const], w=[uTb])

    B_ss = Buf("ss")

    def proj(actT, actb, ntok, wdram, wbuf, K, c0, ncols, mode, evac, wt, wtb, WC):
        nkc = K // 128
        kgs = [(k, min(nkc, k + 32)) for k in range(0, nkc, 32)]
        wr = [0]
        for cb0 in range(0, ncols, WC):
            wc = min(WC, ncols - cb0)
            tiles = []

            def load(kg):
                i = wr[0] % len(wt); wr[0] += 1
                ka, kb = kgs[kg]
                for k8 in range(ka, kb, 8):
                    k9 = min(kb, k8 + 8)
                    dma(wt[i][:, k8 - ka:k9 - ka, 0:wc],
                        wdram.ap()[k8 * 128:k9 * 128, c0 + cb0:c0 + cb0 + wc].rearrange("(k p) c -> p k c", p=128),
                        [wbuf], [wtb[i]], "ld_" + wtb[i].name)
                return i
            if mode == "tm":
                ntt = (ntok + 127) // 128
                for th0 in range(0, ntt, 4):
                    tts = list(range(th0, min(ntt, th0 + 4)))
                    banks = {tt: next_ps() for tt in tts}
                    for kg, (ka, kb) in enumerate(kgs):
                        if th0 == 0 or len(kgs) > 1:
                            wi = load(kg)
                            if len(kgs) == 1:
                                tiles = [wi]
                        else:
                            wi = tiles[0]
                        for tt in tts:
                            n = min(128, ntok - tt * 128)
                            for kc in range(ka, kb):
                                P.op("pe", lambda E, tt=tt, n=n, kc=kc, ka=ka, wi=wi, b=banks[tt]: E.matmul(
                                    PS[b][:n, 0:wc], lhsT=actT[:, kc, tt * 128:tt * 128 + n], rhs=wt[wi][:, kc - ka, 0:wc],
                                    start=(kc == 0), stop=(kc == nkc - 1)), r=[actb, wtb[wi]], w=[PB[banks[tt]]])
                    for tt in tts:
                        n = min(128, ntok - tt * 128)
                        evac(banks[tt], PS[banks[tt]][:n, 0:wc], tt * 128, n, cb0, wc)
            else:
                wi = load(0)
                for cc in range(0, wc, 128):
                    for tg in range(0, ntok, 512):
                        n = min(512, ntok - tg)
                        b = next_ps()
                        for kc in range(nkc):
                            P.op("pe", lambda E, kc=kc, cc=cc, tg=tg, n=n, b=b, wi=wi: E.matmul(
                                PS[b][:, 0:n], lhsT=wt[wi][:, kc, cc:cc + 128], rhs=actT[:, kc, tg:tg + n],
                                start=(kc == 0), stop=(kc == nkc - 1)), r=[actb, wtb[wi]], w=[PB[b]])
                        evac(b, PS[b][:, 0:n], tg, n, cb0 + cc, 128)

    SL = slopes(cfg)

    def key_tiles(h):
        R = 60.0 / SL[h]
        r = int(math.ceil(R / 128.0))
        nother = NTALL - NT
        if 2 * r >= nother:
            oth = list(range(NT, NTALL))
        else:
            oth = list(range(NT, NT + r)) + list(range(NTALL - r, NTALL))
        return list(range(NT)) + oth + [NTALL]

    ar = Arena()
    uT = ar.a16(KC * TB).rearrange("p (k t) -> p k t", t=TB); uTb = Buf("uT")
    WC1 = 512
    wt = [ar.a16(32 * WC1).rearrange("p (k c) -> p k c", c=WC1) for _ in range(2)]
    wtb = [Buf("wt0"), Buf("wt1")]
    ub = ar.a16(D); ubb = Buf("ub")
    ost = [ar.a16(512) for _ in range(4)]; ostb = [Buf("ost%d" % i) for i in range(4)]
    xt = ar.a32(D); xtb = Buf("xt")
    osr = [0]
    B_scr = {n: Buf("scr_" + n) for n in ("rKtm", "rVtm", "dVtm", "dKT", "rKT", "rQT", "rGT", "dQT", "gaT", "gbT", "ZrT", "ZdT", "h1", "y")}

    def mk_evac(dst, dstb, tm, row0, scale=None, func=None):
        def evac(b, pap, t0, n, c0, ncol):
            i = osr[0] % 4; osr[0] += 1
            if tm:
                o = ost[i][:n, 0:ncol]
            else:
                o = ost[i][:, 0:n]
            if func is not None:
                P.op("act", lambda E: E.activation(out=o, in_=pap, func=func), r=[PB[b]], w=[ostb[i]])
            else:
                copy_op(ev_eng(), o, pap, [PB[b]], [ostb[i]], scale)
            if tm:
                dma(dst.ap()[row0 + t0:row0 + t0 + n, c0:c0 + ncol], o, [ostb[i]], [dstb], "st_" + dstb.name, eng="act")
            else:
                dma(dst.ap()[c0:c0 + ncol, row0 + t0:row0 + t0 + n], o, [ostb[i]], [dstb], "st_" + dstb.name, eng="act")
        return evac

    blocks = [(r0, TB) for r0 in range(0, S, TB)] + [(S, N_META)]

    def stage1_block(r0, ntok):
        own = r0 < T
        for t0 in range(0, ntok, 128):
            np_ = min(128, ntok - t0)
            src = (x_rot.ap()[r0 + t0:r0 + t0 + np_, :] if r0 < S else meta.ap()[0:np_, :])
            norm_T(ar, src, np_, 0, uT, uTb, t0, xt, xtb, ub, ubb, sm)
        def P_(g, dst, tm, scale=None, func=None, row0=r0, cmin=0):
            pcnt[0] += 1
            if pcnt[0] % 2 == 0 and len(pending_casts) > 6:
                next_cast()
            ev0 = mk_evac(dst, B_scr[dst.name], tm, row0, scale, func)
            ev = ev0 if cmin == 0 else (lambda b, pap, t0, n, cc, ncl: ev0(b, pap, t0, n, cc + cmin, ncl))
            proj(uT, uTb, ntok, wb_in[g], WB[g], D, cmin, D - cmin, "tm" if tm else "fm", ev, wt, wtb, WC1)
        P_("rk", rKtm, True, scale=256.0 ** -0.5)
        P_("rv", rVtm, True)
        cmin = 0
        if r0 >= T and r0 < S:
            tiles = set(range(r0 // 128, (r0 + ntok) // 128))
            need = [h for h in range(H) if tiles & set(key_tiles(h))]
            cmin = ((min(need) * 256) // WC1) * WC1 if need else 0
        P_("dv", dVtm, True, cmin=cmin)
        P_("dk", dKT, False, cmin=cmin)
        if own:
            P_("rk", rKT, False, scale=256.0 ** -0.5)
            P_("rq", rQT, False)
            P_("dq", dQT, False, scale=128.0 ** -0.5)
            P_("rg", rGT, False, func=AF.Silu)
            P_("ga", gaT, False, func=AF.Sigmoid)
            P_("gb", gbT, False, func=AF.Sigmoid)
    blocks = [b_ for b_ in blocks if b_[0] >= T] + [b_ for b_ in blocks if b_[0] < T]
    for (r0_, ntok_) in blocks:
        stage1_block(r0_, ntok_)
    while len(pending_casts) > 6:
        next_cast()
    P.barrier()

    ar = Arena()
    rK = ar.a16(NKT * 256).rearrange("p (t c) -> p t c", c=256); rKb = Buf("rK")
    rV = ar.a16(NKT * 256).rearrange("p (t c) -> p t c", c=256); rVb = Buf("rV")
    kT = ar.a16(2 * T).rearrange("p (k t) -> p k t", t=T); kTb = Buf("kT")
    qT = ar.a16(2 * T).rearrange("p (k t) -> p k t", t=T); qTb = Buf("qT")
    gTt = ar.a16(2 * T).rearrange("p (k t) -> p k t", t=T); gTb = Buf("gTt")
    qf = ar.a16(2 * T).rearrange("p (k t) -> p k t", t=T); qfb = Buf("qf")
    qb_ = ar.a16(2 * T).rearrange("p (k t) -> p k t", t=T); qbb = Buf("qb")
    Sbs = ar.a16(NT * 512).rearrange("p (c k v) -> p c k v", k=2, v=256); Sbsb = Buf("Sbs")
    Sf16 = ar.a16(512).rearrange("p (k v) -> p k v", v=256); Sf16b = Buf("Sf16")
    kw = [ar.a16(256) for _ in range(2)]; kwb = [Buf("kw0"), Buf("kw1")]
    aT = [ar.a16(128) for _ in range(2)]; aTb = [Buf("aT0"), Buf("aT1")]
    zo = [ar.a16(512) for _ in range(2)]; zob = [Buf("zo0"), Buf("zo1")]
    Mh = ar.a32(128); Mhb = Buf("Mh")
    Mt = ar.a32(128)
    QD = ar.a32(256).rearrange("p (a t) -> p a t", t=128); QDb = Buf("QD")
    wfb_ = ar.a32(2 * NKT).rearrange("p (a t) -> p a t", t=NKT); wfbb = Buf("wfb")
    cdist = ar.a32(2 * NKT).rearrange("p (a t) -> p a t", t=NKT)
    kdc = ar.a32(4); kdcb = Buf("kdc")
    Sm = ar.a32(1024).rearrange("p (d k v) -> p d k v", k=2, v=256); Smb = [Buf("Smf"), Buf("Smb")]
    osq = ar.a32(1024).rearrange("p (k t) -> p k t", t=512); osqb = Buf("osq")
    orr = ar.a32(512); orrb = Buf("orr")
    ld_raw = ar.a32(2 * H).rearrange("p (a h) -> p a h", h=H); ldb = Buf("ld")
    dma(cdist[:], c_dist.ap().rearrange("a p t -> p a t"), [B_x], [B_const], "c0")
    dma(ld_raw[:].rearrange("p a h -> p (a h)"), rdecay.ap().rearrange("a h -> (a h)").partition_broadcast(128), [B_x], [ldb], "c0")
    P.op("act", lambda E: E.activation(out=ld_raw[:], in_=ld_raw[:], func=AF.Exp), r=[ldb], w=[ldb])
    P.op("dve", lambda E: E.tensor_scalar(out=ld_raw[:], in0=ld_raw[:], scalar1=-1.0, scalar2=None, op0=ALU.mult),
         r=[ldb], w=[ldb])

    def tile_np(kt):
        return N_META if kt == NKT - 1 else 128

    def flat(a):
        return a.rearrange("p k v -> p (k v)")

    def ret_upd_state(d, c):
        i = (c + d) % 2
        P.op("dve", lambda E: E.tensor_scalar(out=kw[i][:, :], in0=rK[:, c, :], scalar1=kdc[:, d:d + 1], scalar2=None,
                                               op0=ALU.mult), r=[rKb, kdcb], w=[kwb[i]])
        b = next_ps(4, 8)
        for dk in range(2):
            P.op("pe", lambda E, dk=dk: E.matmul(PS[b][:, dk * 256:(dk + 1) * 256], lhsT=kw[i][:, dk * 128:(dk + 1) * 128],
                                                 rhs=rV[:, c, :], start=(dk == 0), stop=(dk == 1)),
                 r=[kwb[i], rVb], w=[PB[b]])
        P.op("dve", lambda E: E.scalar_tensor_tensor(
            out=flat(Sm[:, d, :, :]), in0=flat(Sm[:, d, :, :]),
            scalar=kdc[:, 2 + d:3 + d], in1=PS[b][:, :], op0=ALU.mult, op1=ALU.add),
            r=[Smb[d], PB[b], kdcb], w=[Smb[d]])

    def ret_chunk(h, c, c4, po):
        cs = slice(c * 128, (c + 1) * 128)
        P.op("act", lambda E: E.activation(out=flat(Sf16[:]), in_=flat(Sm[:, 0, :, :]), func=AF.Copy),
             r=[Smb[0]], w=[Sf16b])
        bs = next_ps(4, 8)
        for dk in range(2):
            P.op("pe", lambda E, dk=dk: E.matmul(PS[bs][:, 0:128], lhsT=kT[:, dk, cs], rhs=qT[:, dk, cs],
                                                 start=(dk == 0), stop=(dk == 1)), r=[kTb, qTb], w=[PB[bs]])
        ia = c % 2
        P.op("dve", lambda E: E.tensor_tensor(out=aT[ia][:], in0=PS[bs][:, 0:128], in1=Mh[:], op=ALU.mult),
             r=[PB[bs], Mhb], w=[aTb[ia]])
        for vh in range(2):
            vs = slice(vh * 128, (vh + 1) * 128)
            oc = slice((c - c4) * 128, (c - c4 + 1) * 128)
            seq = [(rV[:, c, vs], aT[ia][:], [rVb, aTb[ia]])]
            for dk in range(2):
                seq.append((Sf16[:, dk, vs], qf[:, dk, cs], [Sf16b, qfb]))
                seq.append((Sbs[:, c, dk, vs], qb_[:, dk, cs], [Sbsb, qbb]))
            ns = len(seq)
            for j, (l, r_, rb_) in enumerate(seq):
                P.op("pe", lambda E, l=l, r_=r_, j=j, vh=vh, oc=oc: E.matmul(
                    PS[po[vh]][:, oc], lhsT=l, rhs=r_, start=(j == 0), stop=(j == ns - 1)),
                    r=rb_, w=[PB[po[vh]]])
        if c < NT - 1:
            ret_upd_state(0, c)

    def ret_group(h, c4):
        po = [2, 3]
        c5 = min(NT, c4 + 4)
        for c in range(c4, c5):
            ret_chunk(h, c, c4, po)
        n = (c5 - c4) * 128
        ts_ = slice(c4 * 128, c4 * 128 + n)
        for vh in range(2):
            P.op("act", lambda E, vh=vh: E.activation(out=osq[:, vh, 0:n], in_=PS[po[vh]][:, 0:n], func=AF.Square),
                 r=[PB[po[vh]]], w=[osqb])
        br = next_ps(4, 8)
        for vh in range(2):
            P.op("pe", lambda E, vh=vh: E.matmul(PS[br][:, 0:n], lhsT=ones32[:], rhs=osq[:, vh, 0:n],
                                                 start=(vh == 0), stop=(vh == 1)), r=[osqb, B_const], w=[PB[br]])
        P.op("act", lambda E: E.activation(out=orr[:, 0:n], in_=PS[br][:, 0:n], func=AF.Ln, scale=1.0 / 256, bias=epsc[:, 0:1]),
             r=[PB[br], B_const], w=[orrb])
        P.op("act", lambda E: E.activation(out=orr[:, 0:n], in_=orr[:, 0:n], func=AF.Exp, scale=-0.5), r=[orrb], w=[orrb])
        for vh in range(2):
            P.op("dve", lambda E, vh=vh: E.tensor_tensor(out=osq[:, vh, 0:n], in0=PS[po[vh]][:, 0:n], in1=orr[:, 0:n],
                                                         op=ALU.mult), r=[PB[po[vh]], orrb], w=[osqb])
            P.op("dve", lambda E, vh=vh: E.tensor_tensor(out=zo[vh][:, 0:n], in0=osq[:, vh, 0:n], in1=gTt[:, vh, ts_],
                                                         op=ALU.mult), r=[osqb, gTb], w=[zob[vh]])
            dma(ZrT.ap()[h * 256 + vh * 128:h * 256 + (vh + 1) * 128, ts_], zo[vh][:, 0:n], [zob[vh]], [B_scr["ZrT"]], "st_ZrT", eng="act")

    def ret_head(h):
        hc = slice(h * 256, (h + 1) * 256)
        for t8 in range(0, NTALL, 16):
            t9 = min(NTALL, t8 + 16)
            dma(rK[:, t8:t9, :], rKtm.ap()[t8 * 128:t9 * 128, hc].rearrange("(t p) c -> p t c", p=128), [B_scr["rKtm"]], [rKb], "rK")
        dma(rK[:N_META, NTALL, :], rKtm.ap()[S:S + N_META, hc], [B_scr["rKtm"]], [rKb], "rK")
        for t8 in range(0, NTALL, 16):
            t9 = min(NTALL, t8 + 16)
            dma(rV[:, t8:t9, :], rVtm.ap()[t8 * 128:t9 * 128, hc].rearrange("(t p) c -> p t c", p=128), [B_scr["rVtm"]], [rVb], "rV")
        dma(rV[:N_META, NTALL, :], rVtm.ap()[S:S + N_META, hc], [B_scr["rVtm"]], [rVb], "rV")
        for (dst, dstb, srcT, nm) in ((kT, kTb, rKT, "rKT"), (qT, qTb, rQT, "rQT"), (gTt, gTb, rGT, "rGT")):
            dma(dst[:], srcT.ap()[hc, :].rearrange("(k p) t -> p k t", p=128), [B_scr[nm]], [dstb], "r" + nm)
        for d in range(2):
            lg = ld_raw[:, d, h:h + 1]
            P.op("act", lambda E, d=d, lg=lg: E.activation(out=wfb_[:, d, :], in_=cdist[:, d, :], func=AF.Exp, scale=lg),
                 r=[B_const, ldb], w=[wfbb])
            P.op("act", lambda E, d=d, lg=lg: E.activation(out=QD[:, d, :], in_=csq[:, 2 + d, :], func=AF.Exp, scale=lg),
                 r=[B_const, ldb], w=[QDb])
            P.op("act", lambda E, d=d, lg=lg: E.activation(out=kdc[:, d:d + 1], in_=ckd[:, d:d + 1], func=AF.Exp, scale=lg),
                 r=[B_const, ldb], w=[kdcb])
            P.op("act", lambda E, d=d, lg=lg: E.activation(out=kdc[:, 2 + d:3 + d], in_=lg, func=AF.Exp, scale=128.0),
                 r=[B_const, ldb], w=[kdcb])
        P.op("act", lambda E: E.activation(out=Mh[:], in_=csq[:, 0, :], func=AF.Exp, scale=ld_raw[:, 0, h:h + 1]),
             r=[B_const, ldb], w=[Mhb])
        P.op("act", lambda E: E.activation(out=Mt[:], in_=csq[:, 1, :], func=AF.Exp, scale=ld_raw[:, 1, h:h + 1]),
             r=[B_const, ldb, Mhb], w=[Mhb])
        P.op("dve", lambda E: E.tensor_tensor(out=Mh[:], in0=Mh[:], in1=Mt[:], op=ALU.add), r=[Mhb], w=[Mhb])
        for d, (dst, dstb) in enumerate(((qf, qfb), (qb_, qbb))):
            P.op("dve", lambda E, d=d, dst=dst: E.tensor_tensor(
                out=dst[:].rearrange("p k (c i) -> p (k c) i", i=128),
                in0=qT[:].rearrange("p k (c i) -> p (k c) i", i=128),
                in1=QD[:, d, :].unsqueeze(1).to_broadcast([128, 2 * NT, 128]), op=ALU.mult),
                r=[qTb, QDb], w=[dstb])
        pin = [0, 1]
        for d in range(2):
            for kt in range(NT, NKT):
                np_ = tile_np(kt)
                i = (kt + d) % 2
                P.op("dve", lambda E, i=i, kt=kt, d=d, np_=np_: E.tensor_scalar(
                    out=kw[i][:np_, :], in0=rK[:np_, kt, :], scalar1=wfb_[:np_, d, kt:kt + 1], scalar2=None, op0=ALU.mult),
                    r=[rKb, wfbb], w=[kwb[i]])
                for dk in range(2):
                    P.op("pe", lambda E, i=i, kt=kt, d=d, dk=dk, np_=np_: E.matmul(
                        PS[pin[d]][:, dk * 256:(dk + 1) * 256], lhsT=kw[i][:np_, dk * 128:(dk + 1) * 128],
                        rhs=rV[:np_, kt, :], start=(kt == NT and dk == 0), stop=(kt == NKT - 1 and dk == 1),
                        skip_group_check=True),
                        r=[kwb[i], rVb], w=[PB[pin[d]]])
            P.op("dve", lambda E, d=d: E.tensor_copy(out=flat(Sm[:, d, :, :]), in_=PS[pin[d]][:, :]),
                 r=[PB[pin[d]]], w=[Smb[d]])
        for c in range(NT - 1, -1, -1):
            P.op("act", lambda E, c=c: E.activation(out=flat(Sbs[:, c, :, :]), in_=flat(Sm[:, 1, :, :]), func=AF.Copy),
                 r=[Smb[1]], w=[Sbsb])
            if c > 0:
                ret_upd_state(1, c)
        for c4 in range(0, NT, 4):
            ret_group(h, c4)

    for h in range(H):
        ret_head(h)
    P.barrier()

    ar = Arena()
    k1 = ar.a16(NKT * 128); k2 = ar.a16(NKT * 128); kkb = Buf("kk")
    vv = ar.a16(NKT * 256).rearrange("p (t c) -> p t c", c=256); vvb = Buf("vv")
    sg = [ar.a16(NKT * 128) for _ in range(2)]; sgb = [Buf("sg0"), Buf("sg1")]
    qq = [ar.a16(1024).rearrange("p (m t) -> p m t", t=512) for _ in range(2)]; qqb = [Buf("qq0"), Buf("qq1")]
    rbt = ar.a16(512); rbb = Buf("rb")
    ee = [ar.a16(512) for _ in range(4)]; eeb = [Buf("ee%d" % i) for i in range(4)]
    zd = [ar.a16(512) for _ in range(2)]; zdb = [Buf("zd0"), Buf("zd1")]
    babs = ar.a32(4 * 512).rearrange("p (a t) -> p a t", t=512)
    absd = ar.a32(NQB * NKT).rearrange("p (a t) -> p a t", t=NKT)
    bcol = ar.a32(NKT); bcolb = Buf("bcol")
    zacc = [ar.a32(512) for _ in range(2)]; zaccb = [Buf("zacc0"), Buf("zacc1")]
    sp_ = [ar.a32(512) for _ in range(2)]; spb = [Buf("sp0"), Buf("sp1")]
    o32 = ar.a32(1024).rearrange("p (k t) -> p k t", t=512); o32b = Buf("o32")
    t32 = ar.a32(1024).rearrange("p (k t) -> p k t", t=512); t32b = Buf("t32")
    rz = ar.a32(1024).rearrange("p (k t) -> p k t", t=512); rzb = Buf("rz")
    lam = ar.a32(8); lamb = Buf("lam")
    sgc = ar.a32(2); sgcb = Buf("sgc")
    lraw = ar.a32(512)
    dma(babs[:], c_babs.ap().rearrange("a p t -> p a t"), [B_x], [B_const], "c0")
    dma(absd[:], c_absd.ap().rearrange("a p t -> p a t"), [B_x], [B_const], "c0")
    dma(sgc[:], subln.ap().rearrange("(k p) -> p k", p=128), [B_x], [sgcb], "c0")
    dma(lraw[:], dlam.ap().rearrange("a f -> (a f)").partition_broadcast(128), [B_x], [lamb], "c0")
    P.op("dve", lambda E: E.tensor_tensor(out=lraw[:, 0:128], in0=lraw[:, 0:128], in1=lraw[:, 128:256], op=ALU.mult), r=[lamb], w=[lamb])
    P.op("dve", lambda E: E.tensor_tensor(out=lraw[:, 256:384], in0=lraw[:, 256:384], in1=lraw[:, 384:512], op=ALU.mult), r=[lamb], w=[lamb])
    P.op("dve", lambda E: E.reduce_sum(out=lam[:, 0:1], in_=lraw[:, 0:128], axis=mybir.AxisListType.X), r=[lamb], w=[lamb])
    P.op("dve", lambda E: E.reduce_sum(out=lam[:, 1:2], in_=lraw[:, 256:384], axis=mybir.AxisListType.X), r=[lamb], w=[lamb])
    P.op("act", lambda E: E.activation(out=lam[:, 2:4], in_=lam[:, 0:2], func=AF.Exp), r=[lamb], w=[lamb])
    P.op("dve", lambda E: E.tensor_tensor(out=lam[:, 4:5], in0=lam[:, 3:4], in1=lam[:, 2:3], op=ALU.subtract), r=[lamb], w=[lamb])
    P.op("dve", lambda E: E.tensor_scalar(out=lam[:, 4:5], in0=lam[:, 4:5], scalar1=-cfg.LAM_INIT, scalar2=None, op0=ALU.add), r=[lamb], w=[lamb])
    P.op("dve", lambda E: E.tensor_scalar(out=sgc[:], in0=sgc[:], scalar1=1.0 - cfg.LAM_INIT, scalar2=None, op0=ALU.mult),
         r=[sgcb], w=[sgcb])

    def diff_S(h, qb, qi, kt, pos):
        np_ = tile_np(kt)
        diag = (qb * 4 <= kt < qb * 4 + 4)
        for m, kk in enumerate((k1, k2)):
            bS = next_ps(4, 8)
            ie = (2 * pos + m) % 4
            P.op("pe", lambda E, kk=kk, bS=bS, m=m: E.matmul(
                PS[bS][:np_, :], lhsT=kk[:, kt * 128:kt * 128 + np_], rhs=qq[qi][:, m, :], start=True, stop=diag),
                r=[kkb, qqb[qi]], w=[PB[bS]])
            if not diag:
                P.op("pe", lambda E, bS=bS: E.matmul(
                    PS[bS][:np_, :], lhsT=sg[qi][0:2, kt * 128:kt * 128 + np_], rhs=rbt[0:2, :], start=False, stop=True),
                    r=[sgb[qi], rbb], w=[PB[bS]])
                P.op("act", lambda E, bS=bS, ie=ie: E.activation(
                    out=ee[ie][:np_, :], in_=PS[bS][:np_, :], func=AF.Exp, bias=bcol[:np_, kt:kt + 1]),
                    r=[PB[bS], bcolb], w=[eeb[ie]])
            else:
                ci = kt - qb * 4
                P.op("dve", lambda E, bS=bS, m=m: E.scalar_tensor_tensor(
                    out=sp_[m][:], in0=babs[:, ci, :], scalar=-SL[h], in1=PS[bS][:, :], op0=ALU.mult, op1=ALU.add),
                    r=[PB[bS], B_const], w=[spb[m]])
                P.op("act", lambda E, m=m, ie=ie: E.activation(out=ee[ie][:], in_=sp_[m][:], func=AF.Exp),
                     r=[spb[m]], w=[eeb[ie]])

    def diff_AV(h, qb, qi, kt, pO, pos, first, last):
        np_ = tile_np(kt)
        for m in range(2):
            ie = (2 * pos + m) % 4
            if first:
                P.op("dve", lambda E, m=m, ie=ie: E.tensor_copy(out=zacc[m][:], in_=ee[ie][:]), r=[eeb[ie]], w=[zaccb[m]])
            else:
                P.op("dve", lambda E, m=m, ie=ie: E.tensor_tensor(out=zacc[m][:np_, :], in0=zacc[m][:np_, :],
                                                                 in1=ee[ie][:np_, :], op=ALU.add),
                     r=[eeb[ie], zaccb[m]], w=[zaccb[m]])
            for vh in range(2):
                P.op("pe", lambda E, vh=vh, m=m, ie=ie: E.matmul(
                    PS[pO[m][vh]][:, :], lhsT=vv[:np_, kt, vh * 128:(vh + 1) * 128], rhs=ee[ie][:np_, :],
                    start=first, stop=last), r=[vvb, eeb[ie]], w=[PB[pO[m][vh]]])

    def diff_iter(h, qb, qi):
        q0 = qb * 512
        dma(qq[qi][:], dQT.ap()[h * 256:(h + 1) * 256, q0:q0 + 512].rearrange("(m p) t -> p m t", p=128),
            [B_scr["dQT"]], [qqb[qi]], "qq%d" % qi)
        dma(sg[qi][0:2, :], c_sgn.ap()[qb], [B_x], [sgb[qi]], "sg%d" % qi)
        P.op("dve", lambda E: E.tensor_scalar(out=bcol[:], in0=absd[:, qb, :], scalar1=-SL[h], scalar2=None,
                                               op0=ALU.mult), r=[B_const], w=[bcolb])
        pO = [[0, 1], [2, 3]]
        tl = key_tiles(h)
        for i in range(len(tl) + 1):
            if i < len(tl):
                diff_S(h, qb, qi, tl[i], i)
            if i >= 1:
                diff_AV(h, qb, qi, tl[i - 1], pO, i - 1, i - 1 == 0, i - 1 == len(tl) - 1)
        for m in range(2):
            bz = next_ps(4, 8)
            P.op("pe", lambda E, m=m, bz=bz: E.matmul(PS[bz][:, :], lhsT=ones32[:], rhs=zacc[m][:], start=True, stop=True),
                 r=[zaccb[m], B_const], w=[PB[bz]])
            P.op("dve", lambda E, m=m, bz=bz: E.reciprocal(out=rz[:, m, :], in_=PS[bz][:, :]), r=[PB[bz]], w=[rzb])
        for vh in range(2):
            P.op("dve", lambda E, vh=vh: E.tensor_tensor(out=t32[:, vh, :], in0=PS[pO[1][vh]][:, :], in1=rz[:, 1, :], op=ALU.mult),
                 r=[PB[pO[1][vh]], rzb], w=[t32b])
            P.op("dve", lambda E, vh=vh: E.tensor_tensor(out=o32[:, vh, :], in0=PS[pO[0][vh]][:, :], in1=rz[:, 0, :], op=ALU.mult),
                 r=[PB[pO[0][vh]], rzb], w=[o32b])
            P.op("dve", lambda E, vh=vh: E.scalar_tensor_tensor(out=o32[:, vh, :], in0=t32[:, vh, :], scalar=lam[:, 4:5],
                                                                in1=o32[:, vh, :], op0=ALU.mult, op1=ALU.add),
                 r=[t32b, o32b, lamb], w=[o32b])
            P.op("act", lambda E, vh=vh: E.activation(out=t32[:, vh, :], in_=o32[:, vh, :], func=AF.Square), r=[o32b, t32b], w=[t32b])
        br = next_ps(4, 8)
        for vh in range(2):
            P.op("pe", lambda E, vh=vh: E.matmul(PS[br][:, :], lhsT=ones32[:], rhs=t32[:, vh, :], start=(vh == 0), stop=(vh == 1)),
                 r=[t32b, B_const], w=[PB[br]])
        P.op("act", lambda E: E.activation(out=rz[:, 0, :], in_=PS[br][:, :], func=AF.Ln, scale=1.0 / 256, bias=epsc[:, 0:1]),
             r=[PB[br], rzb, B_const], w=[rzb])
        P.op("act", lambda E: E.activation(out=rz[:, 0, :], in_=rz[:, 0, :], func=AF.Exp, scale=-0.5), r=[rzb], w=[rzb])
        for vh in range(2):
            P.op("dve", lambda E, vh=vh: E.scalar_tensor_tensor(out=zd[vh][:], in0=o32[:, vh, :], scalar=sgc[:, vh:vh + 1],
                                                                in1=rz[:, 0, :], op0=ALU.mult, op1=ALU.mult),
                 r=[o32b, rzb, sgcb], w=[zdb[vh]])
            dma(ZdT.ap()[h * 256 + vh * 128:h * 256 + (vh + 1) * 128, q0:q0 + 512], zd[vh][:], [zdb[vh]], [B_scr["ZdT"]], "st_ZdT", eng="act")

    def diff_head(h, it0):
        dma(k1[:, 0:S + N_META], dKT.ap()[h * 256:h * 256 + 128, 0:S + N_META], [B_scr["dKT"]], [kkb], "kk")
        dma(k2[:, 0:S + N_META], dKT.ap()[h * 256 + 128:h * 256 + 256, 0:S + N_META], [B_scr["dKT"]], [kkb], "kk")
        for t8 in range(0, NTALL, 16):
            t9 = min(NTALL, t8 + 16)
            dma(vv[:, t8:t9, :], dVtm.ap()[t8 * 128:t9 * 128, h * 256:(h + 1) * 256].rearrange("(t p) c -> p t c", p=128), [B_scr["dVtm"]], [vvb], "vv")
        dma(vv[:N_META, NTALL, :], dVtm.ap()[S:S + N_META, h * 256:(h + 1) * 256], [B_scr["dVtm"]], [vvb], "vv")
        dma(rbt[0:2, :], c_rb.ap()[h], [B_x], [rbb], "rb")
        for qb in range(NQB):
            diff_iter(h, qb, (it0 + qb) % 2)

    for h in range(H):
        next_cast()
        diff_head(h, h * NQB)
    while pending_casts:
        next_cast()
    P.barrier()

    ar = Arena()
    TF = 512
    WC4 = 512
    zT = ar.a16(KC * TF).rearrange("p (k t) -> p k t", t=TF); zTb = Buf("zT")
    mxbase = ar.a16(max(KC, 32) * TF).rearrange("p (k t) -> p k t", t=TF); mxTb = Buf("mxT")
    mxT = mxbase[:, 0:KC, :]
    aTf = mxbase; aTfb = mxTb
    wt4 = [ar.a16(32 * WC4).rearrange("p (k c) -> p k c", c=WC4) for _ in range(2)]
    wt4b = [Buf("w40"), Buf("w41")]
    gg = [ar.a16(TF) for _ in range(4)]; ggb = [Buf("gg%d" % i) for i in range(4)]
    ub4 = ar.a16(D); ub4b = Buf("ub4")
    gsb = ar.a16(TF); gsbb = Buf("gsb")
    xt4 = ar.a32(D); xt4b = Buf("xt4")
    gfin = ar.a32(D)
    hst = [ar.a32(WC4) for _ in range(3)]; hstb = [Buf("hst%d" % i) for i in range(3)]
    hrow = [ar.a32(WC4) for _ in range(3)]; hrowb = [Buf("hrow%d" % i) for i in range(3)]
    B_h1 = [Buf("h1_%d" % i) for i in range(4)]
    rr = [0]
    dma(gfin[:], gvec.ap()[2].partition_broadcast(128), [B_x], [B_const], "c0")

    def s4_mix(tb, pas, Zs, nm, wdr, wn, gsrc, gn):
        for k8 in range(0, KC, 8):
            k9 = min(KC, k8 + 8)
            dma(zT[:, k8:k9, :], Zs.ap()[k8 * 128:k9 * 128, tb:tb + TF].rearrange("(k p) t -> p k t", p=128), [B_scr[nm]], [zTb], "zT")

        def ev_mix(b, pap, t0, n, c0, ncol):
            i = rr[0] % 4; rr[0] += 1
            kc = c0 // 128
            dma(gg[i][:, 0:n], gsrc.ap()[c0:c0 + 128, tb + t0:tb + t0 + n], [B_scr[gn]], [ggb[i]], "gg%d" % i)
            if pas == 0:
                P.op("dve", lambda E: E.tensor_tensor(out=mxT[:, kc, t0:t0 + n], in0=pap, in1=gg[i][:, 0:n], op=ALU.mult),
                     r=[PB[b], ggb[i]], w=[mxTb])
            else:
                P.op("dve", lambda E: E.tensor_tensor(out=gsb[:, 0:n], in0=pap, in1=gg[i][:, 0:n], op=ALU.mult),
                     r=[PB[b], ggb[i]], w=[gsbb])
                P.op("dve", lambda E: E.tensor_tensor(out=mxT[:, kc, t0:t0 + n], in0=mxT[:, kc, t0:t0 + n], in1=gsb[:, 0:n], op=ALU.add),
                     r=[gsbb, mxTb], w=[mxTb])
        proj(zT, zTb, TF, wdr, WB[wn], D, 0, D, "fm", ev_mix, wt4, wt4b, WC4)

    def s4_seg(tb, f0, f1):
        def ev_gate(b, pap, t0, n, c0, ncol):
            kc = (c0 // 128)
            P.op("act", lambda E: E.activation(out=aTf[:, kc, t0:t0 + n], in_=pap, func=AF.Silu), r=[PB[b]], w=[aTfb])

        def ev_up(b, pap, t0, n, c0, ncol):
            kc = (c0 // 128)
            P.op("dve", lambda E: E.tensor_tensor(out=aTf[:, kc, t0:t0 + n], in0=pap, in1=aTf[:, kc, t0:t0 + n], op=ALU.mult),
                 r=[PB[b], aTfb], w=[aTfb])

        class _W:
            def __init__(s, a):
                s.a = a

            def ap(s):
                return s.a
        proj(zT, zTb, TF, _W(wb_g.ap()[:, f0 * 128:f1 * 128]), WB["g"], D, 0, (f1 - f0) * 128, "fm", ev_gate, wt4, wt4b, WC4)
        proj(zT, zTb, TF, _W(wb_u.ap()[:, f0 * 128:f1 * 128]), WB["u"], D, 0, (f1 - f0) * 128, "fm", ev_up, wt4, wt4b, WC4)

        def ev_dn(b, pap, t0, n, c0, ncol):
            i = rr[0] % 3; rr[0] += 1
            hb = B_h1[t0 // 128]
            dma(hrow[i][:n, 0:ncol], h1.ap()[tb + t0:tb + t0 + n, c0:c0 + ncol], [hb], [hrowb[i]], "hrow%d" % i)
            P.op("dve", lambda E: E.tensor_tensor(out=hst[i][:n, 0:ncol], in0=pap, in1=hrow[i][:n, 0:ncol], op=ALU.add),
                 r=[PB[b], hrowb[i]], w=[hstb[i]])
            dma(h1.ap()[tb + t0:tb + t0 + n, c0:c0 + ncol], hst[i][:n, 0:ncol], [hstb[i]], [hb], "st_h1_%d" % (t0 // 128), eng="act")
        proj(aTf, aTfb, TF, _W(wb_d.ap()[f0 * 128:f1 * 128, :]), WB["d"], (f1 - f0) * 128, 0, D, "tm", ev_dn, wt4, wt4b, WC4)

    def s4_block(tb):
        s4_mix(tb, 0, ZrT, "ZrT", wb_br, "br", gaT, "gaT")
        s4_mix(tb, 1, ZdT, "ZdT", wb_bd, "bd", gbT, "gbT")

        def ev_h1(b, pap, t0, n, c0, ncol):
            i = rr[0] % 3; rr[0] += 1
            hb = B_h1[t0 // 128]
            dma(hrow[i][:n, 0:ncol], x_rot.ap()[tb + t0:tb + t0 + n, c0:c0 + ncol], [B_x], [hrowb[i]], "hrow%d" % i)
            P.op("dve", lambda E: E.tensor_tensor(out=hst[i][:n, 0:ncol], in0=pap, in1=hrow[i][:n, 0:ncol], op=ALU.add),
                 r=[PB[b], hrowb[i]], w=[hstb[i]])
            dma(h1.ap()[tb + t0:tb + t0 + n, c0:c0 + ncol], hst[i][:n, 0:ncol], [hstb[i]], [hb], "st_h1_%d" % (t0 // 128), eng="act")
        proj(mxT, mxTb, TF, wb_out, WB["out"], D, 0, D, "tm", ev_h1, wt4, wt4b, WC4)
        for t0 in range(0, TF, 128):
            norm_T(ar, h1.ap()[tb + t0:tb + t0 + 128, :], 128, 1, zT, zTb, t0, xt4, xt4b, ub4, ub4b, sm, src_buf=B_h1[t0 // 128])
        for f0 in range(0, FC, 32):
            s4_seg(tb, f0, min(FC, f0 + 32))
        for t0 in range(0, TF, 128):
            s4_fin(tb, t0)

    def s4_fin(tb, t0):
        dma(xt4[:, :], h1.ap()[tb + t0:tb + t0 + 128, :], [B_h1[t0 // 128]], [xt4b], "xt")
        P.op("dve", lambda E: E.memset(sm[:, 0:1], 0.0), w=[B_ss])
        P.op("act", lambda E: E.activation(out=ub4[:, :], in_=xt4[:, :], func=AF.Square, accum_out=sm[:, 0:1]),
             r=[xt4b], w=[ub4b, B_ss])
        P.op("act", lambda E: E.activation(out=sm[:, 1:2], in_=sm[:, 0:1], func=AF.Ln, scale=1.0 / D, bias=epsc[:, 0:1]),
             r=[B_ss, B_const], w=[B_ss])
        P.op("act", lambda E: E.activation(out=sm[:, 2:3], in_=sm[:, 1:2], func=AF.Exp, scale=-0.5), r=[B_ss], w=[B_ss])
        P.op("dve", lambda E: E.scalar_tensor_tensor(out=xt4[:, :], in0=xt4[:, :], scalar=sm[:, 2:3], in1=gfin[:, :],
                                                     op0=ALU.mult, op1=ALU.mult), r=[xt4b, B_ss, B_const], w=[xt4b])
        dma(y.ap()[tb + t0:tb + t0 + 128, :], xt4[:, :], [xt4b], [B_scr["y"]], "st_y", eng="act")

    for tb in range(0, T, TF):
        s4_block(tb)

    with nc.allow_non_contiguous_dma(reason="small strided constant/layout loads"):
        with nc.Block() as block:
            sems = P.emit(nc, block)
        sems.close()
    stack.close()
    return nc


def host_consts(cfg, qt):
    S, T, H, NKT, NQB, NTALL = cfg.S, cfg.T, cfg.H, cfg.NKT, cfg.NQB, cfg.NTALL
    own0 = qt * T
    perm = (own0 + np.arange(S)) % S
    kpos = np.full((NKT, 128), 1e9, np.float64)
    kpos[:NTALL] = (N_META + perm).reshape(NTALL, 128)
    kpos[NTALL, :N_META] = np.arange(N_META)
    absd = np.zeros((NQB, 128, NKT), np.float32)
    sgn = np.zeros((NQB, 2, NKT * 128), np.float32)
    for qb in range(NQB):
        qc = N_META + own0 + qb * 512 + 255.5
        absd[qb] = np.abs(kpos - qc).T
        sg = np.where(kpos < qc, 1.0, -1.0).reshape(-1)
        sgn[qb, 0] = sg; sgn[qb, 1] = sg
    absd = np.minimum(absd, 1e6).astype(np.float32)
    sl = np.array(slopes(cfg), np.float64)
    qrel = np.arange(512) - 255.5
    val = (-sl[:, None] * qrel[None, :]).astype(np.float32)
    hi = val.astype(ml_dtypes.bfloat16)
    lo = (val - hi.astype(np.float32)).astype(ml_dtypes.bfloat16)
    rb = np.stack([hi, lo], axis=1)
    p = np.arange(128)[:, None]; f = np.arange(512)[None, :]
    babs = np.stack([np.abs(ci * 128 + p - f) for ci in range(4)]).astype(np.float32)
    p0 = N_META + own0; p1 = p0 + T
    distf = np.where(kpos < p0, p0 - 1 - kpos, BIG)
    distb = np.where((kpos >= p1) & (kpos < 1e8), kpos - p1, BIG)
    dist = np.stack([distf.T, distb.T]).astype(np.float32)
    s_ = np.arange(128)[:, None]; t_ = np.arange(128)[None, :]
    dpos = np.where(t_ >= s_, t_ - s_, BIG)
    dneg = np.where(s_ > t_, s_ - t_, BIG)
    qdf = np.broadcast_to(t_ + 1.0, (128, 128))
    qdb = np.broadcast_to(128.0 - t_, (128, 128))
    sq = np.stack([dpos, dneg, qdf, qdb]).astype(np.float32)
    kd = np.stack([127.0 - np.arange(128), np.arange(128) * 1.0], axis=1).astype(np.float32)
    return perm, dict(c_absd=absd, c_sgn=sgn.astype(ml_dtypes.bfloat16), c_rb=rb, c_babs=babs, c_dist=dist,
                      c_sq=sq, c_kd=kd, c_ident=np.eye(128, dtype=np.float32).astype(ml_dtypes.bfloat16))


_NC_CACHE = {}


def run(cfg, inp):
    key = (cfg.D, cfg.S)
    if key not in _NC_CACHE:
        _NC_CACHE[key] = build(cfg)
    nc = _NC_CACHE[key]
    f32 = lambda a: np.ascontiguousarray(np.asarray(a, dtype=np.float32))
    x = f32(inp["x"])
    shared = dict(
        meta=f32(inp["meta_tokens"]),
        gvec=np.stack([f32(inp["norm_mix_g"])[0], f32(inp["norm_ffn_g"])[0], f32(inp["norm_final_g"])]),
        w_in=f32(inp["w_in"])[0], w_br=f32(inp["w_branch_ret"])[0], w_bd=f32(inp["w_branch_diff"])[0],
        w_out=f32(inp["w_out"])[0], w_g=f32(inp["w_ffn_gate"])[0], w_u=f32(inp["w_ffn_up"])[0],
        w_d=f32(inp["w_ffn_down"])[0], rdecay=f32(inp["ret_log_decay"])[0], dlam=f32(inp["diff_lambda"])[0],
        subln=f32(inp["diff_subln_g"])[0])
    in_maps = []
    for c in range(8):
        b, qt = c // 4, c % 4
        perm, consts = host_consts(cfg, qt)
        m = dict(shared)
        m["x_rot"] = np.ascontiguousarray(x[b][perm])
        m.update(consts)
        in_maps.append(m)
    res = run_bass_kernel_spmd(nc, in_maps, core_ids=list(range(8)))
    out = np.zeros((2, cfg.S, cfg.D), np.float32)
    for c in range(8):
        b, qt = c // 4, c % 4
        out[b, qt * cfg.T:(qt + 1) * cfg.T] = res.results[c]["y"]
    return out


def kernel(**inputs):
    return run(Cfg(4096, 8192), inputs)
```
